# Optimizing a Trainium2 kernel written in Bass

```python
import jax, jax.numpy as jnp
from jax import lax
import numpy as np

D_MODEL = 1024
BATCH = 32
SEQ = 256
DEPTH = 2
DEC_BATCH = 8
DEC_SEQ = 4096
PAST_LEN = 512

GRID_W = 64
D_A = 512
CONV_A = 31
D_B = 512
CONV_B = 3
N_HEADS = 8
HEAD_DIM = 64
D_C = N_HEADS * HEAD_DIM
WIN_ROWS = 8
WIN_COLS = 16
Q_BLOCK_COLS = 16
K_BLOCK_COLS = 32
N_COL_BLOCKS = GRID_W // Q_BLOCK_COLS
D_FF = 2816
N_MOD = 9
N_IN = 2 * D_A + 3 * D_B + 3 * D_C + 3 * D_MODEL
CTX_Q_BLOCK = 128
EPS = 1e-6
NEG_INF = -1e30

kernel_name = "hybrid_conv_natten_prefix_dit_step"


def _rms(x, g):
    xf = x.astype(jnp.float32)
    y = xf * lax.rsqrt(jnp.mean(jnp.square(xf), axis=-1, keepdims=True) + EPS)
    return (y * g.astype(jnp.float32)).astype(x.dtype)


def _layer_norm(x, g, b):
    xf = x.astype(jnp.float32)
    mu = jnp.mean(xf, axis=-1, keepdims=True)
    xc = xf - mu
    y = xc * lax.rsqrt(jnp.mean(jnp.square(xc), axis=-1, keepdims=True) + EPS)
    return (y * g.astype(jnp.float32) + b.astype(jnp.float32)).astype(x.dtype)


def _swiglu(u, w_gate, w_up, w_down):
    return (jax.nn.silu(u @ w_gate) * (u @ w_up)) @ w_down


def _modulation(cvec, w_ada, b_ada):
    m = (jax.nn.silu(cvec) @ w_ada + b_ada).reshape(cvec.shape[0], N_MOD, D_MODEL)
    return [m[:, i][:, None, :] for i in range(N_MOD)]


def _dwconv(x, w, b=None):
    k = w.shape[0]
    y = lax.conv_general_dilated(
        x, w[:, None, :].astype(x.dtype), window_strides=(1,),
        padding=[(k // 2, k // 2)], dimension_numbers=('NWC', 'WIO', 'NWC'),
        feature_group_count=x.shape[-1])
    return y if b is None else y + b


def _project(u, w_in):
    sizes = [D_A, D_A, D_B, D_B, D_B, D_C, D_C, D_C, D_MODEL, D_MODEL]
    idx = [int(i) for i in np.cumsum(sizes)]
    return jnp.split(u @ w_in, idx, axis=-1)


def _heads(t):
    return t.reshape(*t.shape[:-1], N_HEADS, HEAD_DIM)


def _context_attention(q, k, v):
    b, t = q.shape[:2]
    scale = HEAD_DIM ** -0.5
    qb = q.reshape(b, t // CTX_Q_BLOCK, CTX_Q_BLOCK, N_HEADS, HEAD_DIM).transpose(1, 0, 2, 3, 4)

    def block(qi):
        s = jnp.einsum('bqhd,bkhd->bhqk', qi, k).astype(jnp.float32) * scale
        p = jax.nn.softmax(s, axis=-1).astype(v.dtype)
        return jnp.einsum('bhqk,bkhd->bqhd', p, v)

    o = lax.map(block, qb)
    return o.transpose(1, 0, 2, 3, 4).reshape(b, t, D_C)


def _neighbourhood_attention(q, k, v, k_ctx, v_ctx, rpb):
    b, t = q.shape[:2]
    rows = t // GRID_W
    kr = min(WIN_ROWS, rows)
    n_loc = kr * K_BLOCK_COLS
    scale = HEAD_DIM ** -0.5
    qg = q.reshape(b, rows, N_COL_BLOCKS, Q_BLOCK_COLS, N_HEADS, HEAD_DIM)
    kg = k.reshape(b, rows, GRID_W, N_HEADS, HEAD_DIM)
    vg = v.reshape(b, rows, GRID_W, N_HEADS, HEAD_DIM)
    q_cols = np.arange(GRID_W).reshape(N_COL_BLOCKS, Q_BLOCK_COLS)
    win_start = np.clip(q_cols - WIN_COLS // 2, 0, GRID_W - WIN_COLS)
    blk_start = np.clip(np.arange(N_COL_BLOCKS) * Q_BLOCK_COLS - WIN_COLS // 2, 0, GRID_W - K_BLOCK_COLS)
    k_cols = blk_start[:, None] + np.arange(K_BLOCK_COLS)
    kc = k_cols[:, None, :]
    col_valid = (kc >= win_start[..., None]) & (kc < win_start[..., None] + WIN_COLS)
    mask = jnp.asarray(np.broadcast_to(col_valid[:, :, None, :],
                                       (N_COL_BLOCKS, Q_BLOCK_COLS, kr, K_BLOCK_COLS))
                       .reshape(N_COL_BLOCKS, Q_BLOCK_COLS, n_loc))
    dc = np.clip(kc - q_cols[..., None], -(WIN_COLS - 1), WIN_COLS - 1) + WIN_COLS - 1
    col_bias = rpb[:, :, dc]

    def row(r):
        rs = jnp.clip(r - kr // 2, 0, rows - kr)
        k_rows = lax.dynamic_slice_in_dim(kg, rs, kr, axis=1)
        v_rows = lax.dynamic_slice_in_dim(vg, rs, kr, axis=1)
        k_loc = jnp.take(k_rows, k_cols, axis=2).transpose(0, 2, 1, 3, 4, 5).reshape(
            b, N_COL_BLOCKS, n_loc, N_HEADS, HEAD_DIM)
        v_loc = jnp.take(v_rows, k_cols, axis=2).transpose(0, 2, 1, 3, 4, 5).reshape(
            b, N_COL_BLOCKS, n_loc, N_HEADS, HEAD_DIM)
        dr = rs + jnp.arange(kr) - r + WIN_ROWS - 1
        bias = jnp.take(col_bias, dr, axis=1).transpose(0, 2, 3, 1, 4).reshape(
            N_HEADS, N_COL_BLOCKS, Q_BLOCK_COLS, n_loc)
        qr = lax.dynamic_index_in_dim(qg, r, axis=1, keepdims=False)
        s_loc = jnp.einsum('bjqhd,bjkhd->bhjqk', qr, k_loc).astype(jnp.float32) * scale \
            + bias.astype(jnp.float32)
        s_loc = jnp.where(mask, s_loc, NEG_INF)
        s_ctx = jnp.einsum('bjqhd,bphd->bhjqp', qr, k_ctx).astype(jnp.float32) * scale
        p = jax.nn.softmax(jnp.concatenate([s_loc, s_ctx], axis=-1), axis=-1).astype(v.dtype)
        o = jnp.einsum('bhjqk,bjkhd->bjqhd', p[..., :n_loc], v_loc) \
            + jnp.einsum('bhjqp,bphd->bjqhd', p[..., n_loc:], v_ctx)
        return o.reshape(b, GRID_W, N_HEADS, HEAD_DIM)

    o = lax.map(row, jnp.arange(rows))
    return o.transpose(1, 0, 2, 3, 4).reshape(b, t, D_C)


def _layer(x, mods, p, ctx_kv):
    sh1, sc1, g1, sh2, sc2, g2, sh3, sc3, g3 = mods
    u = _rms(x, p['g_ff1']) * (1 + sc1) + sh1
    x = x + 0.5 * g1 * _swiglu(u, p['w_ff1_gate'], p['w_ff1_up'], p['w_ff1_down'])

    u = _rms(x, p['g_mix']) * (1 + sc2) + sh2
    a_val, a_gate, b_g, c_g, h_b, q, k, v, ga, gb, gc = _project(u, p['w_in'])
    ha = _dwconv(a_val * jax.nn.sigmoid(a_gate), p['conv_a_w'], p['conv_a_b'])
    ya = jax.nn.silu(_layer_norm(ha, p['ln_a_g'], p['ln_a_b'])) @ p['w_a_out']
    yb = (b_g * _dwconv(c_g * h_b, p['conv_b_w'])) @ p['w_b_out']
    q = _rms(_heads(q), p['q_norm_g'])
    k = _rms(_heads(k), p['k_norm_g'])
    v = _heads(v)
    if ctx_kv is None:
        o = _context_attention(q, k, v)
        new_kv = (k, v)
    else:
        o = _neighbourhood_attention(q, k, v, ctx_kv[0], ctx_kv[1], p['rpb'])
        new_kv = None
    yc = o @ p['w_c_out']
    m = jax.nn.sigmoid(ga) * ya + jax.nn.sigmoid(gb) * yb + jax.nn.sigmoid(gc) * yc
    x = x + g2 * (m @ p['w_merge'])

    u = _rms(x, p['g_ff2']) * (1 + sc3) + sh3
    x = x + 0.5 * g3 * _swiglu(u, p['w_ff2_gate'], p['w_ff2_up'], p['w_ff2_down'])
    return x, new_kv


def setup_inputs(seed: int = 0) -> dict:
    key = jax.random.key(seed)
    ks = jax.random.split(key, 32)
    f32 = jnp.float32

    def nrm(k, shape, scale):
        return jax.random.normal(k, shape, f32) * scale

    L, D = DEPTH, D_MODEL
    return {
        "x_prompt": nrm(ks[0], (BATCH, SEQ, D), 1.0),
        "x_sample": nrm(ks[1], (DEC_BATCH, DEC_SEQ, D), 1.0),
        "cache_k": nrm(ks[2], (DEC_BATCH, DEPTH, PAST_LEN, N_HEADS, HEAD_DIM), 1.0),
        "cache_v": nrm(ks[3], (DEC_BATCH, DEPTH, PAST_LEN, N_HEADS, HEAD_DIM), 1.0),
        "c": nrm(ks[4], (DEC_BATCH, D), 1.0),
        "c_ctx": nrm(ks[5], (D,), 1.0),
        "w_ada": nrm(ks[6], (L, D, N_MOD * D), 0.5 * D ** -0.5),
        "b_ada": nrm(ks[7], (L, N_MOD * D), 0.02),
        "g_ff1": 1.0 + nrm(ks[8], (L, D), 0.02),
        "w_ff1_gate": nrm(ks[9], (L, D, D_FF), D ** -0.5),
        "w_ff1_up": nrm(ks[10], (L, D, D_FF), D ** -0.5),
        "w_ff1_down": nrm(ks[11], (L, D_FF, D), D_FF ** -0.5),
        "g_mix": 1.0 + nrm(ks[12], (L, D), 0.02),
        "w_in": nrm(ks[13], (L, D, N_IN), D ** -0.5),
        "conv_a_w": nrm(ks[14], (L, CONV_A, D_A), CONV_A ** -0.5),
        "conv_a_b": nrm(ks[15], (L, D_A), 0.02),
        "ln_a_g": 1.0 + nrm(ks[16], (L, D_A), 0.02),
        "ln_a_b": nrm(ks[17], (L, D_A), 0.02),
        "w_a_out": nrm(ks[18], (L, D_A, D), D_A ** -0.5),
        "conv_b_w": nrm(ks[19], (L, CONV_B, D_B), CONV_B ** -0.5),
        "w_b_out": nrm(ks[20], (L, D_B, D), D_B ** -0.5),
        "q_norm_g": 1.0 + nrm(ks[21], (L, HEAD_DIM), 0.02),
        "k_norm_g": 1.0 + nrm(ks[22], (L, HEAD_DIM), 0.02),
        "rpb": nrm(ks[23], (L, N_HEADS, 2 * WIN_ROWS - 1, 2 * WIN_COLS - 1), 0.1),
        "w_c_out": nrm(ks[24], (L, D_C, D), D_C ** -0.5),
        "w_merge": nrm(ks[25], (L, D, D), D ** -0.5),
        "g_ff2": 1.0 + nrm(ks[26], (L, D), 0.02),
        "w_ff2_gate": nrm(ks[27], (L, D, D_FF), D ** -0.5),
        "w_ff2_up": nrm(ks[28], (L, D, D_FF), D ** -0.5),
        "w_ff2_down": nrm(ks[29], (L, D_FF, D), D_FF ** -0.5),
    }


def reference(x_prompt, x_sample, cache_k, cache_v, c, c_ctx, w_ada, b_ada,
              g_ff1, w_ff1_gate, w_ff1_up, w_ff1_down, g_mix, w_in,
              conv_a_w, conv_a_b, ln_a_g, ln_a_b, w_a_out, conv_b_w, w_b_out,
              q_norm_g, k_norm_g, rpb, w_c_out, w_merge,
              g_ff2, w_ff2_gate, w_ff2_up, w_ff2_down):
    h_ctx = x_prompt
    h_lat = x_sample
    new_ks = []
    new_vs = []
    for l in range(DEPTH):
        p = {
            'g_ff1': g_ff1[l], 'w_ff1_gate': w_ff1_gate[l], 'w_ff1_up': w_ff1_up[l],
            'w_ff1_down': w_ff1_down[l], 'g_mix': g_mix[l], 'w_in': w_in[l],
            'conv_a_w': conv_a_w[l], 'conv_a_b': conv_a_b[l], 'ln_a_g': ln_a_g[l],
            'ln_a_b': ln_a_b[l], 'w_a_out': w_a_out[l], 'conv_b_w': conv_b_w[l],
            'w_b_out': w_b_out[l], 'q_norm_g': q_norm_g[l], 'k_norm_g': k_norm_g[l],
            'rpb': rpb[l], 'w_c_out': w_c_out[l], 'w_merge': w_merge[l],
            'g_ff2': g_ff2[l], 'w_ff2_gate': w_ff2_gate[l], 'w_ff2_up': w_ff2_up[l],
            'w_ff2_down': w_ff2_down[l],
        }
        mods_ctx = _modulation(c_ctx[None, :], w_ada[l], b_ada[l])
        mods_lat = _modulation(c, w_ada[l], b_ada[l])
        h_ctx, kv = _layer(h_ctx, mods_ctx, p, None)
        new_ks.append(kv[0])
        new_vs.append(kv[1])
        h_lat, _ = _layer(h_lat, mods_lat, p, (cache_k[:, l], cache_v[:, l]))
    new_k = jnp.stack(new_ks, axis=1)
    new_v = jnp.stack(new_vs, axis=1)
    return (h_ctx, h_lat, new_k, new_v)
```

```python
import contextlib
import numpy as np
import concourse.bass as bass
import concourse.mybir as mybir
from concourse.bass_utils import run_bass_kernel_spmd

F32 = mybir.dt.float32
BF16 = mybir.dt.bfloat16
AF = mybir.ActivationFunctionType
ALU = mybir.AluOpType
AX = mybir.AxisListType

D = 1024
KC = 8
DFF = 2816
FC = 22
NIN = 7168
T = 512
NH = 8
HD = 64
PAST = 512
EPS = 1e-6
NSLOT = 9
SLOT = 2048
BLK_START = [0, 8, 24, 32]
CP_OF_J = [0, 1, 1, 2]
CP_OFF = [16, 8, 0]
RP_R, RP_C = 23, 63
NCOL = 420
C_G = 0
C_CAW = 24
C_CAB = 148
C_LNG = 152
C_LNB = 156
C_CBW = 160
C_GQ = 172
C_GK = 173
C_MOD = 176
C_A = 320
C_GT = 368
C_EPS = 416
WEIGHT_SHAPES = [("w_ada", [D, 9 * D]), ("b_ada", [9 * D]), ("g_ff1", [D]), ("w_ff1_gate", [D, DFF]), ("w_ff1_up", [D, DFF]),
                 ("w_ff1_down", [DFF, D]), ("g_mix", [D]), ("w_in", [D, NIN]), ("conv_a_w", [31, 512]),
                 ("conv_a_b", [512]), ("ln_a_g", [512]), ("ln_a_b", [512]), ("w_a_out", [512, D]),
                 ("conv_b_w", [3, 512]), ("w_b_out", [512, D]), ("q_norm_g", [64]), ("k_norm_g", [64]),
                 ("rpb", [8, 15, 31]), ("w_c_out", [512, D]), ("w_merge", [D, D]), ("g_ff2", [D]),
                 ("w_ff2_gate", [D, DFF]), ("w_ff2_up", [D, DFF]), ("w_ff2_down", [DFF, D])]


class Buf:
    __slots__ = ("name", "w", "r", "const")

    def __init__(self, name):
        self.name = name
        self.w = None
        self.r = {}
        self.const = False


class Eng:
    def __init__(self, name):
        self.name = name
        self.ops = []
        self.count = 0
        self.sem = None
        self.waited = {}
        self.dcount = 0


class DSem:
    def __init__(self, idx):
        self.idx = idx
        self.val = 0


class Slab:
    __slots__ = ("n", "slot", "ap", "buf")


def mask_for_tile(t, ntl, cp):
    rows = 8 * ntl
    kr_n = min(8, rows)
    m = np.zeros((128, 4, 8, 16), np.float32)
    j = [0, 1, 3][cp]
    bs = BLK_START[j]
    w0 = 8 * t - 4
    for gi in range(4):
        for krl in range(4):
            krow = w0 + 4 * gi + krl
            if krow < 0 or krow >= rows:
                continue
            for a in range(8):
                r = 8 * t + a
                rs = min(max(r - kr_n // 2, 0), rows - kr_n)
                if not (rs <= krow < rs + kr_n):
                    continue
                for kcl in range(32):
                    kc_ = bs + kcl
                    for qc in range(16):
                        q = 16 * j + qc
                        ws = min(max(q - 8, 0), 48)
                        if ws <= kc_ < ws + 16:
                            m[krl * 32 + kcl, gi, a, qc] = 1.0
    return m.reshape(128, 512)


class Prog:
    def __init__(self, nc, NL, n_ctx_tiles, n_lat_tiles):
        self.nc = nc
        self.NL = NL
        self.nct = n_ctx_tiles
        self.nlt = n_lat_tiles
        self.NT = n_ctx_tiles + n_lat_tiles
        self.nseq = 2 * n_ctx_tiles
        self.LT = T * n_lat_tiles
        self.NTOK = T * self.NT
        self.eng = {k: Eng(k) for k in ("pe", "act", "dve", "pool", "sp")}
        self.sem_handles = []
        self.dry = False
        self.requests = []
        self.req_list = []
        self.final_events = []
        pats = {}
        self.mask_idx = {}
        self.mask_list = []
        for t in range(n_lat_tiles):
            for cp in range(3):
                m = mask_for_tile(t, n_lat_tiles, cp)
                key = m.tobytes()
                if key not in pats:
                    pats[key] = len(self.mask_list)
                    self.mask_list.append(m)
                self.mask_idx[(t, cp)] = pats[key]

    def new_sem(self, stack, name):
        h = stack.enter_context(self.nc.semaphore(name))
        self.sem_handles.append(h)
        return len(self.sem_handles) - 1

    def _collect(self, reads, writes):
        deps = {}

        def add(ev):
            if ev is None:
                return
            s, v = ev
            if deps.get(s, 0) < v:
                deps[s] = v

        for b in reads:
            add(b.w)
        for b in writes:
            add(b.w)
            for s, v in b.r.items():
                add((s, v))
        return deps

    def _prune(self, E, deps):
        waits = []
        for s, v in deps.items():
            if E.name == "pe" and s == E.sem:
                continue
            if E.waited.get(s, 0) >= v:
                continue
            E.waited[s] = v
            waits.append((s, v))
        return waits

    def _mark(self, ev, reads, writes):
        s, v = ev
        for b in reads:
            if b.const:
                continue
            if b.r.get(s, 0) < v:
                b.r[s] = v
        for b in writes:
            b.w = ev
            b.r = {}

    def op(self, eng, fn, reads=(), writes=()):
        if self.dry:
            return None
        E = self.eng[eng]
        deps = self._collect(reads, writes)
        waits = self._prune(E, deps)
        E.count += 1
        ev = (E.sem, E.count)
        E.ops.append((waits, fn, (E.sem, 1)))
        self._mark(ev, reads, writes)
        return ev

    def dma(self, q, out, in_, reads=(), writes=(), sem=None, throttle=True, slow=False):
        if self.dry:
            return None
        E = self.eng[q]
        if sem is None:
            pool = self.dsems[q]
            S = pool[E.dcount % len(pool)]
            E.dcount += 1
        else:
            S = sem
        deps = self._collect(reads, writes)
        if throttle and S.val > 0:
            if deps.get(S.idx, 0) < S.val:
                deps[S.idx] = S.val
        waits = self._prune(E, deps)
        S.val += 16
        ev = (S.idx, S.val)
        E.ops.append((waits, (lambda e, o=out, i=in_, sl=slow: e.dma_start(out=o, in_=i, allow_slow_non_contiguous=True) if sl else e.dma_start(out=o, in_=i)), (S.idx, 16)))
        self._mark(ev, reads, writes)
        return ev

    def bank(self):
        st = self.st
        for _ in range(8):
            b = st["bank_rr"]
            st["bank_rr"] = (b + 1) % 8
            if b not in st["held"]:
                return b
        raise RuntimeError("no free psum bank")

    def hold(self, b):
        self.st["held"].add(b)

    def unhold(self, b):
        self.st["held"].discard(b)

    def rot(self, key, n):
        v = self.st.get(key, 0)
        self.st[key] = (v + 1) % n
        return v

    def slab(self, src_ap, nelem, srcbuf, shape):
        st = self.st
        n = st["req_i"]
        st["req_i"] += 1
        s = Slab()
        s.n = n
        s.slot = n % NSLOT
        s.buf = self.R[s.slot]
        base = self.ring[:, s.slot * SLOT: s.slot * SLOT + nelem]
        s.ap = base.rearrange("p (a b) -> p a b", a=shape[0])
        dst = s.ap if src_ap.ndim == 3 else base
        if self.dry:
            self.req_list.append((src_ap, dst, srcbuf))
            return s
        self.pump()
        assert st["loaded"] > n, "slab load not emitted"
        return s

    def pump(self):
        st = self.st
        L = self.req_list
        while st["loaded"] < len(L):
            m = st["loaded"]
            if m >= NSLOT and (m - NSLOT) not in st["released"]:
                break
            src_ap, dst, srcbuf = L[m]
            if srcbuf.w is None and m >= st["req_i"]:
                break
            slot = m % NSLOT
            self.dma("sp", dst, src_ap, reads=[srcbuf], writes=[self.R[slot]])
            st["loaded"] += 1
            st["released"].discard(m - NSLOT)

    def done(self, s):
        if self.dry:
            return
        self.st["released"].add(s.n)
        self.pump()

    def mm(self, out, lhsT, rhs, start, stop, reads, bankbuf):
        return self.op("pe", lambda e, o=out, l=lhsT, r=rhs, a=start, b=stop: e.matmul(o, l, r, start=a, stop=b),
                       reads=reads, writes=[bankbuf])

    def tr(self, out, in_, ident, reads, bankbuf):
        return self.op("pe", lambda e, o=out, i=in_, d=ident: e.transpose(o, i, d), reads=reads, writes=[bankbuf])

    def act(self, out, in_, func, reads, writes, bias=None, scale=None):
        kw = {}
        if bias is not None:
            kw["bias"] = bias
        if scale is not None:
            kw["scale"] = scale
        return self.op("act", lambda e, o=out, i=in_, f=func, k=kw: e.activation(out=o, in_=i, func=f, **k),
                       reads=list(reads) + [self.MODS, self.COLS], writes=writes)

    def tt(self, eng, out, in0, in1, op, reads, writes):
        return self.op(eng, lambda e, o=out, a=in0, b=in1, p=op: e.tensor_tensor(out=o, in0=a, in1=b, op=p),
                       reads=reads, writes=writes)

    def stt(self, eng, out, in0, scalar, in1, op0, op1, reads, writes):
        return self.op(eng, lambda e, o=out, a=in0, s=scalar, b=in1, p0=op0, p1=op1:
                       e.scalar_tensor_tensor(out=o, in0=a, scalar=s, in1=b, op0=p0, op1=p1),
                       reads=reads, writes=writes)

    def ts(self, eng, out, in0, s1, s2, op0, op1, reads, writes):
        if s2 is None:
            return self.op(eng, lambda e, o=out, a=in0, x=s1, p0=op0: e.tensor_scalar(out=o, in0=a, scalar1=x, scalar2=None, op0=p0),
                           reads=reads, writes=writes)
        return self.op(eng, lambda e, o=out, a=in0, x=s1, y=s2, p0=op0, p1=op1:
                       e.tensor_scalar(out=o, in0=a, scalar1=x, scalar2=y, op0=p0, op1=p1),
                       reads=reads, writes=writes)

    def cp(self, eng, out, in_, reads, writes):
        if eng == "act":
            return self.op("act", lambda e, o=out, i=in_: e.copy(out=o, in_=i), reads=reads, writes=writes)
        return self.op(eng, lambda e, o=out, i=in_: e.tensor_copy(out=o, in_=i), reads=reads, writes=writes)

    def recip(self, out, in_, reads, writes):
        return self.op("dve", lambda e, o=out, i=in_: e.reciprocal(out=o, in_=i), reads=reads, writes=writes)

    def memset(self, eng, ap, val, writes):
        return self.op(eng, lambda e, a=ap, v=val: e.memset(a, v), reads=(), writes=writes)

    def declare(self):
        nc = self.nc
        NL = self.NL

        def din(name, shape):
            return nc.dram_tensor(name, list(shape), F32, kind="ExternalInput").ap()

        def dout(name, shape):
            return nc.dram_tensor(name, list(shape), F32, kind="ExternalOutput").ap()

        def scr(name, shape, dt=BF16):
            return nc.dram_tensor(name, list(shape), dt, kind="Internal").ap()

        I = {}
        I["xp"] = din("xp", [self.nct * T, D])
        I["xs"] = din("xs", [self.LT, D])
        I["ck"] = din("ck", [NL, PAST, 512])
        I["cv"] = din("cv", [NL, PAST, 512])
        I["cvec"] = din("cvec", [2, D])
        for nm, sh in WEIGHT_SHAPES:
            I[nm] = din(nm, [NL] + sh)
        I["c_ident"] = din("c_ident", [128, 128])
        I["c_anti"] = din("c_anti", [128, 128])
        I["c_ones"] = din("c_ones", [128, 128])
        I["c_blk"] = din("c_blk", [128, 128])
        I["c_mask"] = din("c_mask", [len(self.mask_list), 128, 512])
        self.I = I
        O = {}
        O["yp"] = dout("yp", [self.nct * T, D])
        O["ys"] = dout("ys", [self.LT, D])
        O["nk"] = dout("nk", [self.nseq, NL, 256, 512])
        O["nv"] = dout("nv", [self.nseq, NL, 256, 512])
        self.O = O
        S = {}
        for l in range(NL):
            for f in ("1", "2"):
                S[f"g{f}_{l}"] = scr(f"wg{f}_{l}", [11, 128, 2048])
                S[f"u{f}_{l}"] = scr(f"wu{f}_{l}", [11, 128, 2048])
                S[f"d{f}_{l}"] = scr(f"wd{f}_{l}", [16, 128, 11 * 128])
            S[f"in_{l}"] = scr(f"win_{l}", [28, 128, 2048])
            for b in "abc":
                S[f"o{b}_{l}"] = scr(f"wo{b}_{l}", [4, 128, 4 * 256])
            S[f"mg_{l}"] = scr(f"wmg_{l}", [4, 128, 2048])
            S[f"ca_{l}"] = scr(f"wca_{l}", [8, 128, 2048])
            S[f"cb_{l}"] = scr(f"wcb_{l}", [1, 128, 12 * 128])
            S[f"bt_{l}"] = scr(f"wbt_{l}", [len(self.mask_list), 8, 128, 512])
            S[f"rp_{l}"] = scr(f"rp_{l}", [8, 1472])
        S["xres"] = scr("xres", [D, self.NTOK], F32)
        S["u2s"] = scr("u2s", [D, self.NTOK])
        S["xas"] = scr("xas", [512, self.NTOK])
        S["chs"] = scr("chs", [512, self.NTOK])
        S["bgs"] = scr("bgs", [512, self.NTOK])
        S["qs"] = scr("qs", [512, self.NTOK])
        S["ks"] = scr("ks", [512, self.NTOK])
        S["vs"] = scr("vs", [self.NTOK, 520])
        self.S = S
        self.Dw = {k: Buf("D" + k) for k in S}
        self.Dt = {}
        for nm in ("xres", "u2s", "xas", "chs", "bgs", "qs", "ks", "vs"):
            self.Dt[nm] = [Buf(f"D{nm}{i}") for i in range(self.NT)]
        self.Dout = Buf("Dout")

    def alloc(self, stack):
        nc = self.nc
        A = nc.alloc_sbuf_tensor
        self.xt = [A(f"xt{i}", [128, KC, T], F32) for i in range(2)]
        self.X = [[Buf(f"X{i}_{c}") for c in range(KC)] for i in range(2)]
        self.ut = [A(f"ut{i}", [128, KC, T], BF16) for i in range(2)]
        self.U = [[Buf(f"U{i}_{c}") for c in range(KC)] for i in range(2)]
        self.harena = A("harena", [128, FC * T], BF16)
        self.H = [Buf(f"H{j}") for j in range(FC)]
        self.ring = A("ring", [128, NSLOT * SLOT], BF16)
        self.R = [Buf(f"R{i}") for i in range(NSLOT)]
        self.tm = [A(f"tm{i}", [128, D], F32) for i in range(2)]
        self.TM = [Buf(f"TM{i}") for i in range(2)]
        self.sqb = [A(f"sqb{i}", [128, T], BF16) for i in range(2)]
        self.SQ = [Buf(f"SQ{i}") for i in range(2)]
        self.tmpf = [A(f"tmpf{i}", [128, T], F32) for i in range(3)]
        self.TF = [Buf(f"TF{i}") for i in range(3)]
        self.stat = [A(f"stat{i}", [128, T], F32) for i in range(3)]
        self.STB = [Buf(f"ST{i}") for i in range(3)]
        self.xa = A("xa", [128, 4, 572], BF16)
        self.XA = [Buf(f"XA{c}") for c in range(4)]
        self.ch = A("ch", [128, 4, 516], BF16)
        self.CH = [Buf(f"CH{c}") for c in range(4)]
        self.bg = A("bg", [128, 4, T], BF16)
        self.BG = [Buf(f"BG{c}") for c in range(4)]
        self.qt = A("qt", [128, 4, T], BF16)
        self.Q = [Buf(f"Q{c}") for c in range(4)]
        self.kt = A("kt", [128, 4, 2 * T], BF16)
        self.K = [Buf(f"K{c}") for c in range(4)]
        self.kblk = [A(f"kblk{i}", [128, 4, 4 * 128], BF16) for i in range(2)]
        self.KB = [[Buf(f"KB{i}_{c}") for c in range(4)] for i in range(2)]
        self.vt = A("vt", [128, 8, 520], BF16)
        self.V = [Buf(f"V{i}") for i in range(8)]
        self.maskt = A("maskt", [128, 3, T], BF16)
        self.MK = [Buf(f"MK{i}") for i in range(3)]
        self.sA = A("sA", [128, 4, T], BF16)
        self.SA = [Buf(f"SA{c}") for c in range(4)]
        self.tB = A("tB", [128, 4, T], BF16)
        self.TB = [Buf(f"TB{c}") for c in range(4)]
        self.oT = A("oT", [128, 4, T], BF16)
        self.OT = [Buf(f"OT{c}") for c in range(4)]
        self.kc = A("kc", [128, 4, PAST], BF16)
        self.KCb = Buf("KC")
        self.vc = A("vc", [128, 4, 520], BF16)
        self.VCb = Buf("VC")
        self.identF = A("identF", [128, 128], F32)
        self.identB = A("identB", [128, 128], BF16)
        self.anti = A("anti", [128, 128], BF16)
        self.ones = A("ones", [128, 128], BF16)
        self.blk = A("blk", [128, 128], BF16)
        self.CONST = Buf("CONST")
        self.cols = A("cols", [128, self.NL * NCOL], F32)
        self.COLS = Buf("COLS")
        self.MODS = Buf("MODS")
        self.gkrow = A("gkrow", [128, self.NL, 512], F32)
        self.small = A("small", [128, 64], F32)
        self.SM = [Buf(f"SM{i}") for i in range(4)]
        self.scT = A("scT", [128, KC, 2], F32)
        self.SCT = Buf("SCT")
        self.ps = [nc.alloc_psum_tensor(f"ps{i}", [128, T], F32) for i in range(8)]
        self.PS = [Buf(f"PS{i}") for i in range(8)]
        ha = self.harena
        self.ha_f = [ha[:, (2 * c) * T:(2 * c + 2) * T].bitcast(F32) for c in range(4)]
        self.HA = [[self.H[2 * c], self.H[2 * c + 1]] for c in range(4)]
        self.otm = [ha[:, (8 + q) * T:(9 + q) * T] for q in range(4)]
        self.OTM = [self.H[8 + q] for q in range(4)]
        self.pl = [ha[:, (12 + i) * T:(13 + i) * T] for i in range(3)]
        self.PL = [self.H[12 + i] for i in range(3)]
        self.pc = [ha[:, (15 + i) * T:(16 + i) * T] for i in range(3)]
        self.PC = [self.H[15 + i] for i in range(3)]
        self.macc = [ha[:, (18 + 2 * i) * T:(20 + 2 * i) * T].bitcast(F32) for i in range(2)]
        self.MA = [[self.H[18 + 2 * i], self.H[19 + 2 * i]] for i in range(2)]
        for k in ("pe", "act", "dve", "pool"):
            self.eng[k].sem = self.new_sem(stack, "c_" + k)
        self.dsems = {"sp": [DSem(self.new_sem(stack, f"dsp{i}")) for i in range(20)],
                      "pool": [DSem(self.new_sem(stack, f"dpl{i}")) for i in range(8)],
                      "act": [DSem(self.new_sem(stack, f"dac{i}")) for i in range(8)]}
        self.KC4 = [Buf(f"KC4_{i}") for i in range(4)]
        self.csem = {}
        for k in self.S:
            if k[0] in "gudiomcbr" and "_" in k:
                self.csem[k] = DSem(self.new_sem(stack, "cv_" + k))

    def hv(self, j):
        return self.harena[:, j * T:(j + 1) * T]

    def col(self, l, idx, n=1):
        b = l * NCOL + idx
        return self.cols[:, b:b + n]

    def reset_state(self):
        self.st = {"bank_rr": 0, "held": set(), "req_i": 0, "pending": [], "loaded": 0, "released": set()}

    def setup(self):
        I, S = self.I, self.S
        NL = self.NL
        C = [self.CONST]
        self.dma("pool", self.identB[:], I["c_ident"], writes=C)
        self.dma("pool", self.anti[:], I["c_anti"], writes=C)
        self.dma("pool", self.ones[:], I["c_ones"], writes=C)
        self.dma("pool", self.blk[:], I["c_blk"], writes=C)
        self.dma("sp", self.identF[:], I["c_ident"], writes=C)
        self.memset("pool", self.kt[:], 0.0, self.K)
        self.memset("pool", self.vt[:], 0.0, self.V)
        self.memset("pool", self.vc[:], 0.0, [self.VCb])
        self.memset("pool", self.xa[:], 0.0, self.XA)
        self.memset("pool", self.ch[:], 0.0, self.CH)
        self.memset("dve", self.kblk[0][:], 0.0, self.KB[0])
        self.memset("dve", self.kblk[1][:], 0.0, self.KB[1])
        self.memset("dve", self.vt[:].rearrange("p b (h e) -> p b h e", e=65)[:, :, :, 64:65], 1.0, self.V)
        self.memset("dve", self.vc[:].rearrange("p b (h e) -> p b h e", e=65)[:, :, :, 64:65], 1.0, [self.VCb])
        self.memset("dve", self.small[:], 0.0, self.SM)
        CL = [self.COLS]
        for l in range(NL):
            for i, nm in enumerate(("g_ff1", "g_mix", "g_ff2")):
                self.dma("sp", self.col(l, C_G + 8 * i, 8), I[nm][l].rearrange("(c p) -> p c", p=128), writes=CL, slow=True)
            for cc in range(4):
                self.dma("sp", self.col(l, C_CAW, 124).rearrange("p (k c) -> p k c", c=4)[:, :, cc],
                         I["conv_a_w"][l][:, cc * 128:(cc + 1) * 128].rearrange("k p -> p k"), writes=CL, slow=True)
                self.dma("sp", self.col(l, C_CBW, 12).rearrange("p (k c) -> p k c", c=4)[:, :, cc],
                         I["conv_b_w"][l][:, cc * 128:(cc + 1) * 128].rearrange("k p -> p k"), writes=CL, slow=True)
            for idx, nm in ((C_CAB, "conv_a_b"), (C_LNG, "ln_a_g"), (C_LNB, "ln_a_b")):
                self.dma("sp", self.col(l, idx, 4), I[nm][l].rearrange("(c p) -> p c", p=128), writes=CL, slow=True)
            for idx, nm in ((C_GQ, "q_norm_g"), (C_GK, "k_norm_g")):
                for hh in range(2):
                    self.dma("sp", self.cols[hh * 64:(hh + 1) * 64, l * NCOL + idx:l * NCOL + idx + 1],
                             I[nm][l].rearrange("(p o) -> p o", o=1), writes=CL, slow=True)
            self.dma("sp", self.gkrow[:, l, :].rearrange("p (h d) -> p h d", h=8),
                     bass.AP(I["k_norm_g"].tensor, l * 64, [[0, 128], [0, 8], [1, 64]]), writes=CL)
            self.ts("dve", self.col(l, C_GQ), self.col(l, C_GQ), 0.125, None, ALU.mult, None, reads=CL, writes=CL)
        for t_ in range(2):
            self.dma("sp", self.scT[:, :, t_], I["cvec"][t_].rearrange("(c p) -> p c", p=128), writes=[self.SCT], slow=True)
        self.act(self.scT[:], self.scT[:], AF.Silu, reads=[self.SCT], writes=[self.SCT])

    def ada(self, l):
        I = self.I
        CL = [self.COLS]
        ML = [self.MODS]
        wv = I["w_ada"][l].rearrange("(kc p) n -> p kc n", p=128)
        for ct in range(36):
            hb = ct % 2
            stg = self.harena[:, hb * 4096:(hb + 1) * 4096].bitcast(F32).rearrange("p (k n) -> p k n", k=KC)
            HB = self.H[hb * 8:hb * 8 + 8]
            self.dma("sp", stg, wv[:, :, ct * 256:(ct + 1) * 256], writes=HB)
            ti = self.rot("tf", 3)
            self.dma("sp", self.tmpf[ti][0:2, 0:256], bass.AP(I["b_ada"].tensor, l * 9 * D + ct * 256, [[0, 2], [1, 256]]),
                     writes=[self.TF[ti]])
            b = self.bank()
            for kc in range(KC):
                self.mm(self.ps[b][0:2, 0:256], self.scT[:, kc, :], stg[:, kc, :], kc == 0, kc == KC - 1,
                        reads=[self.SCT] + HB, bankbuf=self.PS[b])
            self.tt("dve", self.tmpf[ti][0:2, 0:256], self.ps[b][0:2, 0:256], self.tmpf[ti][0:2, 0:256], ALU.add,
                    reads=[self.PS[b], self.TF[ti]], writes=[self.TF[ti]])
            b2 = self.bank()
            for i in range(2):
                self.tr(self.ps[b2][:, 2 * i:2 * i + 2], self.tmpf[ti][0:2, i * 128:(i + 1) * 128], self.identF[0:2, 0:2],
                        reads=[self.TF[ti], self.CONST], bankbuf=self.PS[b2])
            self.cp("dve", self.col(l, C_MOD + ct * 4, 4), self.ps[b2][:, 0:4], reads=[self.PS[b2]], writes=ML)
        for i in range(3):
            sc = self.col(l, C_MOD + (3 * i + 1) * 16, 16)
            a = self.col(l, C_A + i * 16, 16)
            self.ts("dve", a, sc, 1.0, None, ALU.add, None, reads=ML, writes=ML)
            g = self.col(l, C_G + 8 * i, 8)
            self.tt("dve", a.rearrange("p (c t) -> p c t", t=2), a.rearrange("p (c t) -> p c t", t=2),
                    g.rearrange("p (c o) -> p c o", o=1).broadcast_to([128, 8, 2]), ALU.mult, reads=CL + ML, writes=ML)
            gt = self.col(l, C_GT + i * 16, 16)
            self.ts("dve", gt, self.col(l, C_MOD + (3 * i + 2) * 16, 16), 1.0 if i == 1 else 0.5, None, ALU.mult, None,
                    reads=ML, writes=ML)

    def convert(self, l, which):
        I, S = self.I, self.S

        def kslabs(key, w, ncol, kc, nslab):
            wv = w.rearrange("(kc p) n -> p kc n", p=128)
            for s in range(nslab):
                self.dma("pool", S[key][s].rearrange("p (a b) -> p a b", a=kc), wv[:, :, s * ncol:(s + 1) * ncol],
                         writes=[self.Dw[key]], sem=self.csem[key], throttle=False)

        def dslabs(key, w):
            wv = w.rearrange("(j p) n -> p j n", p=128)
            for mc in range(8):
                for hf in range(2):
                    self.dma("pool", S[key][mc * 2 + hf].rearrange("p (a b) -> p a b", a=11),
                             wv[:, hf * 11:(hf + 1) * 11, mc * 128:(mc + 1) * 128],
                             writes=[self.Dw[key]], sem=self.csem[key], throttle=False)

        if which == "p1":
            kslabs(f"g1_{l}", I["w_ff1_gate"][l], 256, 8, 11)
            kslabs(f"u1_{l}", I["w_ff1_up"][l], 256, 8, 11)
            dslabs(f"d1_{l}", I["w_ff1_down"][l])
            kslabs(f"in_{l}", I["w_in"][l], 256, 8, 28)
        else:
            kslabs(f"oa_{l}", I["w_a_out"][l], 256, 4, 4)
            kslabs(f"ob_{l}", I["w_b_out"][l], 256, 4, 4)
            kslabs(f"oc_{l}", I["w_c_out"][l], 256, 4, 4)
            kslabs(f"mg_{l}", I["w_merge"][l], 256, 8, 4)
            kslabs(f"g2_{l}", I["w_ff2_gate"][l], 256, 8, 11)
            kslabs(f"u2_{l}", I["w_ff2_up"][l], 256, 8, 11)
            dslabs(f"d2_{l}", I["w_ff2_down"][l])

    def convert_p1(self, l):
        self.convert(l, "p1")

    def convert_p2(self, l):
        self.convert(l, "p2")

    def table_tasks(self, l):
        I, S = self.I, self.S
        CL = [self.COLS]
        NP_ = 1472
        kcf = self.kc[:].rearrange("p c t -> p (c t)")
        KCB = [self.KCb] + self.KC4
        rpt = S[f"rp_{l}"].tensor

        def neg_view(i_):
            if i_ < 4:
                return self.sA[:, i_, :], self.SA[i_]
            if i_ < 8:
                return self.tB[:, i_ - 4, :], self.TB[i_ - 4]
            return self.oT[:, i_ - 8, :], self.OT[i_ - 8]

        def loads(h):
            pb = self.kblk[h % 2][:].rearrange("p c t -> p (c t)")
            for krl in range(4):
                src = bass.AP(rpt, h * NP_ + krl * RP_C, [[1, 32], [1, 1232]])
                self.dma("pool", pb[krl * 32:(krl + 1) * 32, 0:1232], src, reads=[self.Dw[f"rp_{l}"]], writes=self.KB[h % 2])

        def sub0():
            for cc in range(4):
                for hf in range(2):
                    si = cc * 2 + hf
                    for i in range(16):
                        k = hf * 16 + i
                        if k < 31:
                            self.ts("pool", kcf[:, i * 128:(i + 1) * 128], self.identF[:], self.col(l, C_CAW + k * 4 + cc), None,
                                    ALU.mult, None, reads=CL + [self.CONST], writes=KCB)
                        else:
                            self.memset("pool", kcf[:, i * 128:(i + 1) * 128], 0.0, KCB)
                    self.dma("pool", S[f"ca_{l}"][si], kcf, reads=KCB, writes=[self.Dw[f"ca_{l}"]], sem=self.csem[f"ca_{l}"], throttle=False)
            for k in range(3):
                for cc in range(4):
                    i = k * 4 + cc
                    self.ts("pool", kcf[:, i * 128:(i + 1) * 128], self.identF[:], self.col(l, C_CBW + k * 4 + cc), None,
                            ALU.mult, None, reads=CL + [self.CONST], writes=KCB)
            self.dma("pool", S[f"cb_{l}"][0], kcf[:, 0:1536], reads=KCB, writes=[self.Dw[f"cb_{l}"]], sem=self.csem[f"cb_{l}"], throttle=False)
            VB = [self.VCb]
            rpb16 = self.vc[:].rearrange("p b f -> p (b f)")[0:8, 0:NP_]
            self.memset("pool", rpb16, 0.0, VB)
            self.dma("pool", rpb16[:, 0:RP_R * RP_C].rearrange("p (r c) -> p r c", c=RP_C)[:, 4:19, 16:47], I["rpb"][l], reads=(), writes=VB, slow=True)
            self.dma("pool", S[f"rp_{l}"][:, 0:NP_], rpb16, reads=VB, writes=[self.Dw[f"rp_{l}"]], sem=self.csem[f"rp_{l}"], throttle=False)
            self.memset("pool", self.vc[:].rearrange("p b (h e) -> p b h e", e=65)[:, :, :, 64:65], 1.0, VB)
            for i_ in range(len(self.mask_list)):
                v_, b_ = neg_view(i_)
                self.dma("pool", v_, I["c_mask"][i_], writes=[b_])
                self.ts("pool", v_, v_, 30000.0, -30000.0, ALU.mult, ALU.add, reads=[b_], writes=[b_])
            loads(0)

        def sub_h(h):
            def f():
                if h + 1 < 8:
                    loads(h + 1)
                pb = self.kblk[h % 2][:].rearrange("p c t -> p (c t)")
                for cp in range(3):
                    off = [0, -8, -16][cp]
                    base = pb[:, 472 + off:473 + off]
                    srcv = bass.AP(base.tensor, base.offset, [list(base.ap[0]), [252, 4], [-RP_C, 8], [-1, 16]])
                    mi = self.rot("btc", 3)
                    self.cp("pool", self.maskt[:, mi, :].rearrange("p (g a q) -> p g a q", g=4, a=8), srcv, reads=self.KB[h % 2], writes=[self.MK[mi]])
                    for id_ in sorted(set(self.mask_idx[(t_, cp)] for t_ in range(self.nlt))):
                        v_, b_ = neg_view(id_)
                        ki = self.rot("bts", 4)
                        self.tt("pool", self.kc[:, ki, :], self.maskt[:, mi, :], v_, ALU.add, reads=[self.MK[mi], b_, self.KCb], writes=[self.KC4[ki]])
                        self.dma("pool", S[f"bt_{l}"][id_, h], self.kc[:, ki, :], reads=[self.KC4[ki]],
                                 writes=[self.Dw[f"bt_{l}"]], sem=self.csem[f"bt_{l}"], throttle=False)
            return f

        return [sub0] + [sub_h(h) for h in range(8)]

    def norm(self, l, i, xb, ub, tokm):
        b = self.bank()
        for c in range(KC):
            si = self.rot("sq", 2)
            self.act(self.sqb[si][:], self.xt[xb][:, c, :], AF.Square, reads=[self.X[xb][c]], writes=[self.SQ[si]])
            self.mm(self.ps[b][:], self.ones[:], self.sqb[si][:], c == 0, c == KC - 1,
                    reads=[self.CONST, self.SQ[si]], bankbuf=self.PS[b])
        s1 = self.rot("st", 3)
        self.act(self.stat[s1][:], self.ps[b][:], AF.Sqrt, reads=[self.PS[b]], writes=[self.STB[s1]],
                 bias=self.col(0, C_EPS), scale=1.0 / D)
        self.recip(self.stat[s1][:], self.stat[s1][:], reads=[self.STB[s1]], writes=[self.STB[s1]])
        for c in range(KC):
            ti = self.rot("tf", 3)
            self.stt("dve", self.tmpf[ti][:], self.xt[xb][:, c, :], self.col(l, C_A + (i * 8 + c) * 2 + tokm),
                     self.stat[s1][:], ALU.mult, ALU.mult,
                     reads=[self.X[xb][c], self.COLS, self.MODS, self.STB[s1]], writes=[self.TF[ti]])
            self.act(self.ut[ub][:, c, :], self.tmpf[ti][:], AF.Identity, reads=[self.TF[ti], self.COLS, self.MODS],
                     writes=[self.U[ub][c]], bias=self.col(l, C_MOD + (3 * i * 8 + c) * 2 + tokm))

    def ffn(self, l, f, xb, ub, tokm, mid=None):
        gi = 0 if f == "1" else 2
        S = self.S
        kg, ku, kd = f"g{f}_{l}", f"u{f}_{l}", f"d{f}_{l}"
        for s in range(11):
            sg = self.slab(S[kg][s], 2048, self.Dw[kg], (8, 256))
            su = self.slab(S[ku][s], 2048, self.Dw[ku], (8, 256))
            for jj in range(2):
                j = 2 * s + jj
                bg = self.bank()
                for kc in range(KC):
                    self.mm(self.ps[bg][:], sg.ap[:, kc, jj * 128:(jj + 1) * 128], self.ut[ub][:, kc, :], kc == 0, kc == KC - 1,
                            reads=[sg.buf, self.U[ub][kc]], bankbuf=self.PS[bg])
                bu = self.bank()
                for kc in range(KC):
                    self.mm(self.ps[bu][:], su.ap[:, kc, jj * 128:(jj + 1) * 128], self.ut[ub][:, kc, :], kc == 0, kc == KC - 1,
                            reads=[su.buf, self.U[ub][kc]], bankbuf=self.PS[bu])
                ti = self.rot("tf", 3)
                self.act(self.tmpf[ti][:], self.ps[bg][:], AF.Silu, reads=[self.PS[bg]], writes=[self.TF[ti]])
                self.tt("dve", self.hv(j), self.tmpf[ti][:], self.ps[bu][:], ALU.mult,
                        reads=[self.TF[ti], self.PS[bu]], writes=[self.H[j]])
            self.done(sg)
            self.done(su)
        if mid is not None:
            mid()
        for mc in range(8):
            b = self.bank()
            for hf in range(2):
                sd = self.slab(S[kd][mc * 2 + hf], 11 * 128, self.Dw[kd], (11, 128))
                for jj in range(11):
                    j = hf * 11 + jj
                    self.mm(self.ps[b][:], sd.ap[:, jj, :], self.hv(j), j == 0, j == FC - 1,
                            reads=[sd.buf, self.H[j]], bankbuf=self.PS[b])
                self.done(sd)
            self.stt("dve", self.xt[xb][:, mc, :], self.ps[b][:], self.col(l, C_GT + (gi * 8 + mc) * 2 + tokm),
                     self.xt[xb][:, mc, :], ALU.mult, ALU.add,
                     reads=[self.PS[b], self.X[xb][mc], self.COLS, self.MODS], writes=[self.X[xb][mc]])

    def load_x_tokmajor(self, ti, xb):
        I = self.I
        src = I["xp"] if ti < self.nct else I["xs"]
        r0 = (ti if ti < self.nct else ti - self.nct) * T
        for blk in range(4):
            tb = self.rot("tm", 2)
            self.dma("sp", self.tm[tb][:], src[r0 + blk * 128:r0 + (blk + 1) * 128, :], writes=[self.TM[tb]])
            for c0 in (0, 4):
                b = self.bank()
                for c in range(c0, c0 + 4):
                    self.tr(self.ps[b][:, (c - c0) * 128:(c - c0 + 1) * 128], self.tm[tb][:, c * 128:(c + 1) * 128], self.identF[:],
                            reads=[self.TM[tb], self.CONST], bankbuf=self.PS[b])
                self.cp("act" if c0 == 0 else "dve", self.xt[xb][:, c0:c0 + 4, blk * 128:(blk + 1) * 128],
                        self.ps[b][:].rearrange("p (c t) -> p c t", c=4), reads=[self.PS[b]], writes=self.X[xb][c0:c0 + 4])

    def store_y_tokmajor(self, ti, xb):
        O = self.O
        dst = O["yp"] if ti < self.nct else O["ys"]
        r0 = (ti if ti < self.nct else ti - self.nct) * T
        for blk in range(4):
            tb = self.rot("tm", 2)
            for c0 in (0, 4):
                b = self.bank()
                for c in range(c0, c0 + 4):
                    self.tr(self.ps[b][:, (c - c0) * 128:(c - c0 + 1) * 128], self.xt[xb][:, c, blk * 128:(blk + 1) * 128], self.identF[:],
                            reads=[self.X[xb][c], self.CONST], bankbuf=self.PS[b])
                self.cp("act" if c0 == 0 else "dve", self.tm[tb][:, c0 * 128:(c0 + 4) * 128], self.ps[b][:],
                        reads=[self.PS[b]], writes=[self.TM[tb]])
            ev = self.dma("sp", dst[r0 + blk * 128:r0 + (blk + 1) * 128, :], self.tm[tb][:], reads=[self.TM[tb]], writes=[self.Dout])
            self.final_events.append(ev)

    def p1_load(self, l, ti, a):
        S = self.S
        tok0 = ti * T
        if l == 0:
            self.load_x_tokmajor(ti, a)
        else:
            self.dma("sp", self.xt[a][:], S["xres"][:, tok0:tok0 + T].rearrange("(c p) t -> p c t", p=128),
                     reads=[self.Dt["xres"][ti]], writes=self.X[a])

    def p1_compute(self, l, ti, a, prefetch):
        S = self.S
        ctx = ti < self.nct
        tokm = 1 if ctx else 0
        tok0 = ti * T
        xb, ub, ub2 = a, a, 1 - a
        self.norm(l, 0, xb, ub, tokm)
        self.ffn(l, "1", xb, ub, tokm, mid=prefetch)
        self.norm(l, 1, xb, ub2, tokm)
        self.dma("sp", S["u2s"][:, tok0:tok0 + T].rearrange("(c p) t -> p c t", p=128), self.ut[ub2][:],
                 reads=self.U[ub2], writes=[self.Dt["u2s"][ti]])
        self.dma("sp", S["xres"][:, tok0:tok0 + T].rearrange("(c p) t -> p c t", p=128), self.xt[xb][:],
                 reads=self.X[xb], writes=[self.Dt["xres"][ti]])
        self.proj_early(l, ti, ub2)

    def in_slab(self, l, s):
        return self.slab(self.S[f"in_{l}"][s], 2048, self.Dw[f"in_{l}"], (8, 256))

    def proj_chunk(self, sl, jj, ub):
        b = self.bank()
        for kc in range(KC):
            self.mm(self.ps[b][:], sl.ap[:, kc, jj * 128:(jj + 1) * 128], self.ut[ub][:, kc, :], kc == 0, kc == KC - 1,
                    reads=[sl.buf, self.U[ub][kc]], bankbuf=self.PS[b])
        return b

    def xa_view(self, cc, ctx):
        if ctx:
            return self.xa[:, cc, :].rearrange("p (s w) -> p s w", s=2)[:, :, 15:271]
        return self.xa[:, cc, 15:15 + T]

    def ch_view(self, cc, ctx):
        if ctx:
            return self.ch[:, cc, :].rearrange("p (s w) -> p s w", s=2)[:, :, 1:257]
        return self.ch[:, cc, 1:1 + T]

    def v3(self, ap, ctx):
        return ap.rearrange("p (s w) -> p s w", s=2) if ctx else ap

    def proj_early(self, l, ti, ub):
        S, I, O = self.S, self.I, self.O
        ctx = ti < self.nct
        tok0 = ti * T
        for p in range(2):
            sv = self.in_slab(l, p)
            sg = self.in_slab(l, 2 + p)
            for jj in range(2):
                cc = 2 * p + jj
                bv = self.proj_chunk(sv, jj, ub)
                bg = self.proj_chunk(sg, jj, ub)
                ti_ = self.rot("tf", 3)
                self.act(self.tmpf[ti_][:], self.ps[bg][:], AF.Sigmoid, reads=[self.PS[bg]], writes=[self.TF[ti_]])
                self.tt("dve", self.xa_view(cc, ctx), self.v3(self.tmpf[ti_][:], ctx), self.v3(self.ps[bv][:], ctx), ALU.mult,
                        reads=[self.TF[ti_], self.PS[bv]], writes=[self.XA[cc]])
            self.done(sv)
            self.done(sg)
        for cc in range(4):
            self.dma("sp", S["xas"][cc * 128:(cc + 1) * 128, tok0:tok0 + T] if not ctx else
                     S["xas"][cc * 128:(cc + 1) * 128, tok0:tok0 + T].rearrange("p (s w) -> p s w", s=2),
                     self.xa_view(cc, ctx), reads=[self.XA[cc]], writes=[self.Dt["xas"][ti]])
        for p in range(2):
            sb = self.in_slab(l, 4 + p)
            for jj in range(2):
                cc = 2 * p + jj
                bb = self.proj_chunk(sb, jj, ub)
                self.cp("act", self.bg[:, cc, :], self.ps[bb][:], reads=[self.PS[bb]], writes=[self.BG[cc]])
            self.done(sb)
        self.dma("sp", S["bgs"][:, tok0:tok0 + T].rearrange("(c p) t -> p c t", p=128), self.bg[:],
                 reads=self.BG, writes=[self.Dt["bgs"][ti]])
        for p in range(2):
            sc = self.in_slab(l, 6 + p)
            sh = self.in_slab(l, 8 + p)
            for jj in range(2):
                cc = 2 * p + jj
                bc = self.proj_chunk(sc, jj, ub)
                bh = self.proj_chunk(sh, jj, ub)
                ti_ = self.rot("tf", 3)
                self.cp("act", self.tmpf[ti_][:], self.ps[bc][:], reads=[self.PS[bc]], writes=[self.TF[ti_]])
                self.tt("dve", self.ch_view(cc, ctx), self.v3(self.tmpf[ti_][:], ctx), self.v3(self.ps[bh][:], ctx), ALU.mult,
                        reads=[self.TF[ti_], self.PS[bh]], writes=[self.CH[cc]])
            self.done(sc)
            self.done(sh)
        for cc in range(4):
            self.dma("sp", S["chs"][cc * 128:(cc + 1) * 128, tok0:tok0 + T] if not ctx else
                     S["chs"][cc * 128:(cc + 1) * 128, tok0:tok0 + T].rearrange("p (s w) -> p s w", s=2),
                     self.ch_view(cc, ctx), reads=[self.CH[cc]], writes=[self.Dt["chs"][ti]])
        for which, s0, dst, DB, gcol in (("q", 10, self.qt, self.Q, C_GQ), ("k", 12, self.kt, self.K, C_GK)):
            for p in range(2):
                sl = self.in_slab(l, s0 + p)
                for jj in range(2):
                    cc = 2 * p + jj
                    bq = self.proj_chunk(sl, jj, ub)
                    si = self.rot("sq", 2)
                    self.act(self.sqb[si][:], self.ps[bq][:], AF.Square, reads=[self.PS[bq]], writes=[self.SQ[si]])
                    bs = self.bank()
                    self.mm(self.ps[bs][:], self.blk[:], self.sqb[si][:], True, True, reads=[self.CONST, self.SQ[si]], bankbuf=self.PS[bs])
                    s1 = self.rot("st", 3)
                    self.act(self.stat[s1][:], self.ps[bs][:], AF.Sqrt, reads=[self.PS[bs]], writes=[self.STB[s1]],
                             bias=self.col(0, C_EPS), scale=1.0 / HD)
                    self.recip(self.stat[s1][:], self.stat[s1][:], reads=[self.STB[s1]], writes=[self.STB[s1]])
                    self.stt("dve", dst[:, cc, 0:T], self.ps[bq][:], self.col(l, gcol), self.stat[s1][:], ALU.mult, ALU.mult,
                             reads=[self.PS[bq], self.STB[s1], self.COLS], writes=[DB[cc]])
                self.done(sl)
            nm = "qs" if which == "q" else "ks"
            self.dma("sp", S[nm][:, tok0:tok0 + T].rearrange("(c p) t -> p c t", p=128), dst[:, :, 0:T],
                     reads=DB, writes=[self.Dt[nm][ti]])
        s14 = self.in_slab(l, 14)
        s15 = self.in_slab(l, 15)
        for blk in range(4):
            b = self.bank()
            for hf, sl in ((0, s14), (1, s15)):
                for kc in range(KC):
                    self.mm(self.ps[b][:, hf * 256:(hf + 1) * 256], self.ut[ub][:, kc, blk * 128:(blk + 1) * 128], sl.ap[:, kc, :],
                            kc == 0, kc == KC - 1, reads=[sl.buf, self.U[ub][kc]], bankbuf=self.PS[b])
            self.cp("act", self.vt[:, blk, :].rearrange("p (h e) -> p h e", e=65)[:, :, 0:64],
                    self.ps[b][:].rearrange("p (h d) -> p h d", d=64), reads=[self.PS[b]], writes=[self.V[blk]])
            if ctx:
                tb = self.rot("tm", 2)
                self.cp("dve", self.tm[tb][:, 0:512], self.ps[b][:], reads=[self.PS[b]], writes=[self.TM[tb]])
                seq = ti * 2 + blk // 2
                ev = self.dma("sp", O["nv"][seq, l, (blk % 2) * 128:(blk % 2 + 1) * 128, :], self.tm[tb][:, 0:512],
                              reads=[self.TM[tb]], writes=[self.Dout])
                self.final_events.append(ev)
        self.done(s14)
        self.done(s15)
        self.dma("sp", S["vs"][tok0:tok0 + T, :].rearrange("(b p) f -> p b f", p=128), self.vt[:, 0:4, :],
                 reads=self.V[0:4], writes=[self.Dt["vs"][ti]])
        if ctx:
            s12 = self.in_slab(l, 12)
            s13 = self.in_slab(l, 13)
            for blk in range(4):
                b = self.bank()
                for hf, sl in ((0, s12), (1, s13)):
                    for kc in range(KC):
                        self.mm(self.ps[b][:, hf * 256:(hf + 1) * 256], self.ut[ub][:, kc, blk * 128:(blk + 1) * 128], sl.ap[:, kc, :],
                                kc == 0, kc == KC - 1, reads=[sl.buf, self.U[ub][kc]], bankbuf=self.PS[b])
                ti_ = self.rot("tf", 3)
                self.act(self.tmpf[ti_][:], self.ps[b][:], AF.Square, reads=[self.PS[b]], writes=[self.TF[ti_]])
                sm = self.rot("sm", 4)
                ss = self.small[:, sm * 16:sm * 16 + 8]
                self.op("dve", lambda e, o=ss, i=self.tmpf[ti_][:].rearrange("p (h d) -> p h d", d=64):
                        e.tensor_reduce(out=o, in_=i, axis=AX.X, op=ALU.add), reads=[self.TF[ti_]], writes=[self.SM[sm]])
                self.act(ss, ss, AF.Sqrt, reads=[self.SM[sm]], writes=[self.SM[sm]], bias=self.col(0, C_EPS), scale=1.0 / HD)
                self.recip(ss, ss, reads=[self.SM[sm]], writes=[self.SM[sm]])
                self.tt("dve", self.tmpf[ti_][:].rearrange("p (h d) -> p h d", d=64), self.ps[b][:].rearrange("p (h d) -> p h d", d=64),
                        ss.rearrange("p (h o) -> p h o", o=1).broadcast_to([128, 8, 64]), ALU.mult,
                        reads=[self.PS[b], self.SM[sm]], writes=[self.TF[ti_]])
                tb = self.rot("tm", 2)
                self.tt("dve", self.tm[tb][:, 0:512], self.tmpf[ti_][:], self.gkrow[:, l, :], ALU.mult,
                        reads=[self.TF[ti_], self.COLS], writes=[self.TM[tb]])
                seq = ti * 2 + blk // 2
                ev = self.dma("sp", O["nk"][seq, l, (blk % 2) * 128:(blk % 2 + 1) * 128, :], self.tm[tb][:, 0:512],
                              reads=[self.TM[tb]], writes=[self.Dout])
                self.final_events.append(ev)
            self.done(s12)
            self.done(s13)
    def p2_load(self, l, ti, a):
        S, I = self.S, self.I
        ctx = ti < self.nct
        tokm = 1 if ctx else 0
        tok0 = ti * T
        lt = ti - self.nct
        xb, ub = a, a
        fm = lambda nm: S[nm][:, tok0:tok0 + T].rearrange("(c p) t -> p c t", p=128)
        self.dma("sp", self.xt[xb][:], fm("xres"), reads=[self.Dt["xres"][ti]], writes=self.X[xb])
        self.dma("sp", self.ut[ub][:], fm("u2s"), reads=[self.Dt["u2s"][ti]], writes=self.U[ub])
        if ctx:
            for cc in range(4):
                self.dma("sp", self.xa_view(cc, True), S["xas"][cc * 128:(cc + 1) * 128, tok0:tok0 + T].rearrange("p (s w) -> p s w", s=2),
                         reads=[self.Dt["xas"][ti]], writes=[self.XA[cc]])
                self.dma("sp", self.ch_view(cc, True), S["chs"][cc * 128:(cc + 1) * 128, tok0:tok0 + T].rearrange("p (s w) -> p s w", s=2),
                         reads=[self.Dt["chs"][ti]], writes=[self.CH[cc]])
                xv = self.xa[:, cc, :].rearrange("p (s w) -> p s w", s=2)
                self.memset("pool", xv[:, :, 0:15], 0.0, [self.XA[cc]])
                self.memset("pool", xv[:, :, 271:286], 0.0, [self.XA[cc]])
                cv = self.ch[:, cc, :].rearrange("p (s w) -> p s w", s=2)
                self.memset("pool", cv[:, :, 0:1], 0.0, [self.CH[cc]])
                self.memset("pool", cv[:, :, 257:258], 0.0, [self.CH[cc]])
        else:
            first, last = lt == 0, lt == self.nlt - 1
            for nm, buf, BB, pad in (("xas", self.xa, self.XA, 15), ("chs", self.ch, self.CH, 1)):
                lo = tok0 - (0 if first else pad)
                hi = tok0 + T + (0 if last else pad)
                o0 = pad if first else 0
                rd = [self.Dt[nm][ti]] + ([] if first else [self.Dt[nm][ti - 1]]) + ([] if last else [self.Dt[nm][ti + 1]])
                self.dma("sp", buf[:, :, o0:o0 + hi - lo], S[nm][:, lo:hi].rearrange("(c p) t -> p c t", p=128), reads=rd, writes=BB)
                if first:
                    self.memset("pool", buf[:, :, 0:pad], 0.0, BB)
                if last:
                    self.memset("pool", buf[:, :, pad + T:pad + T + pad], 0.0, BB)
        self.dma("sp", self.bg[:], fm("bgs"), reads=[self.Dt["bgs"][ti]], writes=self.BG)
        self.dma("sp", self.qt[:], fm("qs"), reads=[self.Dt["qs"][ti]], writes=self.Q)
        if ctx:
            self.dma("sp", self.kt[:, :, 0:T], fm("ks"), reads=[self.Dt["ks"][ti]], writes=self.K)
            self.dma("sp", self.vt[:, 0:4, :], S["vs"][tok0:tok0 + T, :].rearrange("(b p) f -> p b f", p=128),
                     reads=[self.Dt["vs"][ti]], writes=self.V[0:4])
        else:
            rows = 8 * self.nlt
            w0 = 8 * lt - 4
            r_lo, r_hi = max(w0, 0), min(w0 + 16, rows)
            base = self.nct * T
            rd = [self.Dt["ks"][t2] for t2 in range(max(ti - 1, self.nct), min(ti + 2, self.NT))]
            self.dma("sp", self.kt[:, :, (r_lo - w0) * 64:(r_hi - w0) * 64],
                     S["ks"][:, base + r_lo * 64:base + r_hi * 64].rearrange("(c p) t -> p c t", p=128), reads=rd, writes=self.K)
    def p2_compute(self, l, ti, a, prefetch):
        S = self.S
        ctx = ti < self.nct
        tokm = 1 if ctx else 0
        tok0 = ti * T
        xb, ub = a, a
        if not ctx:
            self.attn_prep(ti, 0)
        self.branch_ab(l, ctx)
        if ctx:
            self.attn_ctx(l)
        else:
            self.attn_lat(l, ti)
        if prefetch is not None:
            prefetch()
        self.gates_merge(l, xb, ub, tokm)
        self.norm(l, 2, xb, ub, tokm)
        self.ffn(l, "2", xb, ub, tokm)
        if l == self.NL - 1:
            self.store_y_tokmajor(ti, xb)
        else:
            self.dma("sp", S["xres"][:, tok0:tok0 + T].rearrange("(c p) t -> p c t", p=128), self.xt[xb][:],
                     reads=self.X[xb], writes=[self.Dt["xres"][ti]])

    def conv(self, l, key, buf, BB, segw, ctx, evac):
        S = self.S
        segs = [(s_ * segw, s_ * 256, 256) for s_ in range(2)] if ctx else [(0, 0, T)]
        deferred = None
        if key == "ca":
            for cc in range(4):
                sls = [self.slab(S[f"ca_{l}"][cc * 2 + hf], 2048, self.Dw[f"ca_{l}"], (16, 128)) for hf in range(2)]
                b = self.bank()
                for (oi, oo, n) in segs:
                    for k in range(31):
                        sl = sls[k // 16]
                        self.mm(self.ps[b][:, oo:oo + n], sl.ap[:, k % 16, :], buf[:, cc, oi + k:oi + k + n], k == 0, k == 30,
                                reads=[sl.buf, BB[cc]], bankbuf=self.PS[b])
                for sl in sls:
                    self.done(sl)
                if deferred is not None:
                    deferred()
                deferred = evac(cc, b)
        else:
            sl = self.slab(S[f"cb_{l}"][0], 12 * 128, self.Dw[f"cb_{l}"], (12, 128))
            for cc in range(4):
                b = self.bank()
                for (oi, oo, n) in segs:
                    for k in range(3):
                        self.mm(self.ps[b][:, oo:oo + n], sl.ap[:, k * 4 + cc, :], buf[:, cc, oi + k:oi + k + n], k == 0, k == 2,
                                reads=[sl.buf, BB[cc]], bankbuf=self.PS[b])
                if deferred is not None:
                    deferred()
                deferred = evac(cc, b)
            self.done(sl)
        return deferred

    def branch_ab(self, l, ctx):
        bs1 = self.bank()
        self.hold(bs1)
        bs2 = self.bank()
        self.hold(bs2)

        def evac_a(cc, b):
            self.act(self.ha_f[cc], self.ps[b][:], AF.Identity, reads=[self.PS[b], self.COLS], writes=self.HA[cc],
                     bias=self.col(l, C_CAB + cc))
            si = self.rot("sq", 2)
            self.act(self.sqb[si][:], self.ps[b][:], AF.Square, reads=[self.PS[b], self.COLS], writes=[self.SQ[si]],
                     bias=self.col(l, C_CAB + cc))
            si2 = self.rot("sq", 2)
            self.act(self.sqb[si2][:], self.ps[b][:], AF.Identity, reads=[self.PS[b], self.COLS], writes=[self.SQ[si2]],
                     bias=self.col(l, C_CAB + cc))

            def stats():
                self.mm(self.ps[bs2][:], self.ones[:], self.sqb[si][:], cc == 0, cc == 3, reads=[self.CONST, self.SQ[si]], bankbuf=self.PS[bs2])
                self.mm(self.ps[bs1][:], self.ones[:], self.sqb[si2][:], cc == 0, cc == 3, reads=[self.CONST, self.SQ[si2]], bankbuf=self.PS[bs1])
            return stats

        last_a = self.conv(l, "ca", self.xa, self.XA, 286, ctx, evac_a)

        def evac_b(cc, b):
            self.tt("dve", self.tB[:, cc, :], self.ps[b][:], self.bg[:, cc, :], ALU.mult,
                    reads=[self.PS[b], self.BG[cc]], writes=[self.TB[cc]])
            if cc == 0:
                return last_a
            return None
        self.conv(l, "cb", self.ch, self.CH, 258, ctx, evac_b)
        m = self.rot("st", 3)
        self.op("act", lambda e, o=self.stat[m][:], i=self.ps[bs1][:]: e.mul(out=o, in_=i, mul=1.0 / 512), reads=[self.PS[bs1]], writes=[self.STB[m]])
        q = self.rot("st", 3)
        self.tt("dve", self.stat[q][:], self.stat[m][:], self.stat[m][:], ALU.mult, reads=[self.STB[m]], writes=[self.STB[q]])
        self.stt("dve", self.stat[q][:], self.ps[bs2][:], 1.0 / 512, self.stat[q][:], ALU.mult, ALU.subtract,
                 reads=[self.PS[bs2], self.STB[q]], writes=[self.STB[q]])
        self.act(self.stat[q][:], self.stat[q][:], AF.Sqrt, reads=[self.STB[q]], writes=[self.STB[q]], bias=self.col(0, C_EPS))
        self.recip(self.stat[q][:], self.stat[q][:], reads=[self.STB[q]], writes=[self.STB[q]])
        self.unhold(bs1)
        self.unhold(bs2)
        for cc in range(4):
            self.tt("dve", self.ha_f[cc], self.ha_f[cc], self.stat[m][:], ALU.subtract, reads=self.HA[cc] + [self.STB[m]], writes=self.HA[cc])
            self.tt("dve", self.ha_f[cc], self.ha_f[cc], self.stat[q][:], ALU.mult, reads=self.HA[cc] + [self.STB[q]], writes=self.HA[cc])
            self.act(self.sA[:, cc, :], self.ha_f[cc], AF.Silu, reads=self.HA[cc] + [self.COLS], writes=[self.SA[cc]],
                     bias=self.col(l, C_LNB + cc), scale=self.col(l, C_LNG + cc))

    def o_evac(self, bO, qb, half):
        sm = self.rot("sm", 4)
        rs = self.small[:, sm * 16:sm * 16 + 4]
        pv = self.ps[bO][:, 0:260].rearrange("p (h e) -> p h e", e=65)
        self.recip(rs.rearrange("p (h o) -> p h o", o=1), pv[:, :, 64:65], reads=[self.PS[bO]], writes=[self.SM[sm]])
        self.tt("dve", self.otm[qb][:, half * 256:(half + 1) * 256].rearrange("p (h d) -> p h d", d=64), pv[:, :, 0:64],
                rs.rearrange("p (h o) -> p h o", o=1).broadcast_to([128, 4, 64]), ALU.mult,
                reads=[self.PS[bO], self.SM[sm]], writes=[self.OTM[qb]])

    def o_transpose(self, qb, dst_view):
        b = self.bank()
        pb = self.ps[b][:].bitcast(BF16)
        for cc in range(4):
            self.tr(pb[:, cc * 128:(cc + 1) * 128], self.otm[qb][:, cc * 128:(cc + 1) * 128], self.identB[:],
                    reads=[self.OTM[qb], self.CONST], bankbuf=self.PS[b])
        self.cp("act", dst_view, pb[:, 0:512].rearrange("p (c q) -> p c q", c=4) if dst_view.ndim == 3 else
                pb[:, 0:512].rearrange("p (c a w) -> p c a w", c=4, a=8), reads=[self.PS[b]], writes=self.OT)

    def attn_ctx(self, l):
        for s in range(2):
            for half in range(2):
                bO = [self.bank(), self.bank()]
                for b_ in bO:
                    self.hold(b_)
                for hh in range(4):
                    h = half * 4 + hh
                    cc, p0 = h // 2, 64 * (h % 2)
                    bS = self.bank()
                    for kb in range(2):
                        self.mm(self.ps[bS][:, kb * 256:(kb + 1) * 256], self.kt[p0:p0 + 64, cc, s * 256 + kb * 128:s * 256 + (kb + 1) * 128],
                                self.qt[p0:p0 + 64, cc, s * 256:(s + 1) * 256], True, True,
                                reads=[self.K[cc], self.Q[cc]], bankbuf=self.PS[bS])
                    pi = self.rot("pc", 3)
                    self.act(self.pc[pi], self.ps[bS][:], AF.Exp, reads=[self.PS[bS]], writes=[self.PC[pi]])
                    for qb in range(2):
                        for kb in range(2):
                            self.mm(self.ps[bO[qb]][:, hh * 65:(hh + 1) * 65], self.pc[pi][:, kb * 256 + qb * 128:kb * 256 + (qb + 1) * 128],
                                    self.vt[:, s * 2 + kb, h * 65:(h + 1) * 65], kb == 0, kb == 1,
                                    reads=[self.PC[pi], self.V[s * 2 + kb]], bankbuf=self.PS[bO[qb]])
                for qb in range(2):
                    self.o_evac(bO[qb], s * 2 + qb, half)
                    self.unhold(bO[qb])
            for qb in range(2):
                q4 = s * 2 + qb
                self.o_transpose(q4, self.oT[:, :, q4 * 128:(q4 + 1) * 128])

    def attn_prep(self, ti, j):
        S = self.S
        lt = ti - self.nct
        rows = 8 * self.nlt
        w0 = 8 * lt - 4
        base = self.nct * T
        bs = BLK_START[j]
        kb_i = j % 2
        for gi in range(4):
            vb = kb_i * 4 + gi
            for krl in range(4):
                r = w0 + 4 * gi + krl
                if 0 <= r < rows:
                    t0 = base + r * 64 + bs
                    rd = [self.Dt["vs"][t0 // T]]
                    self.dma("sp", self.vt[krl * 32:(krl + 1) * 32, vb, :], S["vs"][t0:t0 + 32, :], reads=rd, writes=[self.V[vb]])
        for cc in range(4):
            self.cp("pool", self.kblk[kb_i][:, cc, :].rearrange("p (r w) -> p r w", w=32),
                    self.kt[:, cc, :].rearrange("p (r w) -> p r w", w=64)[:, :, bs:bs + 32],
                    reads=[self.K[cc]], writes=[self.KB[kb_i][cc]])

    def attn_lat(self, l, ti):
        S, I = self.S, self.I
        lt = ti - self.nct
        rows = 8 * self.nlt
        w0 = 8 * lt - 4
        base = self.nct * T
        if lt == 0:
            self.load_cache(l)
        for j in range(4):
            cp = CP_OF_J[j]
            bs = BLK_START[j]
            kb_i = j % 2
            if j + 1 < 4:
                self.attn_prep(ti, j + 1)
            slabs = [self.slab(S[f"bt_{l}"][self.mask_idx[(lt, cp)], hq * 4:(hq + 1) * 4].rearrange("h p k -> p h k"), 2048, self.Dw[f"bt_{l}"], (4, 512))
                     for hq in range(2)]
            bOs = {}

            def scores(h):
                half, hh = divmod(h, 4)
                cc, p0 = h // 2, 64 * (h % 2)
                qv = self.qt[p0:p0 + 64, cc, :].rearrange("p (a w) -> p a w", w=64)[:, :, 16 * j:16 * j + 16]
                bS = self.bank()
                for gi in range(4):
                    self.mm(self.ps[bS][:, gi * 128:(gi + 1) * 128], self.kblk[kb_i][p0:p0 + 64, cc, gi * 128:(gi + 1) * 128], qv,
                            True, True, reads=[self.KB[kb_i][cc], self.Q[cc]], bankbuf=self.PS[bS])
                bC = self.bank()
                for cb in range(4):
                    self.mm(self.ps[bC][:, cb * 128:(cb + 1) * 128], self.kc[p0:p0 + 64, cc, cb * 128:(cb + 1) * 128], qv,
                            True, True, reads=[self.KCb, self.Q[cc]], bankbuf=self.PS[bC])
                pi = self.rot("pl", 3)
                tfi = self.rot("tf", 3)
                self.tt("dve", self.tmpf[tfi][:], self.ps[bS][:], slabs[half].ap[:, hh, :], ALU.add,
                        reads=[self.PS[bS], slabs[half].buf], writes=[self.TF[tfi]])
                ci = self.rot("pc", 3)
                self.act(self.pc[ci], self.ps[bC][:], AF.Exp, reads=[self.PS[bC]], writes=[self.PC[ci]])
                self.act(self.pl[pi], self.tmpf[tfi][:], AF.Exp, reads=[self.TF[tfi]], writes=[self.PL[pi]])
                return (h, pi, ci)

            def pv(st_):
                h, pi, ci = st_
                half, hh = divmod(h, 4)
                bO = bOs[half]
                oview = self.ps[bO][:, hh * 65:(hh + 1) * 65]
                for gi in range(4):
                    self.mm(oview, self.pl[pi][:, gi * 128:(gi + 1) * 128], self.vt[:, kb_i * 4 + gi, h * 65:(h + 1) * 65],
                            gi == 0, False, reads=[self.PL[pi], self.V[kb_i * 4 + gi]], bankbuf=self.PS[bO])
                for cb in range(4):
                    self.mm(oview, self.pc[ci][:, cb * 128:(cb + 1) * 128], self.vc[:, cb, h * 65:(h + 1) * 65],
                            False, cb == 3, reads=[self.PC[ci], self.VCb], bankbuf=self.PS[bO])
                if hh == 3:
                    self.o_evac(bO, j, half)
                    self.unhold(bO)

            pending = None
            for h in range(8):
                if h % 4 == 0:
                    bOs[h // 4] = self.bank()
                    self.hold(bOs[h // 4])
                cur = scores(h)
                if pending is not None:
                    pv(pending)
                pending = cur
            pv(pending)
            for sl in slabs:
                self.done(sl)
            self.o_transpose(j, self.oT[:, :, :].rearrange("p c (a w) -> p c a w", w=64)[:, :, :, 16 * j:16 * j + 16])

    def load_cache(self, l):
        I = self.I
        for kb in range(4):
            self.dma("pool", self.vc[:, kb, :].rearrange("p (h e) -> p h e", e=65)[:, :, 0:64],
                     I["cv"][l, kb * 128:(kb + 1) * 128, :].rearrange("p (h d) -> p h d", d=64), writes=[self.VCb], slow=True)
        for kb in range(4):
            tb = self.rot("tm", 2)
            self.dma("sp", self.tm[tb][:, 0:512], I["ck"][l, kb * 128:(kb + 1) * 128, :], writes=[self.TM[tb]])
            b = self.bank()
            for cc in range(4):
                self.tr(self.ps[b][:, cc * 128:(cc + 1) * 128], self.tm[tb][:, cc * 128:(cc + 1) * 128], self.identF[:],
                        reads=[self.TM[tb], self.CONST], bankbuf=self.PS[b])
            self.cp("dve", self.kc[:, :, kb * 128:(kb + 1) * 128], self.ps[b][:].rearrange("p (c t) -> p c t", c=4),
                    reads=[self.PS[b]], writes=[self.KCb] + self.KC4)

    def gates_merge(self, l, xb, ub, tokm):
        S = self.S
        for p in range(4):
            sl = {}
            for bi, br in enumerate("abc"):
                sl["g" + br] = self.in_slab(l, 16 + 4 * bi + p)
                sl["o" + br] = self.slab(S[f"o{br}_{l}"][p], 1024, self.Dw[f"o{br}_{l}"], (4, 256))
            for jj in range(2):
                mc = 2 * p + jj
                mi = self.rot("ma", 2)
                for bi, (br, src, SB) in enumerate((("a", self.sA, self.SA), ("b", self.tB, self.TB), ("c", self.oT, self.OT))):
                    bg = self.proj_chunk(sl["g" + br], jj, ub)
                    by = self.bank()
                    for kc in range(4):
                        self.mm(self.ps[by][:], sl["o" + br].ap[:, kc, jj * 128:(jj + 1) * 128], src[:, kc, :], kc == 0, kc == 3,
                                reads=[sl["o" + br].buf, SB[kc]], bankbuf=self.PS[by])
                    ti_ = self.rot("tf", 3)
                    self.act(self.tmpf[ti_][:], self.ps[bg][:], AF.Sigmoid, reads=[self.PS[bg]], writes=[self.TF[ti_]])
                    if bi == 0:
                        self.tt("dve", self.macc[mi], self.tmpf[ti_][:], self.ps[by][:], ALU.mult,
                                reads=[self.TF[ti_], self.PS[by]], writes=self.MA[mi])
                    else:
                        self.tt("dve", self.tmpf[ti_][:], self.tmpf[ti_][:], self.ps[by][:], ALU.mult,
                                reads=[self.TF[ti_], self.PS[by]], writes=[self.TF[ti_]])
                        if bi == 1:
                            self.tt("pool", self.macc[mi], self.macc[mi], self.tmpf[ti_][:], ALU.add,
                                    reads=self.MA[mi] + [self.TF[ti_]], writes=self.MA[mi])
                        else:
                            self.tt("pool", self.hv(mc), self.macc[mi], self.tmpf[ti_][:], ALU.add,
                                    reads=self.MA[mi] + [self.TF[ti_]], writes=[self.H[mc]])
            for s_ in sl.values():
                self.done(s_)
        for p in range(4):
            sm = self.slab(S[f"mg_{l}"][p], 2048, self.Dw[f"mg_{l}"], (8, 256))
            for jj in range(2):
                mc2 = 2 * p + jj
                b = self.bank()
                for mc in range(8):
                    self.mm(self.ps[b][:], sm.ap[:, mc, jj * 128:(jj + 1) * 128], self.hv(mc), mc == 0, mc == 7,
                            reads=[sm.buf, self.H[mc]], bankbuf=self.PS[b])
                self.stt("dve", self.xt[xb][:, mc2, :], self.ps[b][:], self.col(l, C_GT + (8 + mc2) * 2 + tokm),
                         self.xt[xb][:, mc2, :], ALU.mult, ALU.add,
                         reads=[self.PS[b], self.X[xb][mc2], self.COLS, self.MODS], writes=[self.X[xb][mc2]])
            self.done(sm)
    def emit_all(self):
        NL = self.NL
        self.reset_state()
        self.setup()
        self.memset("dve", self.col(0, C_EPS), EPS, [self.MODS])
        self.convert_p1(0)
        self.ada(0)
        self.COLS.const = True
        self.CONST.const = True
        self.SCT.const = True
        tasks = self.table_tasks(0) + [lambda: self.convert_p2(0)]
        for l in range(1, NL):
            def ada_l(ll=l):
                self.MODS.const = False
                self.ada(ll)
                self.MODS.const = True
            tasks += self.table_tasks(l) + [ada_l, (lambda ll=l: self.convert_p1(ll)), (lambda ll=l: self.convert_p2(ll))]
        per_slot = -(-len(tasks) // self.NT)

        def run_task():
            tasks.pop(0)()

        self.MODS.const = True
        steps = []
        for l in range(NL):
            steps += [("p1", l, ti) for ti in range(self.NT)]
            steps += [("p2", l, ti) for ti in range(self.NT)]
        loaded = [-1]

        def load_step(k):
            ph, l_, ti_ = steps[k]
            loaded[0] = k
            (self.p1_load if ph == "p1" else self.p2_load)(l_, ti_, k % 2)

        for k, (ph, l, ti) in enumerate(steps):
            if loaded[0] < k:
                load_step(k)
            nxt = steps[k + 1] if k + 1 < len(steps) else None
            pf = None
            if nxt is not None and not (ph == "p1" and nxt[0] == "p2"):
                pf = (lambda kk=k + 1: load_step(kk))
            if ph == "p1":
                self.p1_compute(l, ti, k % 2, pf)
                for _ in range(per_slot):
                    if tasks:
                        run_task()
                if ti == self.NT - 1:
                    while l == 0 and tasks:
                        run_task()
            else:
                self.p2_compute(l, ti, k % 2, pf)
        if not self.dry:
            E = self.eng["pool"]
            deps = {}
            for ev in self.final_events:
                if ev is not None and deps.get(ev[0], 0) < ev[1]:
                    deps[ev[0]] = ev[1]
            waits = self._prune(E, deps)
            E.count += 1
            E.ops.append((waits, lambda e: e.memset(self.small[:, 60:64], 0.0), (E.sem, 1)))

    def replay(self):
        nc = self.nc
        sems = self.sem_handles

        def run(e, ops):
            for waits, fn, inc in ops:
                for s, v in waits:
                    e.wait_ge(sems[s], v)
                ins = fn(e)
                ins.then_inc(sems[inc[0]], inc[1])

        with nc.Block() as block:
            @block.tensor
            def _(e):
                run(e, self.eng["pe"].ops)

            @block.scalar
            def _(e):
                run(e, self.eng["act"].ops)

            @block.vector
            def _(e):
                run(e, self.eng["dve"].ops)

            @block.gpsimd
            def _(e):
                run(e, self.eng["pool"].ops)

            @block.sync
            def _(e):
                run(e, self.eng["sp"].ops)


def build_program(NL, nct, nlt):
    nc = bass.Bass("TRN2", target_bir_lowering=False)
    P = Prog(nc, NL, nct, nlt)
    P.declare()
    with contextlib.ExitStack() as stack:
        P.alloc(stack)
        P.dry = True
        P.emit_all()
        P.dry = False
        P.final_events = []
        P.emit_all()
        P.replay()
    return nc, P


def host_consts(P):
    ident = np.eye(128, dtype=np.float32)
    anti = np.ascontiguousarray(ident[::-1])
    ones = np.ones((128, 128), np.float32)
    blk = np.zeros((128, 128), np.float32)
    blk[:64, :64] = 1.0
    blk[64:, 64:] = 1.0
    return {"c_ident": ident, "c_anti": anti, "c_ones": ones, "c_blk": blk,
            "c_mask": np.stack(P.mask_list).astype(np.float32)}


_CACHE = {}


def run_cores(inputs, NL, nct, nlt, ncores):
    key = (NL, nct, nlt)
    if key not in _CACHE:
        _CACHE[key] = build_program(NL, nct, nlt)
    nc, P = _CACHE[key]
    cst = host_consts(P)
    f = lambda a: np.ascontiguousarray(np.asarray(a, dtype=np.float32))
    nseq = 2 * nct
    in_maps = []
    for c in range(ncores):
        m = dict(cst)
        m["xp"] = f(inputs["x_prompt"][c * nseq:(c + 1) * nseq]).reshape(nseq * 256, D)
        m["xs"] = f(inputs["x_sample"][c]).reshape(-1, D)
        m["ck"] = f(inputs["cache_k"][c]).reshape(NL, PAST, 512)
        m["cv"] = f(inputs["cache_v"][c]).reshape(NL, PAST, 512)
        m["cvec"] = np.stack([f(inputs["c"][c]), f(inputs["c_ctx"])])
        for nm, _ in WEIGHT_SHAPES:
            m[nm] = f(inputs[nm])
        in_maps.append(m)
    res = run_bass_kernel_spmd(nc, in_maps, core_ids=list(range(ncores)))
    outs = res.results
    yp = np.concatenate([o["yp"].reshape(nseq, 256, D) for o in outs], axis=0)
    ys = np.stack([o["ys"] for o in outs], axis=0)
    nk = np.concatenate([o["nk"].reshape(nseq, NL, 256, NH, HD) for o in outs], axis=0)
    nv = np.concatenate([o["nv"].reshape(nseq, NL, 256, NH, HD) for o in outs], axis=0)
    return (yp.astype(np.float32), ys.astype(np.float32), nk.astype(np.float32), nv.astype(np.float32))


def kernel(**inputs):
    return run_cores(inputs, 2, 2, 8, 8)
```

```python
import contextlib
import numpy as np
import concourse.bass as bass
import concourse.mybir as mybir
from concourse.bass_utils import run_bass_kernel_spmd

F32 = mybir.dt.float32
BF16 = mybir.dt.bfloat16
AF = mybir.ActivationFunctionType
ALU = mybir.AluOpType
AX = mybir.AxisListType

D = 1024
KC = 8
DFF = 2816
FC = 22
NIN = 7168
T = 512
NH = 8
HD = 64
PAST = 512
EPS = 1e-6
NSLOT = 9
SLOT = 2048
BLK_START = [0, 8, 24, 32]
CP_OF_J = [0, 1, 1, 2]
CP_OFF = [16, 8, 0]
RP_R, RP_C = 23, 63
NCOL = 420
C_G = 0
C_CAW = 24
C_CAB = 148
C_LNG = 152
C_LNB = 156
C_CBW = 160
C_GQ = 172
C_GK = 173
C_MOD = 176
C_A = 320
C_GT = 368
C_EPS = 416
WEIGHT_SHAPES = [("w_ada", [D, 9 * D]), ("b_ada", [9 * D]), ("g_ff1", [D]), ("w_ff1_gate", [D, DFF]), ("w_ff1_up", [D, DFF]),
                 ("w_ff1_down", [DFF, D]), ("g_mix", [D]), ("w_in", [D, NIN]), ("conv_a_w", [31, 512]),
                 ("conv_a_b", [512]), ("ln_a_g", [512]), ("ln_a_b", [512]), ("w_a_out", [512, D]),
                 ("conv_b_w", [3, 512]), ("w_b_out", [512, D]), ("q_norm_g", [64]), ("k_norm_g", [64]),
                 ("rpb", [8, 15, 31]), ("w_c_out", [512, D]), ("w_merge", [D, D]), ("g_ff2", [D]),
                 ("w_ff2_gate", [D, DFF]), ("w_ff2_up", [D, DFF]), ("w_ff2_down", [DFF, D])]


class Buf:
    __slots__ = ("name", "w", "r", "const")

    def __init__(self, name):
        self.name = name
        self.w = None
        self.r = {}
        self.const = False


class Eng:
    def __init__(self, name):
        self.name = name
        self.ops = []
        self.count = 0
        self.sem = None
        self.waited = {}
        self.dcount = 0


class DSem:
    def __init__(self, idx):
        self.idx = idx
        self.val = 0


class Slab:
    __slots__ = ("n", "slot", "ap", "buf")


def mask_for_tile(t, ntl, cp):
    rows = 8 * ntl
    kr_n = min(8, rows)
    m = np.zeros((128, 4, 8, 16), np.float32)
    j = [0, 1, 3][cp]
    bs = BLK_START[j]
    w0 = 8 * t - 4
    for gi in range(4):
        for krl in range(4):
            krow = w0 + 4 * gi + krl
            if krow < 0 or krow >= rows:
                continue
            for a in range(8):
                r = 8 * t + a
                rs = min(max(r - kr_n // 2, 0), rows - kr_n)
                if not (rs <= krow < rs + kr_n):
                    continue
                for kcl in range(32):
                    kc_ = bs + kcl
                    for qc in range(16):
                        q = 16 * j + qc
                        ws = min(max(q - 8, 0), 48)
                        if ws <= kc_ < ws + 16:
                            m[krl * 32 + kcl, gi, a, qc] = 1.0
    return m.reshape(128, 512)


class Prog:
    def __init__(self, nc, NL, n_ctx_tiles, n_lat_tiles):
        self.nc = nc
        self.NL = NL
        self.nct = n_ctx_tiles
        self.nlt = n_lat_tiles
        self.NT = n_ctx_tiles + n_lat_tiles
        self.nseq = 2 * n_ctx_tiles
        self.LT = T * n_lat_tiles
        self.NTOK = T * self.NT
        self.eng = {k: Eng(k) for k in ("pe", "act", "dve", "pool", "sp")}
        self.sem_handles = []
        self.dry = False
        self.requests = []
        self.req_list = []
        self.final_events = []
        pats = {}
        self.mask_idx = {}
        self.mask_list = []
        for t in range(n_lat_tiles):
            for cp in range(3):
                m = mask_for_tile(t, n_lat_tiles, cp)
                key = m.tobytes()
                if key not in pats:
                    pats[key] = len(self.mask_list)
                    self.mask_list.append(m)
                self.mask_idx[(t, cp)] = pats[key]

    def new_sem(self, stack, name):
        h = stack.enter_context(self.nc.semaphore(name))
        self.sem_handles.append(h)
        return len(self.sem_handles) - 1

    def _collect(self, reads, writes):
        deps = {}

        def add(ev):
            if ev is None:
                return
            s, v = ev
            if deps.get(s, 0) < v:
                deps[s] = v

        for b in reads:
            add(b.w)
        for b in writes:
            add(b.w)
            for s, v in b.r.items():
                add((s, v))
        return deps

    def _prune(self, E, deps):
        waits = []
        for s, v in deps.items():
            if E.name == "pe" and s == E.sem:
                continue
            if E.waited.get(s, 0) >= v:
                continue
            E.waited[s] = v
            waits.append((s, v))
        return waits

    def _mark(self, ev, reads, writes):
        s, v = ev
        for b in reads:
            if b.const:
                continue
            if b.r.get(s, 0) < v:
                b.r[s] = v
        for b in writes:
            b.w = ev
            b.r = {}

    def op(self, eng, fn, reads=(), writes=()):
        if self.dry:
            return None
        E = self.eng[eng]
        deps = self._collect(reads, writes)
        waits = self._prune(E, deps)
        E.count += 1
        ev = (E.sem, E.count)
        E.ops.append((waits, fn, (E.sem, 1)))
        self._mark(ev, reads, writes)
        return ev

    def dma(self, q, out, in_, reads=(), writes=(), sem=None, throttle=True, slow=False):
        if self.dry:
            return None
        E = self.eng[q]
        if sem is None:
            pool = self.dsems[q]
            S = pool[E.dcount % len(pool)]
            E.dcount += 1
        else:
            S = sem
        deps = self._collect(reads, writes)
        if throttle and S.val > 0:
            if deps.get(S.idx, 0) < S.val:
                deps[S.idx] = S.val
        waits = self._prune(E, deps)
        S.val += 16
        ev = (S.idx, S.val)
        E.ops.append((waits, (lambda e, o=out, i=in_, sl=slow: e.dma_start(out=o, in_=i, allow_slow_non_contiguous=True) if sl else e.dma_start(out=o, in_=i)), (S.idx, 16)))
        self._mark(ev, reads, writes)
        return ev

    def bank(self):
        st = self.st
        for _ in range(8):
            b = st["bank_rr"]
            st["bank_rr"] = (b + 1) % 8
            if b not in st["held"]:
                return b
        raise RuntimeError("no free psum bank")

    def hold(self, b):
        self.st["held"].add(b)

    def unhold(self, b):
        self.st["held"].discard(b)

    def rot(self, key, n):
        v = self.st.get(key, 0)
        self.st[key] = (v + 1) % n
        return v

    def slab(self, src_ap, nelem, srcbuf, shape):
        st = self.st
        n = st["req_i"]
        st["req_i"] += 1
        s = Slab()
        s.n = n
        s.slot = n % NSLOT
        s.buf = self.R[s.slot]
        base = self.ring[:, s.slot * SLOT: s.slot * SLOT + nelem]
        s.ap = base.rearrange("p (a b) -> p a b", a=shape[0])
        dst = s.ap if src_ap.ndim == 3 else base
        if self.dry:
            self.req_list.append((src_ap, dst, srcbuf))
            return s
        self.pump()
        assert st["loaded"] > n, "slab load not emitted"
        return s

    def pump(self):
        st = self.st
        L = self.req_list
        while st["loaded"] < len(L):
            m = st["loaded"]
            if m >= NSLOT and (m - NSLOT) not in st["released"]:
                break
            src_ap, dst, srcbuf = L[m]
            if srcbuf.w is None and m >= st["req_i"]:
                break
            slot = m % NSLOT
            self.dma("sp", dst, src_ap, reads=[srcbuf], writes=[self.R[slot]])
            st["loaded"] += 1
            st["released"].discard(m - NSLOT)

    def done(self, s):
        if self.dry:
            return
        self.st["released"].add(s.n)
        self.pump()

    def mm(self, out, lhsT, rhs, start, stop, reads, bankbuf):
        return self.op("pe", lambda e, o=out, l=lhsT, r=rhs, a=start, b=stop: e.matmul(o, l, r, start=a, stop=b),
                       reads=reads, writes=[bankbuf])

    def tr(self, out, in_, ident, reads, bankbuf):
        return self.op("pe", lambda e, o=out, i=in_, d=ident: e.transpose(o, i, d), reads=reads, writes=[bankbuf])

    def act(self, out, in_, func, reads, writes, bias=None, scale=None):
        kw = {}
        if bias is not None:
            kw["bias"] = bias
        if scale is not None:
            kw["scale"] = scale
        return self.op("act", lambda e, o=out, i=in_, f=func, k=kw: e.activation(out=o, in_=i, func=f, **k),
                       reads=list(reads) + [self.MODS, self.COLS], writes=writes)

    def tt(self, eng, out, in0, in1, op, reads, writes):
        return self.op(eng, lambda e, o=out, a=in0, b=in1, p=op: e.tensor_tensor(out=o, in0=a, in1=b, op=p),
                       reads=reads, writes=writes)

    def stt(self, eng, out, in0, scalar, in1, op0, op1, reads, writes):
        return self.op(eng, lambda e, o=out, a=in0, s=scalar, b=in1, p0=op0, p1=op1:
                       e.scalar_tensor_tensor(out=o, in0=a, scalar=s, in1=b, op0=p0, op1=p1),
                       reads=reads, writes=writes)

    def ts(self, eng, out, in0, s1, s2, op0, op1, reads, writes):
        if s2 is None:
            return self.op(eng, lambda e, o=out, a=in0, x=s1, p0=op0: e.tensor_scalar(out=o, in0=a, scalar1=x, scalar2=None, op0=p0),
                           reads=reads, writes=writes)
        return self.op(eng, lambda e, o=out, a=in0, x=s1, y=s2, p0=op0, p1=op1:
                       e.tensor_scalar(out=o, in0=a, scalar1=x, scalar2=y, op0=p0, op1=p1),
                       reads=reads, writes=writes)

    def cp(self, eng, out, in_, reads, writes):
        if eng == "act":
            return self.op("act", lambda e, o=out, i=in_: e.copy(out=o, in_=i), reads=reads, writes=writes)
        return self.op(eng, lambda e, o=out, i=in_: e.tensor_copy(out=o, in_=i), reads=reads, writes=writes)

    def recip(self, out, in_, reads, writes):
        return self.op("dve", lambda e, o=out, i=in_: e.reciprocal(out=o, in_=i), reads=reads, writes=writes)

    def memset(self, eng, ap, val, writes):
        return self.op(eng, lambda e, a=ap, v=val: e.memset(a, v), reads=(), writes=writes)

    def declare(self):
        nc = self.nc
        NL = self.NL

        def din(name, shape):
            return nc.dram_tensor(name, list(shape), F32, kind="ExternalInput").ap()

        def dout(name, shape):
            return nc.dram_tensor(name, list(shape), F32, kind="ExternalOutput").ap()

        def scr(name, shape, dt=BF16):
            return nc.dram_tensor(name, list(shape), dt, kind="Internal").ap()

        I = {}
        I["xp"] = din("xp", [self.nct * T, D])
        I["xs"] = din("xs", [self.LT, D])
        I["ck"] = din("ck", [NL, PAST, 512])
        I["cv"] = din("cv", [NL, PAST, 512])
        I["cvec"] = din("cvec", [2, D])
        for nm, sh in WEIGHT_SHAPES:
            I[nm] = din(nm, [NL] + sh)
        I["c_ident"] = din("c_ident", [128, 128])
        I["c_anti"] = din("c_anti", [128, 128])
        I["c_ones"] = din("c_ones", [128, 128])
        I["c_blk"] = din("c_blk", [128, 128])
        I["c_mask"] = din("c_mask", [len(self.mask_list), 128, 512])
        self.I = I
        O = {}
        O["yp"] = dout("yp", [self.nct * T, D])
        O["ys"] = dout("ys", [self.LT, D])
        O["nk"] = dout("nk", [self.nseq, NL, 256, 512])
        O["nv"] = dout("nv", [self.nseq, NL, 256, 512])
        self.O = O
        S = {}
        for l in range(NL):
            for f in ("1", "2"):
                S[f"g{f}_{l}"] = scr(f"wg{f}_{l}", [11, 128, 2048])
                S[f"u{f}_{l}"] = scr(f"wu{f}_{l}", [11, 128, 2048])
                S[f"d{f}_{l}"] = scr(f"wd{f}_{l}", [16, 128, 11 * 128])
            S[f"in_{l}"] = scr(f"win_{l}", [28, 128, 2048])
            for b in "abc":
                S[f"o{b}_{l}"] = scr(f"wo{b}_{l}", [4, 128, 4 * 256])
            S[f"mg_{l}"] = scr(f"wmg_{l}", [4, 128, 2048])
            S[f"ca_{l}"] = scr(f"wca_{l}", [8, 128, 2048])
            S[f"cb_{l}"] = scr(f"wcb_{l}", [1, 128, 12 * 128])
            S[f"bt_{l}"] = scr(f"wbt_{l}", [len(self.mask_list), 8, 128, 512])
            S[f"rp_{l}"] = scr(f"rp_{l}", [8, 1472])
        S["xres"] = scr("xres", [D, self.NTOK], F32)
        S["u2s"] = scr("u2s", [D, self.NTOK])
        S["xas"] = scr("xas", [512, self.NTOK])
        S["chs"] = scr("chs", [512, self.NTOK])
        S["bgs"] = scr("bgs", [512, self.NTOK])
        S["qs"] = scr("qs", [512, self.NTOK])
        S["ks"] = scr("ks", [512, self.NTOK])
        S["vs"] = scr("vs", [self.NTOK, 520])
        self.S = S
        self.Dw = {k: Buf("D" + k) for k in S}
        self.Dt = {}
        for nm in ("xres", "u2s", "xas", "chs", "bgs", "qs", "ks", "vs"):
            self.Dt[nm] = [Buf(f"D{nm}{i}") for i in range(self.NT)]
        self.Dout = Buf("Dout")

    def alloc(self, stack):
        nc = self.nc
        A = nc.alloc_sbuf_tensor
        self.xt = [A(f"xt{i}", [128, KC, T], F32) for i in range(2)]
        self.X = [[Buf(f"X{i}_{c}") for c in range(KC)] for i in range(2)]
        self.ut = [A(f"ut{i}", [128, KC, T], BF16) for i in range(2)]
        self.U = [[Buf(f"U{i}_{c}") for c in range(KC)] for i in range(2)]
        self.harena = A("harena", [128, FC * T], BF16)
        self.H = [Buf(f"H{j}") for j in range(FC)]
        self.ring = A("ring", [128, NSLOT * SLOT], BF16)
        self.R = [Buf(f"R{i}") for i in range(NSLOT)]
        self.tm = [A(f"tm{i}", [128, D], F32) for i in range(2)]
        self.TM = [Buf(f"TM{i}") for i in range(2)]
        self.sqb = [A(f"sqb{i}", [128, T], BF16) for i in range(2)]
        self.SQ = [Buf(f"SQ{i}") for i in range(2)]
        self.tmpf = [A(f"tmpf{i}", [128, T], F32) for i in range(3)]
        self.TF = [Buf(f"TF{i}") for i in range(3)]
        self.stat = [A(f"stat{i}", [128, T], F32) for i in range(3)]
        self.STB = [Buf(f"ST{i}") for i in range(3)]
        self.xa = A("xa", [128, 4, 572], BF16)
        self.XA = [Buf(f"XA{c}") for c in range(4)]
        self.ch = A("ch", [128, 4, 516], BF16)
        self.CH = [Buf(f"CH{c}") for c in range(4)]
        self.bg = A("bg", [128, 4, T], BF16)
        self.BG = [Buf(f"BG{c}") for c in range(4)]
        self.qt = A("qt", [128, 4, T], BF16)
        self.Q = [Buf(f"Q{c}") for c in range(4)]
        self.kt = A("kt", [128, 4, 2 * T], BF16)
        self.K = [Buf(f"K{c}") for c in range(4)]
        self.kblk = [A(f"kblk{i}", [128, 4, 4 * 128], BF16) for i in range(2)]
        self.KB = [[Buf(f"KB{i}_{c}") for c in range(4)] for i in range(2)]
        self.vt = A("vt", [128, 8, 520], BF16)
        self.V = [Buf(f"V{i}") for i in range(8)]
        self.maskt = A("maskt", [128, 3, T], BF16)
        self.MK = [Buf(f"MK{i}") for i in range(3)]
        self.sA = A("sA", [128, 4, T], BF16)
        self.SA = [Buf(f"SA{c}") for c in range(4)]
        self.tB = A("tB", [128, 4, T], BF16)
        self.TB = [Buf(f"TB{c}") for c in range(4)]
        self.oT = A("oT", [128, 4, T], BF16)
        self.OT = [Buf(f"OT{c}") for c in range(4)]
        self.kc = A("kc", [128, 4, PAST], BF16)
        self.KCb = Buf("KC")
        self.vc = A("vc", [128, 4, 520], BF16)
        self.VCb = Buf("VC")
        self.identF = A("identF", [128, 128], F32)
        self.identB = A("identB", [128, 128], BF16)
        self.anti = A("anti", [128, 128], BF16)
        self.ones = A("ones", [128, 128], BF16)
        self.blk = A("blk", [128, 128], BF16)
        self.CONST = Buf("CONST")
        self.cols = A("cols", [128, self.NL * NCOL], F32)
        self.COLS = Buf("COLS")
        self.MODS = Buf("MODS")
        self.gkrow = A("gkrow", [128, self.NL, 512], F32)
        self.small = A("small", [128, 64], F32)
        self.SM = [Buf(f"SM{i}") for i in range(4)]
        self.scT = A("scT", [128, KC, 2], F32)
        self.SCT = Buf("SCT")
        self.ps = [nc.alloc_psum_tensor(f"ps{i}", [128, T], F32) for i in range(8)]
        self.PS = [Buf(f"PS{i}") for i in range(8)]
        ha = self.harena
        self.ha_f = [ha[:, (2 * c) * T:(2 * c + 2) * T].bitcast(F32) for c in range(4)]
        self.HA = [[self.H[2 * c], self.H[2 * c + 1]] for c in range(4)]
        self.otm = [ha[:, (8 + q) * T:(9 + q) * T] for q in range(4)]
        self.OTM = [self.H[8 + q] for q in range(4)]
        self.pl = [ha[:, (12 + i) * T:(13 + i) * T] for i in range(3)]
        self.PL = [self.H[12 + i] for i in range(3)]
        self.pc = [ha[:, (15 + i) * T:(16 + i) * T] for i in range(3)]
        self.PC = [self.H[15 + i] for i in range(3)]
        self.macc = [ha[:, (18 + 2 * i) * T:(20 + 2 * i) * T].bitcast(F32) for i in range(2)]
        self.MA = [[self.H[18 + 2 * i], self.H[19 + 2 * i]] for i in range(2)]
        for k in ("pe", "act", "dve", "pool"):
            self.eng[k].sem = self.new_sem(stack, "c_" + k)
        self.dsems = {"sp": [DSem(self.new_sem(stack, f"dsp{i}")) for i in range(20)],
                      "pool": [DSem(self.new_sem(stack, f"dpl{i}")) for i in range(8)],
                      "act": [DSem(self.new_sem(stack, f"dac{i}")) for i in range(8)]}
        self.KC4 = [Buf(f"KC4_{i}") for i in range(4)]
        self.csem = {}
        for k in self.S:
            if k[0] in "gudiomcbr" and "_" in k:
                self.csem[k] = DSem(self.new_sem(stack, "cv_" + k))

    def hv(self, j):
        return self.harena[:, j * T:(j + 1) * T]

    def col(self, l, idx, n=1):
        b = l * NCOL + idx
        return self.cols[:, b:b + n]

    def reset_state(self):
        self.st = {"bank_rr": 0, "held": set(), "req_i": 0, "pending": [], "loaded": 0, "released": set()}

    def setup(self):
        I, S = self.I, self.S
        NL = self.NL
        C = [self.CONST]
        self.dma("pool", self.identB[:], I["c_ident"], writes=C)
        self.dma("pool", self.anti[:], I["c_anti"], writes=C)
        self.dma("pool", self.ones[:], I["c_ones"], writes=C)
        self.dma("pool", self.blk[:], I["c_blk"], writes=C)
        self.dma("sp", self.identF[:], I["c_ident"], writes=C)
        self.memset("pool", self.kt[:], 0.0, self.K)
        self.memset("pool", self.vt[:], 0.0, self.V)
        self.memset("pool", self.vc[:], 0.0, [self.VCb])
        self.memset("pool", self.xa[:], 0.0, self.XA)
        self.memset("pool", self.ch[:], 0.0, self.CH)
        self.memset("dve", self.kblk[0][:], 0.0, self.KB[0])
        self.memset("dve", self.kblk[1][:], 0.0, self.KB[1])
        self.memset("dve", self.vt[:].rearrange("p b (h e) -> p b h e", e=65)[:, :, :, 64:65], 1.0, self.V)
        self.memset("dve", self.vc[:].rearrange("p b (h e) -> p b h e", e=65)[:, :, :, 64:65], 1.0, [self.VCb])
        self.memset("dve", self.small[:], 0.0, self.SM)
        CL = [self.COLS]
        for l in range(NL):
            for i, nm in enumerate(("g_ff1", "g_mix", "g_ff2")):
                self.dma("sp", self.col(l, C_G + 8 * i, 8), I[nm][l].rearrange("(c p) -> p c", p=128), writes=CL, slow=True)
            for cc in range(4):
                self.dma("sp", self.col(l, C_CAW, 124).rearrange("p (k c) -> p k c", c=4)[:, :, cc],
                         I["conv_a_w"][l][:, cc * 128:(cc + 1) * 128].rearrange("k p -> p k"), writes=CL, slow=True)
                self.dma("sp", self.col(l, C_CBW, 12).rearrange("p (k c) -> p k c", c=4)[:, :, cc],
                         I["conv_b_w"][l][:, cc * 128:(cc + 1) * 128].rearrange("k p -> p k"), writes=CL, slow=True)
            for idx, nm in ((C_CAB, "conv_a_b"), (C_LNG, "ln_a_g"), (C_LNB, "ln_a_b")):
                self.dma("sp", self.col(l, idx, 4), I[nm][l].rearrange("(c p) -> p c", p=128), writes=CL, slow=True)
            for idx, nm in ((C_GQ, "q_norm_g"), (C_GK, "k_norm_g")):
                for hh in range(2):
                    self.dma("sp", self.cols[hh * 64:(hh + 1) * 64, l * NCOL + idx:l * NCOL + idx + 1],
                             I[nm][l].rearrange("(p o) -> p o", o=1), writes=CL, slow=True)
            self.dma("sp", self.gkrow[:, l, :].rearrange("p (h d) -> p h d", h=8),
                     bass.AP(I["k_norm_g"].tensor, l * 64, [[0, 128], [0, 8], [1, 64]]), writes=CL)
            self.ts("dve", self.col(l, C_GQ), self.col(l, C_GQ), 0.125, None, ALU.mult, None, reads=CL, writes=CL)
        for t_ in range(2):
            self.dma("sp", self.scT[:, :, t_], I["cvec"][t_].rearrange("(c p) -> p c", p=128), writes=[self.SCT], slow=True)
        self.act(self.scT[:], self.scT[:], AF.Silu, reads=[self.SCT], writes=[self.SCT])

    def ada(self, l):
        I = self.I
        CL = [self.COLS]
        ML = [self.MODS]
        wv = I["w_ada"][l].rearrange("(kc p) n -> p kc n", p=128)
        for ct in range(36):
            hb = ct % 2
            stg = self.harena[:, hb * 4096:(hb + 1) * 4096].bitcast(F32).rearrange("p (k n) -> p k n", k=KC)
            HB = self.H[hb * 8:hb * 8 + 8]
            self.dma("sp", stg, wv[:, :, ct * 256:(ct + 1) * 256], writes=HB)
            ti = self.rot("tf", 3)
            self.dma("sp", self.tmpf[ti][0:2, 0:256], bass.AP(I["b_ada"].tensor, l * 9 * D + ct * 256, [[0, 2], [1, 256]]),
                     writes=[self.TF[ti]])
            b = self.bank()
            for kc in range(KC):
                self.mm(self.ps[b][0:2, 0:256], self.scT[:, kc, :], stg[:, kc, :], kc == 0, kc == KC - 1,
                        reads=[self.SCT] + HB, bankbuf=self.PS[b])
            self.tt("dve", self.tmpf[ti][0:2, 0:256], self.ps[b][0:2, 0:256], self.tmpf[ti][0:2, 0:256], ALU.add,
                    reads=[self.PS[b], self.TF[ti]], writes=[self.TF[ti]])
            b2 = self.bank()
            for i in range(2):
                self.tr(self.ps[b2][:, 2 * i:2 * i + 2], self.tmpf[ti][0:2, i * 128:(i + 1) * 128], self.identF[0:2, 0:2],
                        reads=[self.TF[ti], self.CONST], bankbuf=self.PS[b2])
            self.cp("dve", self.col(l, C_MOD + ct * 4, 4), self.ps[b2][:, 0:4], reads=[self.PS[b2]], writes=ML)
        for i in range(3):
            sc = self.col(l, C_MOD + (3 * i + 1) * 16, 16)
            a = self.col(l, C_A + i * 16, 16)
            self.ts("dve", a, sc, 1.0, None, ALU.add, None, reads=ML, writes=ML)
            g = self.col(l, C_G + 8 * i, 8)
            self.tt("dve", a.rearrange("p (c t) -> p c t", t=2), a.rearrange("p (c t) -> p c t", t=2),
                    g.rearrange("p (c o) -> p c o", o=1).broadcast_to([128, 8, 2]), ALU.mult, reads=CL + ML, writes=ML)
            gt = self.col(l, C_GT + i * 16, 16)
            self.ts("dve", gt, self.col(l, C_MOD + (3 * i + 2) * 16, 16), 1.0 if i == 1 else 0.5, None, ALU.mult, None,
                    reads=ML, writes=ML)

    def convert(self, l, which):
        I, S = self.I, self.S

        def kslabs(key, w, ncol, kc, nslab):
            wv = w.rearrange("(kc p) n -> p kc n", p=128)
            for s in range(nslab):
                self.dma("pool", S[key][s].rearrange("p (a b) -> p a b", a=kc), wv[:, :, s * ncol:(s + 1) * ncol],
                         writes=[self.Dw[key]], sem=self.csem[key], throttle=False)

        def dslabs(key, w):
            wv = w.rearrange("(j p) n -> p j n", p=128)
            for mc in range(8):
                for hf in range(2):
                    self.dma("pool", S[key][mc * 2 + hf].rearrange("p (a b) -> p a b", a=11),
                             wv[:, hf * 11:(hf + 1) * 11, mc * 128:(mc + 1) * 128],
                             writes=[self.Dw[key]], sem=self.csem[key], throttle=False)

        if which == "p1":
            kslabs(f"g1_{l}", I["w_ff1_gate"][l], 256, 8, 11)
            kslabs(f"u1_{l}", I["w_ff1_up"][l], 256, 8, 11)
            dslabs(f"d1_{l}", I["w_ff1_down"][l])
            kslabs(f"in_{l}", I["w_in"][l], 256, 8, 28)
        else:
            kslabs(f"oa_{l}", I["w_a_out"][l], 256, 4, 4)
            kslabs(f"ob_{l}", I["w_b_out"][l], 256, 4, 4)
            kslabs(f"oc_{l}", I["w_c_out"][l], 256, 4, 4)
            kslabs(f"mg_{l}", I["w_merge"][l], 256, 8, 4)
            kslabs(f"g2_{l}", I["w_ff2_gate"][l], 256, 8, 11)
            kslabs(f"u2_{l}", I["w_ff2_up"][l], 256, 8, 11)
            dslabs(f"d2_{l}", I["w_ff2_down"][l])

    def convert_p1(self, l):
        self.convert(l, "p1")

    def convert_p2(self, l):
        self.convert(l, "p2")

    def table_tasks(self, l):
        I, S = self.I, self.S
        CL = [self.COLS]
        NP_ = 1472
        kcf = self.kc[:].rearrange("p c t -> p (c t)")
        KCB = [self.KCb] + self.KC4
        rpt = S[f"rp_{l}"].tensor

        def neg_view(i_):
            if i_ < 4:
                return self.sA[:, i_, :], self.SA[i_]
            if i_ < 8:
                return self.tB[:, i_ - 4, :], self.TB[i_ - 4]
            return self.oT[:, i_ - 8, :], self.OT[i_ - 8]

        def loads(h):
            pb = self.kblk[h % 2][:].rearrange("p c t -> p (c t)")
            for krl in range(4):
                src = bass.AP(rpt, h * NP_ + krl * RP_C, [[1, 32], [1, 1232]])
                self.dma("pool", pb[krl * 32:(krl + 1) * 32, 0:1232], src, reads=[self.Dw[f"rp_{l}"]], writes=self.KB[h % 2])

        def sub0():
            for cc in range(4):
                for hf in range(2):
                    si = cc * 2 + hf
                    for i in range(16):
                        k = hf * 16 + i
                        if k < 31:
                            self.ts("pool", kcf[:, i * 128:(i + 1) * 128], self.identF[:], self.col(l, C_CAW + k * 4 + cc), None,
                                    ALU.mult, None, reads=CL + [self.CONST], writes=KCB)
                        else:
                            self.memset("pool", kcf[:, i * 128:(i + 1) * 128], 0.0, KCB)
                    self.dma("pool", S[f"ca_{l}"][si], kcf, reads=KCB, writes=[self.Dw[f"ca_{l}"]], sem=self.csem[f"ca_{l}"], throttle=False)
            for k in range(3):
                for cc in range(4):
                    i = k * 4 + cc
                    self.ts("pool", kcf[:, i * 128:(i + 1) * 128], self.identF[:], self.col(l, C_CBW + k * 4 + cc), None,
                            ALU.mult, None, reads=CL + [self.CONST], writes=KCB)
            self.dma("pool", S[f"cb_{l}"][0], kcf[:, 0:1536], reads=KCB, writes=[self.Dw[f"cb_{l}"]], sem=self.csem[f"cb_{l}"], throttle=False)
            VB = [self.VCb]
            rpb16 = self.vc[:].rearrange("p b f -> p (b f)")[0:8, 0:NP_]
            self.memset("pool", rpb16, 0.0, VB)
            self.dma("pool", rpb16[:, 0:RP_R * RP_C].rearrange("p (r c) -> p r c", c=RP_C)[:, 4:19, 16:47], I["rpb"][l], reads=(), writes=VB, slow=True)
            self.dma("pool", S[f"rp_{l}"][:, 0:NP_], rpb16, reads=VB, writes=[self.Dw[f"rp_{l}"]], sem=self.csem[f"rp_{l}"], throttle=False)
            self.memset("pool", self.vc[:].rearrange("p b (h e) -> p b h e", e=65)[:, :, :, 64:65], 1.0, VB)
            for i_ in range(len(self.mask_list)):
                v_, b_ = neg_view(i_)
                self.dma("pool", v_, I["c_mask"][i_], writes=[b_])
                self.ts("pool", v_, v_, 30000.0, -30000.0, ALU.mult, ALU.add, reads=[b_], writes=[b_])
            loads(0)

        def sub_h(h):
            def f():
                if h + 1 < 8:
                    loads(h + 1)
                pb = self.kblk[h % 2][:].rearrange("p c t -> p (c t)")
                for cp in range(3):
                    off = [0, -8, -16][cp]
                    base = pb[:, 472 + off:473 + off]
                    srcv = bass.AP(base.tensor, base.offset, [list(base.ap[0]), [252, 4], [-RP_C, 8], [-1, 16]])
                    mi = self.rot("btc", 3)
                    self.cp("pool", self.maskt[:, mi, :].rearrange("p (g a q) -> p g a q", g=4, a=8), srcv, reads=self.KB[h % 2], writes=[self.MK[mi]])
                    for id_ in sorted(set(self.mask_idx[(t_, cp)] for t_ in range(self.nlt))):
                        v_, b_ = neg_view(id_)
                        ki = self.rot("bts", 4)
                        self.tt("pool", self.kc[:, ki, :], self.maskt[:, mi, :], v_, ALU.add, reads=[self.MK[mi], b_, self.KCb], writes=[self.KC4[ki]])
                        self.dma("pool", S[f"bt_{l}"][id_, h], self.kc[:, ki, :], reads=[self.KC4[ki]],
                                 writes=[self.Dw[f"bt_{l}"]], sem=self.csem[f"bt_{l}"], throttle=False)
            return f

        return [sub0] + [sub_h(h) for h in range(8)]

    def norm(self, l, i, xb, ub, tokm):
        b = self.bank()
        for c in range(KC):
            si = self.rot("sq", 2)
            self.act(self.sqb[si][:], self.xt[xb][:, c, :], AF.Square, reads=[self.X[xb][c]], writes=[self.SQ[si]])
            self.mm(self.ps[b][:], self.ones[:], self.sqb[si][:], c == 0, c == KC - 1,
                    reads=[self.CONST, self.SQ[si]], bankbuf=self.PS[b])
        s1 = self.rot("st", 3)
        self.act(self.stat[s1][:], self.ps[b][:], AF.Sqrt, reads=[self.PS[b]], writes=[self.STB[s1]],
                 bias=self.col(0, C_EPS), scale=1.0 / D)
        self.recip(self.stat[s1][:], self.stat[s1][:], reads=[self.STB[s1]], writes=[self.STB[s1]])
        for c in range(KC):
            ti = self.rot("tf", 3)
            self.stt("dve", self.tmpf[ti][:], self.xt[xb][:, c, :], self.col(l, C_A + (i * 8 + c) * 2 + tokm),
                     self.stat[s1][:], ALU.mult, ALU.mult,
                     reads=[self.X[xb][c], self.COLS, self.MODS, self.STB[s1]], writes=[self.TF[ti]])
            self.act(self.ut[ub][:, c, :], self.tmpf[ti][:], AF.Identity, reads=[self.TF[ti], self.COLS, self.MODS],
                     writes=[self.U[ub][c]], bias=self.col(l, C_MOD + (3 * i * 8 + c) * 2 + tokm))

    def ffn(self, l, f, xb, ub, tokm, mid=None):
        gi = 0 if f == "1" else 2
        S = self.S
        kg, ku, kd = f"g{f}_{l}", f"u{f}_{l}", f"d{f}_{l}"
        for s in range(11):
            sg = self.slab(S[kg][s], 2048, self.Dw[kg], (8, 256))
            su = self.slab(S[ku][s], 2048, self.Dw[ku], (8, 256))
            for jj in range(2):
                j = 2 * s + jj
                bg = self.bank()
                for kc in range(KC):
                    self.mm(self.ps[bg][:], sg.ap[:, kc, jj * 128:(jj + 1) * 128], self.ut[ub][:, kc, :], kc == 0, kc == KC - 1,
                            reads=[sg.buf, self.U[ub][kc]], bankbuf=self.PS[bg])
                bu = self.bank()
                for kc in range(KC):
                    self.mm(self.ps[bu][:], su.ap[:, kc, jj * 128:(jj + 1) * 128], self.ut[ub][:, kc, :], kc == 0, kc == KC - 1,
                            reads=[su.buf, self.U[ub][kc]], bankbuf=self.PS[bu])
                ti = self.rot("tf", 3)
                self.act(self.tmpf[ti][:], self.ps[bg][:], AF.Silu, reads=[self.PS[bg]], writes=[self.TF[ti]])
                self.tt("dve", self.hv(j), self.tmpf[ti][:], self.ps[bu][:], ALU.mult,
                        reads=[self.TF[ti], self.PS[bu]], writes=[self.H[j]])
            self.done(sg)
            self.done(su)
        if mid is not None:
            mid()
        for mc in range(8):
            b = self.bank()
            for hf in range(2):
                sd = self.slab(S[kd][mc * 2 + hf], 11 * 128, self.Dw[kd], (11, 128))
                for jj in range(11):
                    j = hf * 11 + jj
                    self.mm(self.ps[b][:], sd.ap[:, jj, :], self.hv(j), j == 0, j == FC - 1,
                            reads=[sd.buf, self.H[j]], bankbuf=self.PS[b])
                self.done(sd)
            self.stt("dve", self.xt[xb][:, mc, :], self.ps[b][:], self.col(l, C_GT + (gi * 8 + mc) * 2 + tokm),
                     self.xt[xb][:, mc, :], ALU.mult, ALU.add,
                     reads=[self.PS[b], self.X[xb][mc], self.COLS, self.MODS], writes=[self.X[xb][mc]])

    def load_x_tokmajor(self, ti, xb):
        I = self.I
        src = I["xp"] if ti < self.nct else I["xs"]
        r0 = (ti if ti < self.nct else ti - self.nct) * T
        for blk in range(4):
            tb = self.rot("tm", 2)
            self.dma("sp", self.tm[tb][:], src[r0 + blk * 128:r0 + (blk + 1) * 128, :], writes=[self.TM[tb]])
            for c0 in (0, 4):
                b = self.bank()
                for c in range(c0, c0 + 4):
                    self.tr(self.ps[b][:, (c - c0) * 128:(c - c0 + 1) * 128], self.tm[tb][:, c * 128:(c + 1) * 128], self.identF[:],
                            reads=[self.TM[tb], self.CONST], bankbuf=self.PS[b])
                self.cp("act" if c0 == 0 else "dve", self.xt[xb][:, c0:c0 + 4, blk * 128:(blk + 1) * 128],
                        self.ps[b][:].rearrange("p (c t) -> p c t", c=4), reads=[self.PS[b]], writes=self.X[xb][c0:c0 + 4])

    def store_y_tokmajor(self, ti, xb):
        O = self.O
        dst = O["yp"] if ti < self.nct else O["ys"]
        r0 = (ti if ti < self.nct else ti - self.nct) * T
        for blk in range(4):
            tb = self.rot("tm", 2)
            for c0 in (0, 4):
                b = self.bank()
                for c in range(c0, c0 + 4):
                    self.tr(self.ps[b][:, (c - c0) * 128:(c - c0 + 1) * 128], self.xt[xb][:, c, blk * 128:(blk + 1) * 128], self.identF[:],
                            reads=[self.X[xb][c], self.CONST], bankbuf=self.PS[b])
                self.cp("act" if c0 == 0 else "dve", self.tm[tb][:, c0 * 128:(c0 + 4) * 128], self.ps[b][:],
                        reads=[self.PS[b]], writes=[self.TM[tb]])
            ev = self.dma("sp", dst[r0 + blk * 128:r0 + (blk + 1) * 128, :], self.tm[tb][:], reads=[self.TM[tb]], writes=[self.Dout])
            self.final_events.append(ev)

    def p1_load(self, l, ti, a):
        S = self.S
        tok0 = ti * T
        if l == 0:
            self.load_x_tokmajor(ti, a)
        else:
            self.dma("sp", self.xt[a][:], S["xres"][:, tok0:tok0 + T].rearrange("(c p) t -> p c t", p=128),
                     reads=[self.Dt["xres"][ti]], writes=self.X[a])

    def p1_compute(self, l, ti, a, prefetch):
        S = self.S
        ctx = ti < self.nct
        tokm = 1 if ctx else 0
        tok0 = ti * T
        xb, ub, ub2 = a, a, 1 - a
        self.norm(l, 0, xb, ub, tokm)
        self.ffn(l, "1", xb, ub, tokm, mid=prefetch)
        self.norm(l, 1, xb, ub2, tokm)
        self.dma("sp", S["u2s"][:, tok0:tok0 + T].rearrange("(c p) t -> p c t", p=128), self.ut[ub2][:],
                 reads=self.U[ub2], writes=[self.Dt["u2s"][ti]])
        self.dma("sp", S["xres"][:, tok0:tok0 + T].rearrange("(c p) t -> p c t", p=128), self.xt[xb][:],
                 reads=self.X[xb], writes=[self.Dt["xres"][ti]])
        self.proj_early(l, ti, ub2)

    def in_slab(self, l, s):
        return self.slab(self.S[f"in_{l}"][s], 2048, self.Dw[f"in_{l}"], (8, 256))

    def proj_chunk(self, sl, jj, ub):
        b = self.bank()
        for kc in range(KC):
            self.mm(self.ps[b][:], sl.ap[:, kc, jj * 128:(jj + 1) * 128], self.ut[ub][:, kc, :], kc == 0, kc == KC - 1,
                    reads=[sl.buf, self.U[ub][kc]], bankbuf=self.PS[b])
        return b

    def xa_view(self, cc, ctx):
        if ctx:
            return self.xa[:, cc, :].rearrange("p (s w) -> p s w", s=2)[:, :, 15:271]
        return self.xa[:, cc, 15:15 + T]

    def ch_view(self, cc, ctx):
        if ctx:
            return self.ch[:, cc, :].rearrange("p (s w) -> p s w", s=2)[:, :, 1:257]
        return self.ch[:, cc, 1:1 + T]

    def v3(self, ap, ctx):
        return ap.rearrange("p (s w) -> p s w", s=2) if ctx else ap

    def proj_early(self, l, ti, ub):
        S, I, O = self.S, self.I, self.O
        ctx = ti < self.nct
        tok0 = ti * T
        for p in range(2):
            sv = self.in_slab(l, p)
            sg = self.in_slab(l, 2 + p)
            for jj in range(2):
                cc = 2 * p + jj
                bv = self.proj_chunk(sv, jj, ub)
                bg = self.proj_chunk(sg, jj, ub)
                ti_ = self.rot("tf", 3)
                self.act(self.tmpf[ti_][:], self.ps[bg][:], AF.Sigmoid, reads=[self.PS[bg]], writes=[self.TF[ti_]])
                self.tt("dve", self.xa_view(cc, ctx), self.v3(self.tmpf[ti_][:], ctx), self.v3(self.ps[bv][:], ctx), ALU.mult,
                        reads=[self.TF[ti_], self.PS[bv]], writes=[self.XA[cc]])
            self.done(sv)
            self.done(sg)
        for cc in range(4):
            self.dma("sp", S["xas"][cc * 128:(cc + 1) * 128, tok0:tok0 + T] if not ctx else
                     S["xas"][cc * 128:(cc + 1) * 128, tok0:tok0 + T].rearrange("p (s w) -> p s w", s=2),
                     self.xa_view(cc, ctx), reads=[self.XA[cc]], writes=[self.Dt["xas"][ti]])
        for p in range(2):
            sb = self.in_slab(l, 4 + p)
            for jj in range(2):
                cc = 2 * p + jj
                bb = self.proj_chunk(sb, jj, ub)
                self.cp("act", self.bg[:, cc, :], self.ps[bb][:], reads=[self.PS[bb]], writes=[self.BG[cc]])
            self.done(sb)
        self.dma("sp", S["bgs"][:, tok0:tok0 + T].rearrange("(c p) t -> p c t", p=128), self.bg[:],
                 reads=self.BG, writes=[self.Dt["bgs"][ti]])
        for p in range(2):
            sc = self.in_slab(l, 6 + p)
            sh = self.in_slab(l, 8 + p)
            for jj in range(2):
                cc = 2 * p + jj
                bc = self.proj_chunk(sc, jj, ub)
                bh = self.proj_chunk(sh, jj, ub)
                ti_ = self.rot("tf", 3)
                self.cp("act", self.tmpf[ti_][:], self.ps[bc][:], reads=[self.PS[bc]], writes=[self.TF[ti_]])
                self.tt("dve", self.ch_view(cc, ctx), self.v3(self.tmpf[ti_][:], ctx), self.v3(self.ps[bh][:], ctx), ALU.mult,
                        reads=[self.TF[ti_], self.PS[bh]], writes=[self.CH[cc]])
            self.done(sc)
            self.done(sh)
        for cc in range(4):
            self.dma("sp", S["chs"][cc * 128:(cc + 1) * 128, tok0:tok0 + T] if not ctx else
                     S["chs"][cc * 128:(cc + 1) * 128, tok0:tok0 + T].rearrange("p (s w) -> p s w", s=2),
                     self.ch_view(cc, ctx), reads=[self.CH[cc]], writes=[self.Dt["chs"][ti]])
        for which, s0, dst, DB, gcol in (("q", 10, self.qt, self.Q, C_GQ), ("k", 12, self.kt, self.K, C_GK)):
            for p in range(2):
                sl = self.in_slab(l, s0 + p)
                for jj in range(2):
                    cc = 2 * p + jj
                    bq = self.proj_chunk(sl, jj, ub)
                    si = self.rot("sq", 2)
                    self.act(self.sqb[si][:], self.ps[bq][:], AF.Square, reads=[self.PS[bq]], writes=[self.SQ[si]])
                    bs = self.bank()
                    self.mm(self.ps[bs][:], self.blk[:], self.sqb[si][:], True, True, reads=[self.CONST, self.SQ[si]], bankbuf=self.PS[bs])
                    s1 = self.rot("st", 3)
                    self.act(self.stat[s1][:], self.ps[bs][:], AF.Sqrt, reads=[self.PS[bs]], writes=[self.STB[s1]],
                             bias=self.col(0, C_EPS), scale=1.0 / HD)
                    self.recip(self.stat[s1][:], self.stat[s1][:], reads=[self.STB[s1]], writes=[self.STB[s1]])
                    self.stt("dve", dst[:, cc, 0:T], self.ps[bq][:], self.col(l, gcol), self.stat[s1][:], ALU.mult, ALU.mult,
                             reads=[self.PS[bq], self.STB[s1], self.COLS], writes=[DB[cc]])
                self.done(sl)
            nm = "qs" if which == "q" else "ks"
            self.dma("sp", S[nm][:, tok0:tok0 + T].rearrange("(c p) t -> p c t", p=128), dst[:, :, 0:T],
                     reads=DB, writes=[self.Dt[nm][ti]])
        s14 = self.in_slab(l, 14)
        s15 = self.in_slab(l, 15)
        for blk in range(4):
            b = self.bank()
            for hf, sl in ((0, s14), (1, s15)):
                for kc in range(KC):
                    self.mm(self.ps[b][:, hf * 256:(hf + 1) * 256], self.ut[ub][:, kc, blk * 128:(blk + 1) * 128], sl.ap[:, kc, :],
                            kc == 0, kc == KC - 1, reads=[sl.buf, self.U[ub][kc]], bankbuf=self.PS[b])
            self.cp("act", self.vt[:, blk, :].rearrange("p (h e) -> p h e", e=65)[:, :, 0:64],
                    self.ps[b][:].rearrange("p (h d) -> p h d", d=64), reads=[self.PS[b]], writes=[self.V[blk]])
            if ctx:
                tb = self.rot("tm", 2)
                self.cp("dve", self.tm[tb][:, 0:512], self.ps[b][:], reads=[self.PS[b]], writes=[self.TM[tb]])
                seq = ti * 2 + blk // 2
                ev = self.dma("sp", O["nv"][seq, l, (blk % 2) * 128:(blk % 2 + 1) * 128, :], self.tm[tb][:, 0:512],
                              reads=[self.TM[tb]], writes=[self.Dout])
                self.final_events.append(ev)
        self.done(s14)
        self.done(s15)
        self.dma("sp", S["vs"][tok0:tok0 + T, :].rearrange("(b p) f -> p b f", p=128), self.vt[:, 0:4, :],
                 reads=self.V[0:4], writes=[self.Dt["vs"][ti]])
        if ctx:
            s12 = self.in_slab(l, 12)
            s13 = self.in_slab(l, 13)
            for blk in range(4):
                b = self.bank()
                for hf, sl in ((0, s12), (1, s13)):
                    for kc in range(KC):
                        self.mm(self.ps[b][:, hf * 256:(hf + 1) * 256], self.ut[ub][:, kc, blk * 128:(blk + 1) * 128], sl.ap[:, kc, :],
                                kc == 0, kc == KC - 1, reads=[sl.buf, self.U[ub][kc]], bankbuf=self.PS[b])
                ti_ = self.rot("tf", 3)
                self.act(self.tmpf[ti_][:], self.ps[b][:], AF.Square, reads=[self.PS[b]], writes=[self.TF[ti_]])
                sm = self.rot("sm", 4)
                ss = self.small[:, sm * 16:sm * 16 + 8]
                self.op("dve", lambda e, o=ss, i=self.tmpf[ti_][:].rearrange("p (h d) -> p h d", d=64):
                        e.tensor_reduce(out=o, in_=i, axis=AX.X, op=ALU.add), reads=[self.TF[ti_]], writes=[self.SM[sm]])
                self.act(ss, ss, AF.Sqrt, reads=[self.SM[sm]], writes=[self.SM[sm]], bias=self.col(0, C_EPS), scale=1.0 / HD)
                self.recip(ss, ss, reads=[self.SM[sm]], writes=[self.SM[sm]])
                self.tt("dve", self.tmpf[ti_][:].rearrange("p (h d) -> p h d", d=64), self.ps[b][:].rearrange("p (h d) -> p h d", d=64),
                        ss.rearrange("p (h o) -> p h o", o=1).broadcast_to([128, 8, 64]), ALU.mult,
                        reads=[self.PS[b], self.SM[sm]], writes=[self.TF[ti_]])
                tb = self.rot("tm", 2)
                self.tt("dve", self.tm[tb][:, 0:512], self.tmpf[ti_][:], self.gkrow[:, l, :], ALU.mult,
                        reads=[self.TF[ti_], self.COLS], writes=[self.TM[tb]])
                seq = ti * 2 + blk // 2
                ev = self.dma("sp", O["nk"][seq, l, (blk % 2) * 128:(blk % 2 + 1) * 128, :], self.tm[tb][:, 0:512],
                              reads=[self.TM[tb]], writes=[self.Dout])
                self.final_events.append(ev)
            self.done(s12)
            self.done(s13)
    def p2_load(self, l, ti, a):
        S, I = self.S, self.I
        ctx = ti < self.nct
        tokm = 1 if ctx else 0
        tok0 = ti * T
        lt = ti - self.nct
        xb, ub = a, a
        fm = lambda nm: S[nm][:, tok0:tok0 + T].rearrange("(c p) t -> p c t", p=128)
        self.dma("sp", self.xt[xb][:], fm("xres"), reads=[self.Dt["xres"][ti]], writes=self.X[xb])
        self.dma("sp", self.ut[ub][:], fm("u2s"), reads=[self.Dt["u2s"][ti]], writes=self.U[ub])
        if ctx:
            for cc in range(4):
                self.dma("sp", self.xa_view(cc, True), S["xas"][cc * 128:(cc + 1) * 128, tok0:tok0 + T].rearrange("p (s w) -> p s w", s=2),
                         reads=[self.Dt["xas"][ti]], writes=[self.XA[cc]])
                self.dma("sp", self.ch_view(cc, True), S["chs"][cc * 128:(cc + 1) * 128, tok0:tok0 + T].rearrange("p (s w) -> p s w", s=2),
                         reads=[self.Dt["chs"][ti]], writes=[self.CH[cc]])
                xv = self.xa[:, cc, :].rearrange("p (s w) -> p s w", s=2)
                self.memset("pool", xv[:, :, 0:15], 0.0, [self.XA[cc]])
                self.memset("pool", xv[:, :, 271:286], 0.0, [self.XA[cc]])
                cv = self.ch[:, cc, :].rearrange("p (s w) -> p s w", s=2)
                self.memset("pool", cv[:, :, 0:1], 0.0, [self.CH[cc]])
                self.memset("pool", cv[:, :, 257:258], 0.0, [self.CH[cc]])
        else:
            first, last = lt == 0, lt == self.nlt - 1
            for nm, buf, BB, pad in (("xas", self.xa, self.XA, 15), ("chs", self.ch, self.CH, 1)):
                lo = tok0 - (0 if first else pad)
                hi = tok0 + T + (0 if last else pad)
                o0 = pad if first else 0
                rd = [self.Dt[nm][ti]] + ([] if first else [self.Dt[nm][ti - 1]]) + ([] if last else [self.Dt[nm][ti + 1]])
                self.dma("sp", buf[:, :, o0:o0 + hi - lo], S[nm][:, lo:hi].rearrange("(c p) t -> p c t", p=128), reads=rd, writes=BB)
                if first:
                    self.memset("pool", buf[:, :, 0:pad], 0.0, BB)
                if last:
                    self.memset("pool", buf[:, :, pad + T:pad + T + pad], 0.0, BB)
        self.dma("sp", self.bg[:], fm("bgs"), reads=[self.Dt["bgs"][ti]], writes=self.BG)
        self.dma("sp", self.qt[:], fm("qs"), reads=[self.Dt["qs"][ti]], writes=self.Q)
        if ctx:
            self.dma("sp", self.kt[:, :, 0:T], fm("ks"), reads=[self.Dt["ks"][ti]], writes=self.K)
            self.dma("sp", self.vt[:, 0:4, :], S["vs"][tok0:tok0 + T, :].rearrange("(b p) f -> p b f", p=128),
                     reads=[self.Dt["vs"][ti]], writes=self.V[0:4])
        else:
            rows = 8 * self.nlt
            w0 = 8 * lt - 4
            r_lo, r_hi = max(w0, 0), min(w0 + 16, rows)
            base = self.nct * T
            rd = [self.Dt["ks"][t2] for t2 in range(max(ti - 1, self.nct), min(ti + 2, self.NT))]
            self.dma("sp", self.kt[:, :, (r_lo - w0) * 64:(r_hi - w0) * 64],
                     S["ks"][:, base + r_lo * 64:base + r_hi * 64].rearrange("(c p) t -> p c t", p=128), reads=rd, writes=self.K)
    def p2_compute(self, l, ti, a, prefetch):
        S = self.S
        ctx = ti < self.nct
        tokm = 1 if ctx else 0
        tok0 = ti * T
        xb, ub = a, a
        if not ctx:
            self.attn_prep(ti, 0)
        self.branch_ab(l, ctx)
        if ctx:
            self.attn_ctx(l)
        else:
            self.attn_lat(l, ti)
        if prefetch is not None:
            prefetch()
        self.gates_merge(l, xb, ub, tokm)
        self.norm(l, 2, xb, ub, tokm)
        self.ffn(l, "2", xb, ub, tokm)
        if l == self.NL - 1:
            self.store_y_tokmajor(ti, xb)
        else:
            self.dma("sp", S["xres"][:, tok0:tok0 + T].rearrange("(c p) t -> p c t", p=128), self.xt[xb][:],
                     reads=self.X[xb], writes=[self.Dt["xres"][ti]])

    def conv(self, l, key, buf, BB, segw, ctx, evac):
        S = self.S
        segs = [(s_ * segw, s_ * 256, 256) for s_ in range(2)] if ctx else [(0, 0, T)]
        deferred = None
        if key == "ca":
            for cc in range(4):
                sls = [self.slab(S[f"ca_{l}"][cc * 2 + hf], 2048, self.Dw[f"ca_{l}"], (16, 128)) for hf in range(2)]
                b = self.bank()
                for (oi, oo, n) in segs:
                    for k in range(31):
                        sl = sls[k // 16]
                        self.mm(self.ps[b][:, oo:oo + n], sl.ap[:, k % 16, :], buf[:, cc, oi + k:oi + k + n], k == 0, k == 30,
                                reads=[sl.buf, BB[cc]], bankbuf=self.PS[b])
                for sl in sls:
                    self.done(sl)
                if deferred is not None:
                    deferred()
                deferred = evac(cc, b)
        else:
            sl = self.slab(S[f"cb_{l}"][0], 12 * 128, self.Dw[f"cb_{l}"], (12, 128))
            for cc in range(4):
                b = self.bank()
                for (oi, oo, n) in segs:
                    for k in range(3):
                        self.mm(self.ps[b][:, oo:oo + n], sl.ap[:, k * 4 + cc, :], buf[:, cc, oi + k:oi + k + n], k == 0, k == 2,
                                reads=[sl.buf, BB[cc]], bankbuf=self.PS[b])
                if deferred is not None:
                    deferred()
                deferred = evac(cc, b)
            self.done(sl)
        return deferred

    def branch_ab(self, l, ctx):
        bs1 = self.bank()
        self.hold(bs1)
        bs2 = self.bank()
        self.hold(bs2)

        def evac_a(cc, b):
            self.act(self.ha_f[cc], self.ps[b][:], AF.Identity, reads=[self.PS[b], self.COLS], writes=self.HA[cc],
                     bias=self.col(l, C_CAB + cc))
            si = self.rot("sq", 2)
            self.act(self.sqb[si][:], self.ps[b][:], AF.Square, reads=[self.PS[b], self.COLS], writes=[self.SQ[si]],
                     bias=self.col(l, C_CAB + cc))
            si2 = self.rot("sq", 2)
            self.act(self.sqb[si2][:], self.ps[b][:], AF.Identity, reads=[self.PS[b], self.COLS], writes=[self.SQ[si2]],
                     bias=self.col(l, C_CAB + cc))

            def stats():
                self.mm(self.ps[bs2][:], self.ones[:], self.sqb[si][:], cc == 0, cc == 3, reads=[self.CONST, self.SQ[si]], bankbuf=self.PS[bs2])
                self.mm(self.ps[bs1][:], self.ones[:], self.sqb[si2][:], cc == 0, cc == 3, reads=[self.CONST, self.SQ[si2]], bankbuf=self.PS[bs1])
            return stats

        last_a = self.conv(l, "ca", self.xa, self.XA, 286, ctx, evac_a)

        def evac_b(cc, b):
            self.tt("dve", self.tB[:, cc, :], self.ps[b][:], self.bg[:, cc, :], ALU.mult,
                    reads=[self.PS[b], self.BG[cc]], writes=[self.TB[cc]])
            if cc == 0:
                return last_a
            return None
        self.conv(l, "cb", self.ch, self.CH, 258, ctx, evac_b)
        m = self.rot("st", 3)
        self.op("act", lambda e, o=self.stat[m][:], i=self.ps[bs1][:]: e.mul(out=o, in_=i, mul=1.0 / 512), reads=[self.PS[bs1]], writes=[self.STB[m]])
        q = self.rot("st", 3)
        self.tt("dve", self.stat[q][:], self.stat[m][:], self.stat[m][:], ALU.mult, reads=[self.STB[m]], writes=[self.STB[q]])
        self.stt("dve", self.stat[q][:], self.ps[bs2][:], 1.0 / 512, self.stat[q][:], ALU.mult, ALU.subtract,
                 reads=[self.PS[bs2], self.STB[q]], writes=[self.STB[q]])
        self.act(self.stat[q][:], self.stat[q][:], AF.Sqrt, reads=[self.STB[q]], writes=[self.STB[q]], bias=self.col(0, C_EPS))
        self.recip(self.stat[q][:], self.stat[q][:], reads=[self.STB[q]], writes=[self.STB[q]])
        self.unhold(bs1)
        self.unhold(bs2)
        for cc in range(4):
            self.tt("dve", self.ha_f[cc], self.ha_f[cc], self.stat[m][:], ALU.subtract, reads=self.HA[cc] + [self.STB[m]], writes=self.HA[cc])
            self.tt("dve", self.ha_f[cc], self.ha_f[cc], self.stat[q][:], ALU.mult, reads=self.HA[cc] + [self.STB[q]], writes=self.HA[cc])
            self.act(self.sA[:, cc, :], self.ha_f[cc], AF.Silu, reads=self.HA[cc] + [self.COLS], writes=[self.SA[cc]],
                     bias=self.col(l, C_LNB + cc), scale=self.col(l, C_LNG + cc))

    def o_evac(self, bO, qb, half):
        sm = self.rot("sm", 4)
        rs = self.small[:, sm * 16:sm * 16 + 4]
        pv = self.ps[bO][:, 0:260].rearrange("p (h e) -> p h e", e=65)
        self.recip(rs.rearrange("p (h o) -> p h o", o=1), pv[:, :, 64:65], reads=[self.PS[bO]], writes=[self.SM[sm]])
        self.tt("dve", self.otm[qb][:, half * 256:(half + 1) * 256].rearrange("p (h d) -> p h d", d=64), pv[:, :, 0:64],
                rs.rearrange("p (h o) -> p h o", o=1).broadcast_to([128, 4, 64]), ALU.mult,
                reads=[self.PS[bO], self.SM[sm]], writes=[self.OTM[qb]])

    def o_transpose(self, qb, dst_view):
        b = self.bank()
        pb = self.ps[b][:].bitcast(BF16)
        for cc in range(4):
            self.tr(pb[:, cc * 128:(cc + 1) * 128], self.otm[qb][:, cc * 128:(cc + 1) * 128], self.identB[:],
                    reads=[self.OTM[qb], self.CONST], bankbuf=self.PS[b])
        self.cp("act", dst_view, pb[:, 0:512].rearrange("p (c q) -> p c q", c=4) if dst_view.ndim == 3 else
                pb[:, 0:512].rearrange("p (c a w) -> p c a w", c=4, a=8), reads=[self.PS[b]], writes=self.OT)

    def attn_ctx(self, l):
        for s in range(2):
            for half in range(2):
                bO = [self.bank(), self.bank()]
                for b_ in bO:
                    self.hold(b_)
                for hh in range(4):
                    h = half * 4 + hh
                    cc, p0 = h // 2, 64 * (h % 2)
                    bS = self.bank()
                    for kb in range(2):
                        self.mm(self.ps[bS][:, kb * 256:(kb + 1) * 256], self.kt[p0:p0 + 64, cc, s * 256 + kb * 128:s * 256 + (kb + 1) * 128],
                                self.qt[p0:p0 + 64, cc, s * 256:(s + 1) * 256], True, True,
                                reads=[self.K[cc], self.Q[cc]], bankbuf=self.PS[bS])
                    pi = self.rot("pc", 3)
                    self.act(self.pc[pi], self.ps[bS][:], AF.Exp, reads=[self.PS[bS]], writes=[self.PC[pi]])
                    for qb in range(2):
                        for kb in range(2):
                            self.mm(self.ps[bO[qb]][:, hh * 65:(hh + 1) * 65], self.pc[pi][:, kb * 256 + qb * 128:kb * 256 + (qb + 1) * 128],
                                    self.vt[:, s * 2 + kb, h * 65:(h + 1) * 65], kb == 0, kb == 1,
                                    reads=[self.PC[pi], self.V[s * 2 + kb]], bankbuf=self.PS[bO[qb]])
                for qb in range(2):
                    self.o_evac(bO[qb], s * 2 + qb, half)
                    self.unhold(bO[qb])
            for qb in range(2):
                q4 = s * 2 + qb
                self.o_transpose(q4, self.oT[:, :, q4 * 128:(q4 + 1) * 128])

    def attn_prep(self, ti, j):
        S = self.S
        lt = ti - self.nct
        rows = 8 * self.nlt
        w0 = 8 * lt - 4
        base = self.nct * T
        bs = BLK_START[j]
        kb_i = j % 2
        for gi in range(4):
            vb = kb_i * 4 + gi
            for krl in range(4):
                r = w0 + 4 * gi + krl
                if 0 <= r < rows:
                    t0 = base + r * 64 + bs
                    rd = [self.Dt["vs"][t0 // T]]
                    self.dma("sp", self.vt[krl * 32:(krl + 1) * 32, vb, :], S["vs"][t0:t0 + 32, :], reads=rd, writes=[self.V[vb]])
        for cc in range(4):
            self.cp("pool", self.kblk[kb_i][:, cc, :].rearrange("p (r w) -> p r w", w=32),
                    self.kt[:, cc, :].rearrange("p (r w) -> p r w", w=64)[:, :, bs:bs + 32],
                    reads=[self.K[cc]], writes=[self.KB[kb_i][cc]])

    def attn_lat(self, l, ti):
        S, I = self.S, self.I
        lt = ti - self.nct
        rows = 8 * self.nlt
        w0 = 8 * lt - 4
        base = self.nct * T
        if lt == 0:
            self.load_cache(l)
        for j in range(4):
            cp = CP_OF_J[j]
            bs = BLK_START[j]
            kb_i = j % 2
            if j + 1 < 4:
                self.attn_prep(ti, j + 1)
            slabs = [self.slab(S[f"bt_{l}"][self.mask_idx[(lt, cp)], hq * 4:(hq + 1) * 4].rearrange("h p k -> p h k"), 2048, self.Dw[f"bt_{l}"], (4, 512))
                     for hq in range(2)]
            bOs = {}

            def scores(h):
                half, hh = divmod(h, 4)
                cc, p0 = h // 2, 64 * (h % 2)
                qv = self.qt[p0:p0 + 64, cc, :].rearrange("p (a w) -> p a w", w=64)[:, :, 16 * j:16 * j + 16]
                bS = self.bank()
                for gi in range(4):
                    self.mm(self.ps[bS][:, gi * 128:(gi + 1) * 128], self.kblk[kb_i][p0:p0 + 64, cc, gi * 128:(gi + 1) * 128], qv,
                            True, True, reads=[self.KB[kb_i][cc], self.Q[cc]], bankbuf=self.PS[bS])
                bC = self.bank()
                for cb in range(4):
                    self.mm(self.ps[bC][:, cb * 128:(cb + 1) * 128], self.kc[p0:p0 + 64, cc, cb * 128:(cb + 1) * 128], qv,
                            True, True, reads=[self.KCb, self.Q[cc]], bankbuf=self.PS[bC])
                pi = self.rot("pl", 3)
                tfi = self.rot("tf", 3)
                self.tt("dve", self.tmpf[tfi][:], self.ps[bS][:], slabs[half].ap[:, hh, :], ALU.add,
                        reads=[self.PS[bS], slabs[half].buf], writes=[self.TF[tfi]])
                ci = self.rot("pc", 3)
                self.act(self.pc[ci], self.ps[bC][:], AF.Exp, reads=[self.PS[bC]], writes=[self.PC[ci]])
                self.act(self.pl[pi], self.tmpf[tfi][:], AF.Exp, reads=[self.TF[tfi]], writes=[self.PL[pi]])
                return (h, pi, ci)

            def pv(st_):
                h, pi, ci = st_
                half, hh = divmod(h, 4)
                bO = bOs[half]
                oview = self.ps[bO][:, hh * 65:(hh + 1) * 65]
                for gi in range(4):
                    self.mm(oview, self.pl[pi][:, gi * 128:(gi + 1) * 128], self.vt[:, kb_i * 4 + gi, h * 65:(h + 1) * 65],
                            gi == 0, False, reads=[self.PL[pi], self.V[kb_i * 4 + gi]], bankbuf=self.PS[bO])
                for cb in range(4):
                    self.mm(oview, self.pc[ci][:, cb * 128:(cb + 1) * 128], self.vc[:, cb, h * 65:(h + 1) * 65],
                            False, cb == 3, reads=[self.PC[ci], self.VCb], bankbuf=self.PS[bO])
                if hh == 3:
                    self.o_evac(bO, j, half)
                    self.unhold(bO)

            pending = None
            for h in range(8):
                if h % 4 == 0:
                    bOs[h // 4] = self.bank()
                    self.hold(bOs[h // 4])
                cur = scores(h)
                if pending is not None:
                    pv(pending)
                pending = cur
            pv(pending)
            for sl in slabs:
                self.done(sl)
            self.o_transpose(j, self.oT[:, :, :].rearrange("p c (a w) -> p c a w", w=64)[:, :, :, 16 * j:16 * j + 16])

    def load_cache(self, l):
        I = self.I
        for kb in range(4):
            self.dma("pool", self.vc[:, kb, :].rearrange("p (h e) -> p h e", e=65)[:, :, 0:64],
                     I["cv"][l, kb * 128:(kb + 1) * 128, :].rearrange("p (h d) -> p h d", d=64), writes=[self.VCb], slow=True)
        for kb in range(4):
            tb = self.rot("tm", 2)
            self.dma("sp", self.tm[tb][:, 0:512], I["ck"][l, kb * 128:(kb + 1) * 128, :], writes=[self.TM[tb]])
            b = self.bank()
            for cc in range(4):
                self.tr(self.ps[b][:, cc * 128:(cc + 1) * 128], self.tm[tb][:, cc * 128:(cc + 1) * 128], self.identF[:],
                        reads=[self.TM[tb], self.CONST], bankbuf=self.PS[b])
            self.cp("dve", self.kc[:, :, kb * 128:(kb + 1) * 128], self.ps[b][:].rearrange("p (c t) -> p c t", c=4),
                    reads=[self.PS[b]], writes=[self.KCb] + self.KC4)

    def gates_merge(self, l, xb, ub, tokm):
        S = self.S
        for p in range(4):
            sl = {}
            for bi, br in enumerate("abc"):
                sl["g" + br] = self.in_slab(l, 16 + 4 * bi + p)
                sl["o" + br] = self.slab(S[f"o{br}_{l}"][p], 1024, self.Dw[f"o{br}_{l}"], (4, 256))
            for jj in range(2):
                mc = 2 * p + jj
                mi = self.rot("ma", 2)
                for bi, (br, src, SB) in enumerate((("a", self.sA, self.SA), ("b", self.tB, self.TB), ("c", self.oT, self.OT))):
                    bg = self.proj_chunk(sl["g" + br], jj, ub)
                    by = self.bank()
                    for kc in range(4):
                        self.mm(self.ps[by][:], sl["o" + br].ap[:, kc, jj * 128:(jj + 1) * 128], src[:, kc, :], kc == 0, kc == 3,
                                reads=[sl["o" + br].buf, SB[kc]], bankbuf=self.PS[by])
                    ti_ = self.rot("tf", 3)
                    self.act(self.tmpf[ti_][:], self.ps[bg][:], AF.Sigmoid, reads=[self.PS[bg]], writes=[self.TF[ti_]])
                    if bi == 0:
                        self.tt("dve", self.macc[mi], self.tmpf[ti_][:], self.ps[by][:], ALU.mult,
                                reads=[self.TF[ti_], self.PS[by]], writes=self.MA[mi])
                    else:
                        self.tt("dve", self.tmpf[ti_][:], self.tmpf[ti_][:], self.ps[by][:], ALU.mult,
                                reads=[self.TF[ti_], self.PS[by]], writes=[self.TF[ti_]])
                        if bi == 1:
                            self.tt("pool", self.macc[mi], self.macc[mi], self.tmpf[ti_][:], ALU.add,
                                    reads=self.MA[mi] + [self.TF[ti_]], writes=self.MA[mi])
                        else:
                            self.tt("pool", self.hv(mc), self.macc[mi], self.tmpf[ti_][:], ALU.add,
                                    reads=self.MA[mi] + [self.TF[ti_]], writes=[self.H[mc]])
            for s_ in sl.values():
                self.done(s_)
        for p in range(4):
            sm = self.slab(S[f"mg_{l}"][p], 2048, self.Dw[f"mg_{l}"], (8, 256))
            for jj in range(2):
                mc2 = 2 * p + jj
                b = self.bank()
                for mc in range(8):
                    self.mm(self.ps[b][:], sm.ap[:, mc, jj * 128:(jj + 1) * 128], self.hv(mc), mc == 0, mc == 7,
                            reads=[sm.buf, self.H[mc]], bankbuf=self.PS[b])
                self.stt("dve", self.xt[xb][:, mc2, :], self.ps[b][:], self.col(l, C_GT + (8 + mc2) * 2 + tokm),
                         self.xt[xb][:, mc2, :], ALU.mult, ALU.add,
                         reads=[self.PS[b], self.X[xb][mc2], self.COLS, self.MODS], writes=[self.X[xb][mc2]])
            self.done(sm)
    def emit_all(self):
        NL = self.NL
        self.reset_state()
        self.setup()
        self.memset("dve", self.col(0, C_EPS), EPS, [self.MODS])
        self.convert_p1(0)
        self.ada(0)
        self.COLS.const = True
        self.CONST.const = True
        self.SCT.const = True
        tasks_by_layer = {0: self.table_tasks(0) + [lambda: self.convert_p2(0)]}
        for l in range(1, NL):
            def ada_l(ll=l):
                self.MODS.const = False
                self.ada(ll)
                self.MODS.const = True
            tasks_by_layer[0] += [ada_l, (lambda ll=l: self.convert_p1(ll)), (lambda ll=l: self.convert_p2(ll))]
            tasks_by_layer[l] = self.table_tasks(l)

        self.MODS.const = True
        steps = []
        for l in range(NL):
            steps += [("p1", l, ti) for ti in range(self.NT)]
            steps += [("p2", l, ti) for ti in range(self.NT)]
        loaded = [-1]

        def load_step(k):
            ph, l_, ti_ = steps[k]
            loaded[0] = k
            (self.p1_load if ph == "p1" else self.p2_load)(l_, ti_, k % 2)

        for k, (ph, l, ti) in enumerate(steps):
            if loaded[0] < k:
                load_step(k)
            nxt = steps[k + 1] if k + 1 < len(steps) else None
            pf = None
            if nxt is not None and not (ph == "p1" and nxt[0] == "p2"):
                pf = (lambda kk=k + 1: load_step(kk))
            if ph == "p1":
                self.p1_compute(l, ti, k % 2, pf)
                tl = tasks_by_layer.get(l, [])
                per_slot = -(-(len(tl) + ti) // self.NT) if tl else 0
                for _ in range(max(per_slot, 2)):
                    if tl:
                        tl.pop(0)()
                if ti == self.NT - 1:
                    while tl:
                        tl.pop(0)()
            else:
                self.p2_compute(l, ti, k % 2, pf)
        if not self.dry:
            E = self.eng["pool"]
            deps = {}
            for ev in self.final_events:
                if ev is not None and deps.get(ev[0], 0) < ev[1]:
                    deps[ev[0]] = ev[1]
            waits = self._prune(E, deps)
            E.count += 1
            E.ops.append((waits, lambda e: e.memset(self.small[:, 60:64], 0.0), (E.sem, 1)))

    def replay(self):
        nc = self.nc
        sems = self.sem_handles

        def run(e, ops):
            for waits, fn, inc in ops:
                for s, v in waits:
                    e.wait_ge(sems[s], v)
                ins = fn(e)
                ins.then_inc(sems[inc[0]], inc[1])

        with nc.Block() as block:
            @block.tensor
            def _(e):
                run(e, self.eng["pe"].ops)

            @block.scalar
            def _(e):
                run(e, self.eng["act"].ops)

            @block.vector
            def _(e):
                run(e, self.eng["dve"].ops)

            @block.gpsimd
            def _(e):
                run(e, self.eng["pool"].ops)

            @block.sync
            def _(e):
                run(e, self.eng["sp"].ops)


def build_program(NL, nct, nlt):
    nc = bass.Bass("TRN2", target_bir_lowering=False)
    P = Prog(nc, NL, nct, nlt)
    P.declare()
    with contextlib.ExitStack() as stack:
        P.alloc(stack)
        P.dry = True
        P.emit_all()
        P.dry = False
        P.final_events = []
        P.emit_all()
        P.replay()
    return nc, P


def host_consts(P):
    ident = np.eye(128, dtype=np.float32)
    anti = np.ascontiguousarray(ident[::-1])
    ones = np.ones((128, 128), np.float32)
    blk = np.zeros((128, 128), np.float32)
    blk[:64, :64] = 1.0
    blk[64:, 64:] = 1.0
    return {"c_ident": ident, "c_anti": anti, "c_ones": ones, "c_blk": blk,
            "c_mask": np.stack(P.mask_list).astype(np.float32)}


_CACHE = {}


def run_cores(inputs, NL, nct, nlt, ncores):
    key = (NL, nct, nlt)
    if key not in _CACHE:
        _CACHE[key] = build_program(NL, nct, nlt)
    nc, P = _CACHE[key]
    cst = host_consts(P)
    f = lambda a: np.ascontiguousarray(np.asarray(a, dtype=np.float32))
    nseq = 2 * nct
    in_maps = []
    for c in range(ncores):
        m = dict(cst)
        m["xp"] = f(inputs["x_prompt"][c * nseq:(c + 1) * nseq]).reshape(nseq * 256, D)
        m["xs"] = f(inputs["x_sample"][c]).reshape(-1, D)
        m["ck"] = f(inputs["cache_k"][c]).reshape(NL, PAST, 512)
        m["cv"] = f(inputs["cache_v"][c]).reshape(NL, PAST, 512)
        m["cvec"] = np.stack([f(inputs["c"][c]), f(inputs["c_ctx"])])
        for nm, _ in WEIGHT_SHAPES:
            m[nm] = f(inputs[nm])
        in_maps.append(m)
    res = run_bass_kernel_spmd(nc, in_maps, core_ids=list(range(ncores)))
    outs = res.results
    yp = np.concatenate([o["yp"].reshape(nseq, 256, D) for o in outs], axis=0)
    ys = np.stack([o["ys"] for o in outs], axis=0)
    nk = np.concatenate([o["nk"].reshape(nseq, NL, 256, NH, HD) for o in outs], axis=0)
    nv = np.concatenate([o["nv"].reshape(nseq, NL, 256, NH, HD) for o in outs], axis=0)
    return (yp.astype(np.float32), ys.astype(np.float32), nk.astype(np.float32), nv.astype(np.float32))


def kernel(**inputs):
    return run_cores(inputs, 2, 2, 8, 8)
```

```python
import contextlib
import numpy as np
import concourse.bass as bass
import concourse.mybir as mybir
from concourse.bass_utils import run_bass_kernel_spmd

F32 = mybir.dt.float32
BF16 = mybir.dt.bfloat16
AF = mybir.ActivationFunctionType
ALU = mybir.AluOpType
AX = mybir.AxisListType

D = 1024
KC = 8
DFF = 2816
FC = 22
NIN = 7168
T = 512
NH = 8
HD = 64
PAST = 512
EPS = 1e-6
NSLOT = 9
SLOT = 2048
BLK_START = [0, 8, 24, 32]
CP_OF_J = [0, 1, 1, 2]
CP_OFF = [16, 8, 0]
RP_R, RP_C = 23, 63
NCOL = 420
C_G = 0
C_CAW = 24
C_CAB = 148
C_LNG = 152
C_LNB = 156
C_CBW = 160
C_GQ = 172
C_GK = 173
C_MOD = 176
C_A = 320
C_GT = 368
C_EPS = 416
WEIGHT_SHAPES = [("w_ada", [D, 9 * D]), ("b_ada", [9 * D]), ("g_ff1", [D]), ("w_ff1_gate", [D, DFF]), ("w_ff1_up", [D, DFF]),
                 ("w_ff1_down", [DFF, D]), ("g_mix", [D]), ("w_in", [D, NIN]), ("conv_a_w", [31, 512]),
                 ("conv_a_b", [512]), ("ln_a_g", [512]), ("ln_a_b", [512]), ("w_a_out", [512, D]),
                 ("conv_b_w", [3, 512]), ("w_b_out", [512, D]), ("q_norm_g", [64]), ("k_norm_g", [64]),
                 ("rpb", [8, 15, 31]), ("w_c_out", [512, D]), ("w_merge", [D, D]), ("g_ff2", [D]),
                 ("w_ff2_gate", [D, DFF]), ("w_ff2_up", [D, DFF]), ("w_ff2_down", [DFF, D])]


class Buf:
    __slots__ = ("name", "w", "r", "const")

    def __init__(self, name):
        self.name = name
        self.w = None
        self.r = {}
        self.const = False


class Eng:
    def __init__(self, name):
        self.name = name
        self.ops = []
        self.count = 0
        self.sem = None
        self.waited = {}
        self.dcount = 0


class DSem:
    def __init__(self, idx):
        self.idx = idx
        self.val = 0


class Slab:
    __slots__ = ("n", "slot", "ap", "buf")


def mask_for_tile(t, ntl, cp):
    rows = 8 * ntl
    kr_n = min(8, rows)
    m = np.zeros((128, 4, 8, 16), np.float32)
    j = [0, 1, 3][cp]
    bs = BLK_START[j]
    w0 = 8 * t - 4
    for gi in range(4):
        for krl in range(4):
            krow = w0 + 4 * gi + krl
            if krow < 0 or krow >= rows:
                continue
            for a in range(8):
                r = 8 * t + a
                rs = min(max(r - kr_n // 2, 0), rows - kr_n)
                if not (rs <= krow < rs + kr_n):
                    continue
                for kcl in range(32):
                    kc_ = bs + kcl
                    for qc in range(16):
                        q = 16 * j + qc
                        ws = min(max(q - 8, 0), 48)
                        if ws <= kc_ < ws + 16:
                            m[krl * 32 + kcl, gi, a, qc] = 1.0
    return m.reshape(128, 512)


class Prog:
    def __init__(self, nc, NL, n_ctx_tiles, n_lat_tiles):
        self.nc = nc
        self.NL = NL
        self.nct = n_ctx_tiles
        self.nlt = n_lat_tiles
        self.NT = n_ctx_tiles + n_lat_tiles
        self.nseq = 2 * n_ctx_tiles
        self.LT = T * n_lat_tiles
        self.NTOK = T * self.NT
        self.eng = {k: Eng(k) for k in ("pe", "act", "dve", "pool", "sp")}
        self.sem_handles = []
        self.dry = False
        self.requests = []
        self.req_list = []
        self.final_events = []
        pats = {}
        self.mask_idx = {}
        self.mask_list = []
        for t in range(n_lat_tiles):
            for cp in range(3):
                m = mask_for_tile(t, n_lat_tiles, cp)
                key = m.tobytes()
                if key not in pats:
                    pats[key] = len(self.mask_list)
                    self.mask_list.append(m)
                self.mask_idx[(t, cp)] = pats[key]

    def new_sem(self, stack, name):
        h = stack.enter_context(self.nc.semaphore(name))
        self.sem_handles.append(h)
        return len(self.sem_handles) - 1

    def _collect(self, reads, writes):
        deps = {}

        def add(ev):
            if ev is None:
                return
            s, v = ev
            if deps.get(s, 0) < v:
                deps[s] = v

        for b in reads:
            add(b.w)
        for b in writes:
            add(b.w)
            for s, v in b.r.items():
                add((s, v))
        return deps

    def _prune(self, E, deps):
        waits = []
        for s, v in deps.items():
            if E.name == "pe" and s == E.sem:
                continue
            if E.waited.get(s, 0) >= v:
                continue
            E.waited[s] = v
            waits.append((s, v))
        return waits

    def _mark(self, ev, reads, writes):
        s, v = ev
        for b in reads:
            if b.const:
                continue
            if b.r.get(s, 0) < v:
                b.r[s] = v
        for b in writes:
            b.w = ev
            b.r = {}

    def op(self, eng, fn, reads=(), writes=()):
        if self.dry:
            return None
        E = self.eng[eng]
        deps = self._collect(reads, writes)
        waits = self._prune(E, deps)
        E.count += 1
        ev = (E.sem, E.count)
        E.ops.append((waits, fn, (E.sem, 1)))
        self._mark(ev, reads, writes)
        return ev

    def dma(self, q, out, in_, reads=(), writes=(), sem=None, throttle=True, slow=False):
        if self.dry:
            return None
        E = self.eng[q]
        if sem is None:
            pool = self.dsems[q]
            S = pool[E.dcount % len(pool)]
            E.dcount += 1
        else:
            S = sem
        deps = self._collect(reads, writes)
        if throttle and S.val > 0:
            if deps.get(S.idx, 0) < S.val:
                deps[S.idx] = S.val
        waits = self._prune(E, deps)
        S.val += 16
        ev = (S.idx, S.val)
        E.ops.append((waits, (lambda e, o=out, i=in_, sl=slow: e.dma_start(out=o, in_=i, allow_slow_non_contiguous=True) if sl else e.dma_start(out=o, in_=i)), (S.idx, 16)))
        self._mark(ev, reads, writes)
        return ev

    def bank(self):
        st = self.st
        for _ in range(8):
            b = st["bank_rr"]
            st["bank_rr"] = (b + 1) % 8
            if b not in st["held"]:
                return b
        raise RuntimeError("no free psum bank")

    def hold(self, b):
        self.st["held"].add(b)

    def unhold(self, b):
        self.st["held"].discard(b)

    def rot(self, key, n):
        v = self.st.get(key, 0)
        self.st[key] = (v + 1) % n
        return v

    def slab(self, src_ap, nelem, srcbuf, shape):
        st = self.st
        n = st["req_i"]
        st["req_i"] += 1
        s = Slab()
        s.n = n
        s.slot = n % NSLOT
        s.buf = self.R[s.slot]
        base = self.ring[:, s.slot * SLOT: s.slot * SLOT + nelem]
        s.ap = base.rearrange("p (a b) -> p a b", a=shape[0])
        dst = s.ap if src_ap.ndim == 3 else base
        if self.dry:
            self.req_list.append((src_ap, dst, srcbuf))
            return s
        self.pump()
        assert st["loaded"] > n, "slab load not emitted"
        return s

    def pump(self):
        st = self.st
        L = self.req_list
        while st["loaded"] < len(L):
            m = st["loaded"]
            if m >= NSLOT and (m - NSLOT) not in st["released"]:
                break
            src_ap, dst, srcbuf = L[m]
            if srcbuf.w is None and m >= st["req_i"]:
                break
            slot = m % NSLOT
            self.dma("sp", dst, src_ap, reads=[srcbuf], writes=[self.R[slot]])
            st["loaded"] += 1
            st["released"].discard(m - NSLOT)

    def done(self, s):
        if self.dry:
            return
        self.st["released"].add(s.n)
        self.pump()

    def mm(self, out, lhsT, rhs, start, stop, reads, bankbuf):
        return self.op("pe", lambda e, o=out, l=lhsT, r=rhs, a=start, b=stop: e.matmul(o, l, r, start=a, stop=b),
                       reads=reads, writes=[bankbuf])

    def tr(self, out, in_, ident, reads, bankbuf):
        return self.op("pe", lambda e, o=out, i=in_, d=ident: e.transpose(o, i, d), reads=reads, writes=[bankbuf])

    def act(self, out, in_, func, reads, writes, bias=None, scale=None):
        kw = {}
        if bias is not None:
            kw["bias"] = bias
        if scale is not None:
            kw["scale"] = scale
        return self.op("act", lambda e, o=out, i=in_, f=func, k=kw: e.activation(out=o, in_=i, func=f, **k),
                       reads=list(reads) + [self.MODS, self.COLS], writes=writes)

    def tt(self, eng, out, in0, in1, op, reads, writes):
        return self.op(eng, lambda e, o=out, a=in0, b=in1, p=op: e.tensor_tensor(out=o, in0=a, in1=b, op=p),
                       reads=reads, writes=writes)

    def stt(self, eng, out, in0, scalar, in1, op0, op1, reads, writes):
        return self.op(eng, lambda e, o=out, a=in0, s=scalar, b=in1, p0=op0, p1=op1:
                       e.scalar_tensor_tensor(out=o, in0=a, scalar=s, in1=b, op0=p0, op1=p1),
                       reads=reads, writes=writes)

    def ts(self, eng, out, in0, s1, s2, op0, op1, reads, writes):
        if s2 is None:
            return self.op(eng, lambda e, o=out, a=in0, x=s1, p0=op0: e.tensor_scalar(out=o, in0=a, scalar1=x, scalar2=None, op0=p0),
                           reads=reads, writes=writes)
        return self.op(eng, lambda e, o=out, a=in0, x=s1, y=s2, p0=op0, p1=op1:
                       e.tensor_scalar(out=o, in0=a, scalar1=x, scalar2=y, op0=p0, op1=p1),
                       reads=reads, writes=writes)

    def cp(self, eng, out, in_, reads, writes):
        if eng == "act":
            return self.op("act", lambda e, o=out, i=in_: e.copy(out=o, in_=i), reads=reads, writes=writes)
        return self.op(eng, lambda e, o=out, i=in_: e.tensor_copy(out=o, in_=i), reads=reads, writes=writes)

    def recip(self, out, in_, reads, writes):
        return self.op("dve", lambda e, o=out, i=in_: e.reciprocal(out=o, in_=i), reads=reads, writes=writes)

    def memset(self, eng, ap, val, writes):
        return self.op(eng, lambda e, a=ap, v=val: e.memset(a, v), reads=(), writes=writes)

    def declare(self):
        nc = self.nc
        NL = self.NL

        def din(name, shape):
            return nc.dram_tensor(name, list(shape), F32, kind="ExternalInput").ap()

        def dout(name, shape):
            return nc.dram_tensor(name, list(shape), F32, kind="ExternalOutput").ap()

        def scr(name, shape, dt=BF16):
            return nc.dram_tensor(name, list(shape), dt, kind="Internal").ap()

        I = {}
        I["xp"] = din("xp", [self.nct * T, D])
        I["xs"] = din("xs", [self.LT, D])
        I["ck"] = din("ck", [NL, PAST, 512])
        I["cv"] = din("cv", [NL, PAST, 512])
        I["cvec"] = din("cvec", [2, D])
        for nm, sh in WEIGHT_SHAPES:
            I[nm] = din(nm, [NL] + sh)
        I["c_ident"] = din("c_ident", [128, 128])
        I["c_anti"] = din("c_anti", [128, 128])
        I["c_ones"] = din("c_ones", [128, 128])
        I["c_blk"] = din("c_blk", [128, 128])
        I["c_mask"] = din("c_mask", [len(self.mask_list), 128, 512])
        self.I = I
        O = {}
        O["yp"] = dout("yp", [self.nct * T, D])
        O["ys"] = dout("ys", [self.LT, D])
        O["nk"] = dout("nk", [self.nseq, NL, 256, 512])
        O["nv"] = dout("nv", [self.nseq, NL, 256, 512])
        self.O = O
        S = {}
        for l in range(NL):
            for f in ("1", "2"):
                S[f"g{f}_{l}"] = scr(f"wg{f}_{l}", [11, 128, 2048])
                S[f"u{f}_{l}"] = scr(f"wu{f}_{l}", [11, 128, 2048])
                S[f"d{f}_{l}"] = scr(f"wd{f}_{l}", [16, 128, 11 * 128])
            S[f"in_{l}"] = scr(f"win_{l}", [28, 128, 2048])
            for b in "abc":
                S[f"o{b}_{l}"] = scr(f"wo{b}_{l}", [4, 128, 4 * 256])
            S[f"mg_{l}"] = scr(f"wmg_{l}", [4, 128, 2048])
            S[f"ca_{l}"] = scr(f"wca_{l}", [8, 128, 2048])
            S[f"cb_{l}"] = scr(f"wcb_{l}", [1, 128, 12 * 128])
            S[f"bt_{l}"] = scr(f"wbt_{l}", [len(self.mask_list), 8, 128, 512])
            S[f"rp_{l}"] = scr(f"rp_{l}", [8, 1472])
        S["xres"] = scr("xres", [D, self.NTOK], F32)
        S["u2s"] = scr("u2s", [D, self.NTOK])
        S["xas"] = scr("xas", [512, self.NTOK])
        S["chs"] = scr("chs", [512, self.NTOK])
        S["bgs"] = scr("bgs", [512, self.NTOK])
        S["qs"] = scr("qs", [512, self.NTOK])
        S["ks"] = scr("ks", [512, self.NTOK])
        S["vs"] = scr("vs", [self.NTOK, 520])
        self.S = S
        self.Dw = {k: Buf("D" + k) for k in S}
        self.Dt = {}
        for nm in ("xres", "u2s", "xas", "chs", "bgs", "qs", "ks", "vs"):
            self.Dt[nm] = [Buf(f"D{nm}{i}") for i in range(self.NT)]
        self.Dout = Buf("Dout")

    def alloc(self, stack):
        nc = self.nc
        A = nc.alloc_sbuf_tensor
        self.xt = [A(f"xt{i}", [128, KC, T], F32) for i in range(2)]
        self.X = [[Buf(f"X{i}_{c}") for c in range(KC)] for i in range(2)]
        self.ut = [A(f"ut{i}", [128, KC, T], BF16) for i in range(2)]
        self.U = [[Buf(f"U{i}_{c}") for c in range(KC)] for i in range(2)]
        self.harena = A("harena", [128, FC * T], BF16)
        self.H = [Buf(f"H{j}") for j in range(FC)]
        self.ring = A("ring", [128, NSLOT * SLOT], BF16)
        self.R = [Buf(f"R{i}") for i in range(NSLOT)]
        self.tm = [A(f"tm{i}", [128, D], F32) for i in range(2)]
        self.TM = [Buf(f"TM{i}") for i in range(2)]
        self.sqb = [A(f"sqb{i}", [128, T], BF16) for i in range(2)]
        self.SQ = [Buf(f"SQ{i}") for i in range(2)]
        self.tmpf = [A(f"tmpf{i}", [128, T], F32) for i in range(3)]
        self.TF = [Buf(f"TF{i}") for i in range(3)]
        self.stat = [A(f"stat{i}", [128, T], F32) for i in range(3)]
        self.STB = [Buf(f"ST{i}") for i in range(3)]
        self.xa = A("xa", [128, 4, 572], BF16)
        self.XA = [Buf(f"XA{c}") for c in range(4)]
        self.ch = A("ch", [128, 4, 516], BF16)
        self.CH = [Buf(f"CH{c}") for c in range(4)]
        self.bg = A("bg", [128, 4, T], BF16)
        self.BG = [Buf(f"BG{c}") for c in range(4)]
        self.qt = A("qt", [128, 4, T], BF16)
        self.Q = [Buf(f"Q{c}") for c in range(4)]
        self.kt = A("kt", [128, 4, 2 * T], BF16)
        self.K = [Buf(f"K{c}") for c in range(4)]
        self.kblk = [A(f"kblk{i}", [128, 4, 4 * 128], BF16) for i in range(2)]
        self.KB = [[Buf(f"KB{i}_{c}") for c in range(4)] for i in range(2)]
        self.vt = A("vt", [128, 8, 520], BF16)
        self.V = [Buf(f"V{i}") for i in range(8)]
        self.maskt = A("maskt", [128, 3, T], BF16)
        self.MK = [Buf(f"MK{i}") for i in range(3)]
        self.sA = A("sA", [128, 4, T], BF16)
        self.SA = [Buf(f"SA{c}") for c in range(4)]
        self.tB = A("tB", [128, 4, T], BF16)
        self.TB = [Buf(f"TB{c}") for c in range(4)]
        self.oT = A("oT", [128, 4, T], BF16)
        self.OT = [Buf(f"OT{c}") for c in range(4)]
        self.kc = A("kc", [128, 4, PAST], BF16)
        self.KCb = Buf("KC")
        self.vc = A("vc", [128, 4, 520], BF16)
        self.VCb = Buf("VC")
        self.identF = A("identF", [128, 128], F32)
        self.identB = A("identB", [128, 128], BF16)
        self.anti = A("anti", [128, 128], BF16)
        self.ones = A("ones", [128, 128], BF16)
        self.blk = A("blk", [128, 128], BF16)
        self.CONST = Buf("CONST")
        self.cols = A("cols", [128, self.NL * NCOL], F32)
        self.COLS = Buf("COLS")
        self.MODS = Buf("MODS")
        self.gkrow = A("gkrow", [128, self.NL, 512], F32)
        self.small = A("small", [128, 64], F32)
        self.SM = [Buf(f"SM{i}") for i in range(4)]
        self.scT = A("scT", [128, KC, 2], F32)
        self.SCT = Buf("SCT")
        self.ps = [nc.alloc_psum_tensor(f"ps{i}", [128, T], F32) for i in range(8)]
        self.PS = [Buf(f"PS{i}") for i in range(8)]
        ha = self.harena
        self.ha_f = [ha[:, (2 * c) * T:(2 * c + 2) * T].bitcast(F32) for c in range(4)]
        self.HA = [[self.H[2 * c], self.H[2 * c + 1]] for c in range(4)]
        self.otm = [ha[:, (8 + q) * T:(9 + q) * T] for q in range(4)]
        self.OTM = [self.H[8 + q] for q in range(4)]
        self.pl = [ha[:, (12 + i) * T:(13 + i) * T] for i in range(3)]
        self.PL = [self.H[12 + i] for i in range(3)]
        self.pc = [ha[:, (15 + i) * T:(16 + i) * T] for i in range(3)]
        self.PC = [self.H[15 + i] for i in range(3)]
        self.macc = [ha[:, (18 + 2 * i) * T:(20 + 2 * i) * T].bitcast(F32) for i in range(2)]
        self.MA = [[self.H[18 + 2 * i], self.H[19 + 2 * i]] for i in range(2)]
        for k in ("pe", "act", "dve", "pool"):
            self.eng[k].sem = self.new_sem(stack, "c_" + k)
        self.dsems = {"sp": [DSem(self.new_sem(stack, f"dsp{i}")) for i in range(20)],
                      "pool": [DSem(self.new_sem(stack, f"dpl{i}")) for i in range(8)],
                      "act": [DSem(self.new_sem(stack, f"dac{i}")) for i in range(8)]}
        self.KC4 = [Buf(f"KC4_{i}") for i in range(4)]
        self.csem = {}
        for k in self.S:
            if k[0] in "gudiomcbr" and "_" in k:
                self.csem[k] = DSem(self.new_sem(stack, "cv_" + k))

    def hv(self, j):
        return self.harena[:, j * T:(j + 1) * T]

    def col(self, l, idx, n=1):
        b = l * NCOL + idx
        return self.cols[:, b:b + n]

    def reset_state(self):
        self.st = {"bank_rr": 0, "held": set(), "req_i": 0, "pending": [], "loaded": 0, "released": set()}

    def setup(self):
        I, S = self.I, self.S
        NL = self.NL
        C = [self.CONST]
        self.dma("pool", self.identB[:], I["c_ident"], writes=C)
        self.dma("pool", self.anti[:], I["c_anti"], writes=C)
        self.dma("pool", self.ones[:], I["c_ones"], writes=C)
        self.dma("pool", self.blk[:], I["c_blk"], writes=C)
        self.dma("sp", self.identF[:], I["c_ident"], writes=C)
        self.memset("pool", self.kt[:], 0.0, self.K)
        self.memset("pool", self.vt[:], 0.0, self.V)
        self.memset("pool", self.vc[:], 0.0, [self.VCb])
        self.memset("pool", self.xa[:], 0.0, self.XA)
        self.memset("pool", self.ch[:], 0.0, self.CH)
        self.memset("dve", self.kblk[0][:], 0.0, self.KB[0])
        self.memset("dve", self.kblk[1][:], 0.0, self.KB[1])
        self.memset("dve", self.vt[:].rearrange("p b (h e) -> p b h e", e=65)[:, :, :, 64:65], 1.0, self.V)
        self.memset("dve", self.vc[:].rearrange("p b (h e) -> p b h e", e=65)[:, :, :, 64:65], 1.0, [self.VCb])
        self.memset("dve", self.small[:], 0.0, self.SM)
        CL = [self.COLS]
        for l in range(NL):
            for i, nm in enumerate(("g_ff1", "g_mix", "g_ff2")):
                self.dma("sp", self.col(l, C_G + 8 * i, 8), I[nm][l].rearrange("(c p) -> p c", p=128), writes=CL, slow=True)
            for cc in range(4):
                self.dma("sp", self.col(l, C_CAW, 124).rearrange("p (k c) -> p k c", c=4)[:, :, cc],
                         I["conv_a_w"][l][:, cc * 128:(cc + 1) * 128].rearrange("k p -> p k"), writes=CL, slow=True)
                self.dma("sp", self.col(l, C_CBW, 12).rearrange("p (k c) -> p k c", c=4)[:, :, cc],
                         I["conv_b_w"][l][:, cc * 128:(cc + 1) * 128].rearrange("k p -> p k"), writes=CL, slow=True)
            for idx, nm in ((C_CAB, "conv_a_b"), (C_LNG, "ln_a_g"), (C_LNB, "ln_a_b")):
                self.dma("sp", self.col(l, idx, 4), I[nm][l].rearrange("(c p) -> p c", p=128), writes=CL, slow=True)
            for idx, nm in ((C_GQ, "q_norm_g"), (C_GK, "k_norm_g")):
                for hh in range(2):
                    self.dma("sp", self.cols[hh * 64:(hh + 1) * 64, l * NCOL + idx:l * NCOL + idx + 1],
                             I[nm][l].rearrange("(p o) -> p o", o=1), writes=CL, slow=True)
            self.dma("sp", self.gkrow[:, l, :].rearrange("p (h d) -> p h d", h=8),
                     bass.AP(I["k_norm_g"].tensor, l * 64, [[0, 128], [0, 8], [1, 64]]), writes=CL)
            self.ts("dve", self.col(l, C_GQ), self.col(l, C_GQ), 0.125, None, ALU.mult, None, reads=CL, writes=CL)
        for t_ in range(2):
            self.dma("sp", self.scT[:, :, t_], I["cvec"][t_].rearrange("(c p) -> p c", p=128), writes=[self.SCT], slow=True)
        self.act(self.scT[:], self.scT[:], AF.Silu, reads=[self.SCT], writes=[self.SCT])

    def ada(self, l):
        I = self.I
        CL = [self.COLS]
        ML = [self.MODS]
        wv = I["w_ada"][l].rearrange("(kc p) n -> p kc n", p=128)
        for ct in range(36):
            hb = ct % 2
            stg = self.harena[:, hb * 4096:(hb + 1) * 4096].bitcast(F32).rearrange("p (k n) -> p k n", k=KC)
            HB = self.H[hb * 8:hb * 8 + 8]
            self.dma("sp", stg, wv[:, :, ct * 256:(ct + 1) * 256], writes=HB)
            ti = self.rot("tf", 3)
            self.dma("sp", self.tmpf[ti][0:2, 0:256], bass.AP(I["b_ada"].tensor, l * 9 * D + ct * 256, [[0, 2], [1, 256]]),
                     writes=[self.TF[ti]])
            b = self.bank()
            for kc in range(KC):
                self.mm(self.ps[b][0:2, 0:256], self.scT[:, kc, :], stg[:, kc, :], kc == 0, kc == KC - 1,
                        reads=[self.SCT] + HB, bankbuf=self.PS[b])
            self.tt("dve", self.tmpf[ti][0:2, 0:256], self.ps[b][0:2, 0:256], self.tmpf[ti][0:2, 0:256], ALU.add,
                    reads=[self.PS[b], self.TF[ti]], writes=[self.TF[ti]])
            b2 = self.bank()
            for i in range(2):
                self.tr(self.ps[b2][:, 2 * i:2 * i + 2], self.tmpf[ti][0:2, i * 128:(i + 1) * 128], self.identF[0:2, 0:2],
                        reads=[self.TF[ti], self.CONST], bankbuf=self.PS[b2])
            self.cp("dve", self.col(l, C_MOD + ct * 4, 4), self.ps[b2][:, 0:4], reads=[self.PS[b2]], writes=ML)
        for i in range(3):
            sc = self.col(l, C_MOD + (3 * i + 1) * 16, 16)
            a = self.col(l, C_A + i * 16, 16)
            self.ts("dve", a, sc, 1.0, None, ALU.add, None, reads=ML, writes=ML)
            g = self.col(l, C_G + 8 * i, 8)
            self.tt("dve", a.rearrange("p (c t) -> p c t", t=2), a.rearrange("p (c t) -> p c t", t=2),
                    g.rearrange("p (c o) -> p c o", o=1).broadcast_to([128, 8, 2]), ALU.mult, reads=CL + ML, writes=ML)
            gt = self.col(l, C_GT + i * 16, 16)
            self.ts("dve", gt, self.col(l, C_MOD + (3 * i + 2) * 16, 16), 1.0 if i == 1 else 0.5, None, ALU.mult, None,
                    reads=ML, writes=ML)

    def convert(self, l, which):
        I, S = self.I, self.S

        def kslabs(key, w, ncol, kc, nslab):
            wv = w.rearrange("(kc p) n -> p kc n", p=128)
            for s in range(nslab):
                self.dma("pool", S[key][s].rearrange("p (a b) -> p a b", a=kc), wv[:, :, s * ncol:(s + 1) * ncol],
                         writes=[self.Dw[key]], sem=self.csem[key], throttle=False)

        def dslabs(key, w):
            wv = w.rearrange("(j p) n -> p j n", p=128)
            for mc in range(8):
                for hf in range(2):
                    self.dma("pool", S[key][mc * 2 + hf].rearrange("p (a b) -> p a b", a=11),
                             wv[:, hf * 11:(hf + 1) * 11, mc * 128:(mc + 1) * 128],
                             writes=[self.Dw[key]], sem=self.csem[key], throttle=False)

        if which == "p1":
            kslabs(f"g1_{l}", I["w_ff1_gate"][l], 256, 8, 11)
            kslabs(f"u1_{l}", I["w_ff1_up"][l], 256, 8, 11)
            dslabs(f"d1_{l}", I["w_ff1_down"][l])
            kslabs(f"in_{l}", I["w_in"][l], 256, 8, 28)
        else:
            kslabs(f"oa_{l}", I["w_a_out"][l], 256, 4, 4)
            kslabs(f"ob_{l}", I["w_b_out"][l], 256, 4, 4)
            kslabs(f"oc_{l}", I["w_c_out"][l], 256, 4, 4)
            kslabs(f"mg_{l}", I["w_merge"][l], 256, 8, 4)
            kslabs(f"g2_{l}", I["w_ff2_gate"][l], 256, 8, 11)
            kslabs(f"u2_{l}", I["w_ff2_up"][l], 256, 8, 11)
            dslabs(f"d2_{l}", I["w_ff2_down"][l])

    def convert_p1(self, l):
        self.convert(l, "p1")

    def convert_p2(self, l):
        self.convert(l, "p2")

    def table_tasks(self, l):
        I, S = self.I, self.S
        CL = [self.COLS]
        NP_ = 1472
        kcf = self.kc[:].rearrange("p c t -> p (c t)")
        KCB = [self.KCb] + self.KC4
        rpt = S[f"rp_{l}"].tensor

        def neg_view(i_):
            if i_ < 4:
                return self.sA[:, i_, :], self.SA[i_]
            if i_ < 8:
                return self.tB[:, i_ - 4, :], self.TB[i_ - 4]
            return self.oT[:, i_ - 8, :], self.OT[i_ - 8]

        def loads(h):
            pb = self.kblk[h % 2][:].rearrange("p c t -> p (c t)")
            for krl in range(4):
                src = bass.AP(rpt, h * NP_ + krl * RP_C, [[1, 32], [1, 1232]])
                self.dma("pool", pb[krl * 32:(krl + 1) * 32, 0:1232], src, reads=[self.Dw[f"rp_{l}"]], writes=self.KB[h % 2])

        def sub0():
            for cc in range(4):
                for hf in range(2):
                    si = cc * 2 + hf
                    for i in range(16):
                        k = hf * 16 + i
                        if k < 31:
                            self.ts("pool", kcf[:, i * 128:(i + 1) * 128], self.identF[:], self.col(l, C_CAW + k * 4 + cc), None,
                                    ALU.mult, None, reads=CL + [self.CONST], writes=KCB)
                        else:
                            self.memset("pool", kcf[:, i * 128:(i + 1) * 128], 0.0, KCB)
                    self.dma("pool", S[f"ca_{l}"][si], kcf, reads=KCB, writes=[self.Dw[f"ca_{l}"]], sem=self.csem[f"ca_{l}"], throttle=False)
            for k in range(3):
                for cc in range(4):
                    i = k * 4 + cc
                    self.ts("pool", kcf[:, i * 128:(i + 1) * 128], self.identF[:], self.col(l, C_CBW + k * 4 + cc), None,
                            ALU.mult, None, reads=CL + [self.CONST], writes=KCB)
            self.dma("pool", S[f"cb_{l}"][0], kcf[:, 0:1536], reads=KCB, writes=[self.Dw[f"cb_{l}"]], sem=self.csem[f"cb_{l}"], throttle=False)
            VB = [self.VCb]
            rpb16 = self.vc[:].rearrange("p b f -> p (b f)")[0:8, 0:NP_]
            self.memset("pool", rpb16, 0.0, VB)
            self.dma("pool", rpb16[:, 0:RP_R * RP_C].rearrange("p (r c) -> p r c", c=RP_C)[:, 4:19, 16:47], I["rpb"][l], reads=(), writes=VB, slow=True)
            self.dma("pool", S[f"rp_{l}"][:, 0:NP_], rpb16, reads=VB, writes=[self.Dw[f"rp_{l}"]], sem=self.csem[f"rp_{l}"], throttle=False)
            self.memset("pool", self.vc[:].rearrange("p b (h e) -> p b h e", e=65)[:, :, :, 64:65], 1.0, VB)
            for i_ in range(len(self.mask_list)):
                v_, b_ = neg_view(i_)
                self.dma("pool", v_, I["c_mask"][i_], writes=[b_])
                self.ts("pool", v_, v_, 30000.0, -30000.0, ALU.mult, ALU.add, reads=[b_], writes=[b_])
            loads(0)

        def sub_h(h):
            def f():
                if h + 1 < 8:
                    loads(h + 1)
                pb = self.kblk[h % 2][:].rearrange("p c t -> p (c t)")
                for cp in range(3):
                    off = [0, -8, -16][cp]
                    base = pb[:, 472 + off:473 + off]
                    srcv = bass.AP(base.tensor, base.offset, [list(base.ap[0]), [252, 4], [-RP_C, 8], [-1, 16]])
                    mi = self.rot("btc", 3)
                    self.cp("pool", self.maskt[:, mi, :].rearrange("p (g a q) -> p g a q", g=4, a=8), srcv, reads=self.KB[h % 2], writes=[self.MK[mi]])
                    for id_ in sorted(set(self.mask_idx[(t_, cp)] for t_ in range(self.nlt))):
                        v_, b_ = neg_view(id_)
                        ki = self.rot("bts", 4)
                        self.tt("pool", self.kc[:, ki, :], self.maskt[:, mi, :], v_, ALU.add, reads=[self.MK[mi], b_, self.KCb], writes=[self.KC4[ki]])
                        self.dma("pool", S[f"bt_{l}"][id_, h], self.kc[:, ki, :], reads=[self.KC4[ki]],
                                 writes=[self.Dw[f"bt_{l}"]], sem=self.csem[f"bt_{l}"], throttle=False)
            return f

        return [sub0] + [sub_h(h) for h in range(8)]

    def norm(self, l, i, xb, ub, tokm):
        b = self.bank()
        for c in range(KC):
            si = self.rot("sq", 2)
            self.act(self.sqb[si][:], self.xt[xb][:, c, :], AF.Square, reads=[self.X[xb][c]], writes=[self.SQ[si]])
            self.mm(self.ps[b][:], self.ones[:], self.sqb[si][:], c == 0, c == KC - 1,
                    reads=[self.CONST, self.SQ[si]], bankbuf=self.PS[b])
        s1 = self.rot("st", 3)
        self.act(self.stat[s1][:], self.ps[b][:], AF.Sqrt, reads=[self.PS[b]], writes=[self.STB[s1]],
                 bias=self.col(0, C_EPS), scale=1.0 / D)
        self.recip(self.stat[s1][:], self.stat[s1][:], reads=[self.STB[s1]], writes=[self.STB[s1]])
        for c in range(KC):
            ti = self.rot("tf", 3)
            self.stt("dve", self.tmpf[ti][:], self.xt[xb][:, c, :], self.col(l, C_A + (i * 8 + c) * 2 + tokm),
                     self.stat[s1][:], ALU.mult, ALU.mult,
                     reads=[self.X[xb][c], self.COLS, self.MODS, self.STB[s1]], writes=[self.TF[ti]])
            self.act(self.ut[ub][:, c, :], self.tmpf[ti][:], AF.Identity, reads=[self.TF[ti], self.COLS, self.MODS],
                     writes=[self.U[ub][c]], bias=self.col(l, C_MOD + (3 * i * 8 + c) * 2 + tokm))

    def ffn(self, l, f, xb, ub, tokm, mid=None):
        gi = 0 if f == "1" else 2
        S = self.S
        kg, ku, kd = f"g{f}_{l}", f"u{f}_{l}", f"d{f}_{l}"
        for s in range(11):
            sg = self.slab(S[kg][s], 2048, self.Dw[kg], (8, 256))
            su = self.slab(S[ku][s], 2048, self.Dw[ku], (8, 256))
            for jj in range(2):
                j = 2 * s + jj
                bg = self.bank()
                for kc in range(KC):
                    self.mm(self.ps[bg][:], sg.ap[:, kc, jj * 128:(jj + 1) * 128], self.ut[ub][:, kc, :], kc == 0, kc == KC - 1,
                            reads=[sg.buf, self.U[ub][kc]], bankbuf=self.PS[bg])
                bu = self.bank()
                for kc in range(KC):
                    self.mm(self.ps[bu][:], su.ap[:, kc, jj * 128:(jj + 1) * 128], self.ut[ub][:, kc, :], kc == 0, kc == KC - 1,
                            reads=[su.buf, self.U[ub][kc]], bankbuf=self.PS[bu])
                ti = self.rot("tf", 3)
                self.act(self.tmpf[ti][:], self.ps[bg][:], AF.Silu, reads=[self.PS[bg]], writes=[self.TF[ti]])
                self.tt("dve", self.hv(j), self.tmpf[ti][:], self.ps[bu][:], ALU.mult,
                        reads=[self.TF[ti], self.PS[bu]], writes=[self.H[j]])
            self.done(sg)
            self.done(su)
        if mid is not None:
            mid()
        for mc in range(8):
            b = self.bank()
            for hf in range(2):
                sd = self.slab(S[kd][mc * 2 + hf], 11 * 128, self.Dw[kd], (11, 128))
                for jj in range(11):
                    j = hf * 11 + jj
                    self.mm(self.ps[b][:], sd.ap[:, jj, :], self.hv(j), j == 0, j == FC - 1,
                            reads=[sd.buf, self.H[j]], bankbuf=self.PS[b])
                self.done(sd)
            self.stt("dve", self.xt[xb][:, mc, :], self.ps[b][:], self.col(l, C_GT + (gi * 8 + mc) * 2 + tokm),
                     self.xt[xb][:, mc, :], ALU.mult, ALU.add,
                     reads=[self.PS[b], self.X[xb][mc], self.COLS, self.MODS], writes=[self.X[xb][mc]])

    def load_x_tokmajor(self, ti, xb):
        I = self.I
        src = I["xp"] if ti < self.nct else I["xs"]
        r0 = (ti if ti < self.nct else ti - self.nct) * T
        for blk in range(4):
            tb = self.rot("tm", 2)
            self.dma("sp", self.tm[tb][:], src[r0 + blk * 128:r0 + (blk + 1) * 128, :], writes=[self.TM[tb]])
            for c0 in (0, 4):
                b = self.bank()
                for c in range(c0, c0 + 4):
                    self.tr(self.ps[b][:, (c - c0) * 128:(c - c0 + 1) * 128], self.tm[tb][:, c * 128:(c + 1) * 128], self.identF[:],
                            reads=[self.TM[tb], self.CONST], bankbuf=self.PS[b])
                self.cp("act" if c0 == 0 else "dve", self.xt[xb][:, c0:c0 + 4, blk * 128:(blk + 1) * 128],
                        self.ps[b][:].rearrange("p (c t) -> p c t", c=4), reads=[self.PS[b]], writes=self.X[xb][c0:c0 + 4])

    def store_y_tokmajor(self, ti, xb):
        O = self.O
        dst = O["yp"] if ti < self.nct else O["ys"]
        r0 = (ti if ti < self.nct else ti - self.nct) * T
        for blk in range(4):
            tb = self.rot("tm", 2)
            for c0 in (0, 4):
                b = self.bank()
                for c in range(c0, c0 + 4):
                    self.tr(self.ps[b][:, (c - c0) * 128:(c - c0 + 1) * 128], self.xt[xb][:, c, blk * 128:(blk + 1) * 128], self.identF[:],
                            reads=[self.X[xb][c], self.CONST], bankbuf=self.PS[b])
                self.cp("act" if c0 == 0 else "dve", self.tm[tb][:, c0 * 128:(c0 + 4) * 128], self.ps[b][:],
                        reads=[self.PS[b]], writes=[self.TM[tb]])
            ev = self.dma("sp", dst[r0 + blk * 128:r0 + (blk + 1) * 128, :], self.tm[tb][:], reads=[self.TM[tb]], writes=[self.Dout])
            self.final_events.append(ev)

    def p1_load(self, l, ti, a):
        S = self.S
        tok0 = ti * T
        if l == 0:
            self.load_x_tokmajor(ti, a)
        else:
            self.dma("sp", self.xt[a][:], S["xres"][:, tok0:tok0 + T].rearrange("(c p) t -> p c t", p=128),
                     reads=[self.Dt["xres"][ti]], writes=self.X[a])

    def p1_compute(self, l, ti, a, prefetch):
        S = self.S
        ctx = ti < self.nct
        tokm = 1 if ctx else 0
        tok0 = ti * T
        xb, ub, ub2 = a, a, 1 - a
        self.norm(l, 0, xb, ub, tokm)
        self.ffn(l, "1", xb, ub, tokm, mid=prefetch)
        self.norm(l, 1, xb, ub2, tokm)
        self.dma("sp", S["u2s"][:, tok0:tok0 + T].rearrange("(c p) t -> p c t", p=128), self.ut[ub2][:],
                 reads=self.U[ub2], writes=[self.Dt["u2s"][ti]])
        self.dma("sp", S["xres"][:, tok0:tok0 + T].rearrange("(c p) t -> p c t", p=128), self.xt[xb][:],
                 reads=self.X[xb], writes=[self.Dt["xres"][ti]])
        self.proj_early(l, ti, ub2)

    def in_slab(self, l, s):
        return self.slab(self.S[f"in_{l}"][s], 2048, self.Dw[f"in_{l}"], (8, 256))

    def proj_chunk(self, sl, jj, ub):
        b = self.bank()
        for kc in range(KC):
            self.mm(self.ps[b][:], sl.ap[:, kc, jj * 128:(jj + 1) * 128], self.ut[ub][:, kc, :], kc == 0, kc == KC - 1,
                    reads=[sl.buf, self.U[ub][kc]], bankbuf=self.PS[b])
        return b

    def xa_view(self, cc, ctx):
        if ctx:
            return self.xa[:, cc, :].rearrange("p (s w) -> p s w", s=2)[:, :, 15:271]
        return self.xa[:, cc, 15:15 + T]

    def ch_view(self, cc, ctx):
        if ctx:
            return self.ch[:, cc, :].rearrange("p (s w) -> p s w", s=2)[:, :, 1:257]
        return self.ch[:, cc, 1:1 + T]

    def v3(self, ap, ctx):
        return ap.rearrange("p (s w) -> p s w", s=2) if ctx else ap

    def proj_early(self, l, ti, ub):
        S, I, O = self.S, self.I, self.O
        ctx = ti < self.nct
        tok0 = ti * T
        for p in range(2):
            sv = self.in_slab(l, p)
            sg = self.in_slab(l, 2 + p)
            for jj in range(2):
                cc = 2 * p + jj
                bv = self.proj_chunk(sv, jj, ub)
                bg = self.proj_chunk(sg, jj, ub)
                ti_ = self.rot("tf", 3)
                self.act(self.tmpf[ti_][:], self.ps[bg][:], AF.Sigmoid, reads=[self.PS[bg]], writes=[self.TF[ti_]])
                self.tt("dve", self.xa_view(cc, ctx), self.v3(self.tmpf[ti_][:], ctx), self.v3(self.ps[bv][:], ctx), ALU.mult,
                        reads=[self.TF[ti_], self.PS[bv]], writes=[self.XA[cc]])
            self.done(sv)
            self.done(sg)
        for cc in range(4):
            self.dma("sp", S["xas"][cc * 128:(cc + 1) * 128, tok0:tok0 + T] if not ctx else
                     S["xas"][cc * 128:(cc + 1) * 128, tok0:tok0 + T].rearrange("p (s w) -> p s w", s=2),
                     self.xa_view(cc, ctx), reads=[self.XA[cc]], writes=[self.Dt["xas"][ti]])
        for p in range(2):
            sb = self.in_slab(l, 4 + p)
            for jj in range(2):
                cc = 2 * p + jj
                bb = self.proj_chunk(sb, jj, ub)
                self.cp("act", self.bg[:, cc, :], self.ps[bb][:], reads=[self.PS[bb]], writes=[self.BG[cc]])
            self.done(sb)
        self.dma("sp", S["bgs"][:, tok0:tok0 + T].rearrange("(c p) t -> p c t", p=128), self.bg[:],
                 reads=self.BG, writes=[self.Dt["bgs"][ti]])
        for p in range(2):
            sc = self.in_slab(l, 6 + p)
            sh = self.in_slab(l, 8 + p)
            for jj in range(2):
                cc = 2 * p + jj
                bc = self.proj_chunk(sc, jj, ub)
                bh = self.proj_chunk(sh, jj, ub)
                ti_ = self.rot("tf", 3)
                self.cp("act", self.tmpf[ti_][:], self.ps[bc][:], reads=[self.PS[bc]], writes=[self.TF[ti_]])
                self.tt("dve", self.ch_view(cc, ctx), self.v3(self.tmpf[ti_][:], ctx), self.v3(self.ps[bh][:], ctx), ALU.mult,
                        reads=[self.TF[ti_], self.PS[bh]], writes=[self.CH[cc]])
            self.done(sc)
            self.done(sh)
        for cc in range(4):
            self.dma("sp", S["chs"][cc * 128:(cc + 1) * 128, tok0:tok0 + T] if not ctx else
                     S["chs"][cc * 128:(cc + 1) * 128, tok0:tok0 + T].rearrange("p (s w) -> p s w", s=2),
                     self.ch_view(cc, ctx), reads=[self.CH[cc]], writes=[self.Dt["chs"][ti]])
        for which, s0, dst, DB, gcol in (("q", 10, self.qt, self.Q, C_GQ), ("k", 12, self.kt, self.K, C_GK)):
            for p in range(2):
                sl = self.in_slab(l, s0 + p)
                for jj in range(2):
                    cc = 2 * p + jj
                    bq = self.proj_chunk(sl, jj, ub)
                    si = self.rot("sq", 2)
                    self.act(self.sqb[si][:], self.ps[bq][:], AF.Square, reads=[self.PS[bq]], writes=[self.SQ[si]])
                    bs = self.bank()
                    self.mm(self.ps[bs][:], self.blk[:], self.sqb[si][:], True, True, reads=[self.CONST, self.SQ[si]], bankbuf=self.PS[bs])
                    s1 = self.rot("st", 3)
                    self.act(self.stat[s1][:], self.ps[bs][:], AF.Sqrt, reads=[self.PS[bs]], writes=[self.STB[s1]],
                             bias=self.col(0, C_EPS), scale=1.0 / HD)
                    self.recip(self.stat[s1][:], self.stat[s1][:], reads=[self.STB[s1]], writes=[self.STB[s1]])
                    self.stt("dve", dst[:, cc, 0:T], self.ps[bq][:], self.col(l, gcol), self.stat[s1][:], ALU.mult, ALU.mult,
                             reads=[self.PS[bq], self.STB[s1], self.COLS], writes=[DB[cc]])
                self.done(sl)
            nm = "qs" if which == "q" else "ks"
            self.dma("sp", S[nm][:, tok0:tok0 + T].rearrange("(c p) t -> p c t", p=128), dst[:, :, 0:T],
                     reads=DB, writes=[self.Dt[nm][ti]])
        s14 = self.in_slab(l, 14)
        s15 = self.in_slab(l, 15)
        for blk in range(4):
            b = self.bank()
            for hf, sl in ((0, s14), (1, s15)):
                for kc in range(KC):
                    self.mm(self.ps[b][:, hf * 256:(hf + 1) * 256], self.ut[ub][:, kc, blk * 128:(blk + 1) * 128], sl.ap[:, kc, :],
                            kc == 0, kc == KC - 1, reads=[sl.buf, self.U[ub][kc]], bankbuf=self.PS[b])
            self.cp("act", self.vt[:, blk, :].rearrange("p (h e) -> p h e", e=65)[:, :, 0:64],
                    self.ps[b][:].rearrange("p (h d) -> p h d", d=64), reads=[self.PS[b]], writes=[self.V[blk]])
            if ctx:
                tb = self.rot("tm", 2)
                self.cp("dve", self.tm[tb][:, 0:512], self.ps[b][:], reads=[self.PS[b]], writes=[self.TM[tb]])
                seq = ti * 2 + blk // 2
                ev = self.dma("sp", O["nv"][seq, l, (blk % 2) * 128:(blk % 2 + 1) * 128, :], self.tm[tb][:, 0:512],
                              reads=[self.TM[tb]], writes=[self.Dout])
                self.final_events.append(ev)
        self.done(s14)
        self.done(s15)
        self.dma("sp", S["vs"][tok0:tok0 + T, :].rearrange("(b p) f -> p b f", p=128), self.vt[:, 0:4, :],
                 reads=self.V[0:4], writes=[self.Dt["vs"][ti]])
        if ctx:
            s12 = self.in_slab(l, 12)
            s13 = self.in_slab(l, 13)
            for blk in range(4):
                b = self.bank()
                for hf, sl in ((0, s12), (1, s13)):
                    for kc in range(KC):
                        self.mm(self.ps[b][:, hf * 256:(hf + 1) * 256], self.ut[ub][:, kc, blk * 128:(blk + 1) * 128], sl.ap[:, kc, :],
                                kc == 0, kc == KC - 1, reads=[sl.buf, self.U[ub][kc]], bankbuf=self.PS[b])
                ti_ = self.rot("tf", 3)
                self.act(self.tmpf[ti_][:], self.ps[b][:], AF.Square, reads=[self.PS[b]], writes=[self.TF[ti_]])
                sm = self.rot("sm", 4)
                ss = self.small[:, sm * 16:sm * 16 + 8]
                self.op("dve", lambda e, o=ss, i=self.tmpf[ti_][:].rearrange("p (h d) -> p h d", d=64):
                        e.tensor_reduce(out=o, in_=i, axis=AX.X, op=ALU.add), reads=[self.TF[ti_]], writes=[self.SM[sm]])
                self.act(ss, ss, AF.Sqrt, reads=[self.SM[sm]], writes=[self.SM[sm]], bias=self.col(0, C_EPS), scale=1.0 / HD)
                self.recip(ss, ss, reads=[self.SM[sm]], writes=[self.SM[sm]])
                self.tt("dve", self.tmpf[ti_][:].rearrange("p (h d) -> p h d", d=64), self.ps[b][:].rearrange("p (h d) -> p h d", d=64),
                        ss.rearrange("p (h o) -> p h o", o=1).broadcast_to([128, 8, 64]), ALU.mult,
                        reads=[self.PS[b], self.SM[sm]], writes=[self.TF[ti_]])
                tb = self.rot("tm", 2)
                self.tt("dve", self.tm[tb][:, 0:512], self.tmpf[ti_][:], self.gkrow[:, l, :], ALU.mult,
                        reads=[self.TF[ti_], self.COLS], writes=[self.TM[tb]])
                seq = ti * 2 + blk // 2
                ev = self.dma("sp", O["nk"][seq, l, (blk % 2) * 128:(blk % 2 + 1) * 128, :], self.tm[tb][:, 0:512],
                              reads=[self.TM[tb]], writes=[self.Dout])
                self.final_events.append(ev)
            self.done(s12)
            self.done(s13)
    def p2_load(self, l, ti, a):
        S, I = self.S, self.I
        ctx = ti < self.nct
        tokm = 1 if ctx else 0
        tok0 = ti * T
        lt = ti - self.nct
        xb, ub = a, a
        fm = lambda nm: S[nm][:, tok0:tok0 + T].rearrange("(c p) t -> p c t", p=128)
        self.dma("sp", self.xt[xb][:], fm("xres"), reads=[self.Dt["xres"][ti]], writes=self.X[xb])
        self.dma("sp", self.ut[ub][:], fm("u2s"), reads=[self.Dt["u2s"][ti]], writes=self.U[ub])
        if ctx:
            for cc in range(4):
                self.dma("sp", self.xa_view(cc, True), S["xas"][cc * 128:(cc + 1) * 128, tok0:tok0 + T].rearrange("p (s w) -> p s w", s=2),
                         reads=[self.Dt["xas"][ti]], writes=[self.XA[cc]])
                self.dma("sp", self.ch_view(cc, True), S["chs"][cc * 128:(cc + 1) * 128, tok0:tok0 + T].rearrange("p (s w) -> p s w", s=2),
                         reads=[self.Dt["chs"][ti]], writes=[self.CH[cc]])
                xv = self.xa[:, cc, :].rearrange("p (s w) -> p s w", s=2)
                self.memset("pool", xv[:, :, 0:15], 0.0, [self.XA[cc]])
                self.memset("pool", xv[:, :, 271:286], 0.0, [self.XA[cc]])
                cv = self.ch[:, cc, :].rearrange("p (s w) -> p s w", s=2)
                self.memset("pool", cv[:, :, 0:1], 0.0, [self.CH[cc]])
                self.memset("pool", cv[:, :, 257:258], 0.0, [self.CH[cc]])
        else:
            first, last = lt == 0, lt == self.nlt - 1
            for nm, buf, BB, pad in (("xas", self.xa, self.XA, 15), ("chs", self.ch, self.CH, 1)):
                lo = tok0 - (0 if first else pad)
                hi = tok0 + T + (0 if last else pad)
                o0 = pad if first else 0
                rd = [self.Dt[nm][ti]] + ([] if first else [self.Dt[nm][ti - 1]]) + ([] if last else [self.Dt[nm][ti + 1]])
                self.dma("sp", buf[:, :, o0:o0 + hi - lo], S[nm][:, lo:hi].rearrange("(c p) t -> p c t", p=128), reads=rd, writes=BB)
                if first:
                    self.memset("pool", buf[:, :, 0:pad], 0.0, BB)
                if last:
                    self.memset("pool", buf[:, :, pad + T:pad + T + pad], 0.0, BB)
        self.dma("sp", self.bg[:], fm("bgs"), reads=[self.Dt["bgs"][ti]], writes=self.BG)
        self.dma("sp", self.qt[:], fm("qs"), reads=[self.Dt["qs"][ti]], writes=self.Q)
        if ctx:
            self.dma("sp", self.kt[:, :, 0:T], fm("ks"), reads=[self.Dt["ks"][ti]], writes=self.K)
            self.dma("sp", self.vt[:, 0:4, :], S["vs"][tok0:tok0 + T, :].rearrange("(b p) f -> p b f", p=128),
                     reads=[self.Dt["vs"][ti]], writes=self.V[0:4])
        else:
            rows = 8 * self.nlt
            w0 = 8 * lt - 4
            r_lo, r_hi = max(w0, 0), min(w0 + 16, rows)
            base = self.nct * T
            rd = [self.Dt["ks"][t2] for t2 in range(max(ti - 1, self.nct), min(ti + 2, self.NT))]
            self.dma("sp", self.kt[:, :, (r_lo - w0) * 64:(r_hi - w0) * 64],
                     S["ks"][:, base + r_lo * 64:base + r_hi * 64].rearrange("(c p) t -> p c t", p=128), reads=rd, writes=self.K)
    def p2_compute(self, l, ti, a, prefetch):
        S = self.S
        ctx = ti < self.nct
        tokm = 1 if ctx else 0
        tok0 = ti * T
        xb, ub = a, a
        if not ctx:
            self.attn_prep(ti, 0)
        ln_finish = self.branch_ab(l, ctx)
        if ctx:
            self.attn_ctx(l)
        else:
            self.attn_lat(l, ti)
        ln_finish()
        if prefetch is not None:
            prefetch()
        self.gates_merge(l, xb, ub, tokm)
        self.norm(l, 2, xb, ub, tokm)
        self.ffn(l, "2", xb, ub, tokm)
        if l == self.NL - 1:
            self.store_y_tokmajor(ti, xb)
        else:
            self.dma("sp", S["xres"][:, tok0:tok0 + T].rearrange("(c p) t -> p c t", p=128), self.xt[xb][:],
                     reads=self.X[xb], writes=[self.Dt["xres"][ti]])

    def conv(self, l, key, buf, BB, segw, ctx, evac):
        S = self.S
        segs = [(s_ * segw, s_ * 256, 256) for s_ in range(2)] if ctx else [(0, 0, T)]
        deferred = None
        if key == "ca":
            for cc in range(4):
                sls = [self.slab(S[f"ca_{l}"][cc * 2 + hf], 2048, self.Dw[f"ca_{l}"], (16, 128)) for hf in range(2)]
                b = self.bank()
                for (oi, oo, n) in segs:
                    for k in range(31):
                        sl = sls[k // 16]
                        self.mm(self.ps[b][:, oo:oo + n], sl.ap[:, k % 16, :], buf[:, cc, oi + k:oi + k + n], k == 0, k == 30,
                                reads=[sl.buf, BB[cc]], bankbuf=self.PS[b])
                for sl in sls:
                    self.done(sl)
                if deferred is not None:
                    deferred()
                deferred = evac(cc, b)
        else:
            sl = self.slab(S[f"cb_{l}"][0], 12 * 128, self.Dw[f"cb_{l}"], (12, 128))
            for cc in range(4):
                b = self.bank()
                for (oi, oo, n) in segs:
                    for k in range(3):
                        self.mm(self.ps[b][:, oo:oo + n], sl.ap[:, k * 4 + cc, :], buf[:, cc, oi + k:oi + k + n], k == 0, k == 2,
                                reads=[sl.buf, BB[cc]], bankbuf=self.PS[b])
                if deferred is not None:
                    deferred()
                deferred = evac(cc, b)
            self.done(sl)
        return deferred

    def branch_ab(self, l, ctx):
        bs1 = self.bank()
        self.hold(bs1)
        bs2 = self.bank()
        self.hold(bs2)

        def evac_a(cc, b):
            self.act(self.ha_f[cc], self.ps[b][:], AF.Identity, reads=[self.PS[b], self.COLS], writes=self.HA[cc],
                     bias=self.col(l, C_CAB + cc))
            si = self.rot("sq", 2)
            self.act(self.sqb[si][:], self.ps[b][:], AF.Square, reads=[self.PS[b], self.COLS], writes=[self.SQ[si]],
                     bias=self.col(l, C_CAB + cc))
            si2 = self.rot("sq", 2)
            self.act(self.sqb[si2][:], self.ps[b][:], AF.Identity, reads=[self.PS[b], self.COLS], writes=[self.SQ[si2]],
                     bias=self.col(l, C_CAB + cc))

            def stats():
                self.mm(self.ps[bs2][:], self.ones[:], self.sqb[si][:], cc == 0, cc == 3, reads=[self.CONST, self.SQ[si]], bankbuf=self.PS[bs2])
                self.mm(self.ps[bs1][:], self.ones[:], self.sqb[si2][:], cc == 0, cc == 3, reads=[self.CONST, self.SQ[si2]], bankbuf=self.PS[bs1])
            return stats

        last_a = self.conv(l, "ca", self.xa, self.XA, 286, ctx, evac_a)

        def evac_b(cc, b):
            self.tt("dve", self.tB[:, cc, :], self.ps[b][:], self.bg[:, cc, :], ALU.mult,
                    reads=[self.PS[b], self.BG[cc]], writes=[self.TB[cc]])
            if cc == 0:
                return last_a
            return None
        self.conv(l, "cb", self.ch, self.CH, 258, ctx, evac_b)
        m = self.rot("st", 3)
        self.op("act", lambda e, o=self.stat[m][:], i=self.ps[bs1][:]: e.mul(out=o, in_=i, mul=1.0 / 512), reads=[self.PS[bs1]], writes=[self.STB[m]])
        q = self.rot("st", 3)
        self.op("act", lambda e, o=self.stat[q][:], i=self.ps[bs2][:]: e.mul(out=o, in_=i, mul=1.0 / 512), reads=[self.PS[bs2]], writes=[self.STB[q]])
        self.unhold(bs1)
        self.unhold(bs2)

        def ln_finish():
            t_ = self.rot("tf", 3)
            self.tt("dve", self.tmpf[t_][:], self.stat[m][:], self.stat[m][:], ALU.mult, reads=[self.STB[m]], writes=[self.TF[t_]])
            self.tt("dve", self.stat[q][:], self.stat[q][:], self.tmpf[t_][:], ALU.subtract, reads=[self.STB[q], self.TF[t_]], writes=[self.STB[q]])
            self.act(self.stat[q][:], self.stat[q][:], AF.Sqrt, reads=[self.STB[q]], writes=[self.STB[q]], bias=self.col(0, C_EPS))
            self.recip(self.stat[q][:], self.stat[q][:], reads=[self.STB[q]], writes=[self.STB[q]])
            for cc in range(4):
                self.tt("dve", self.ha_f[cc], self.ha_f[cc], self.stat[m][:], ALU.subtract, reads=self.HA[cc] + [self.STB[m]], writes=self.HA[cc])
                self.tt("dve", self.ha_f[cc], self.ha_f[cc], self.stat[q][:], ALU.mult, reads=self.HA[cc] + [self.STB[q]], writes=self.HA[cc])
                self.act(self.sA[:, cc, :], self.ha_f[cc], AF.Silu, reads=self.HA[cc] + [self.COLS], writes=[self.SA[cc]],
                         bias=self.col(l, C_LNB + cc), scale=self.col(l, C_LNG + cc))
        return ln_finish

    def o_evac(self, bO, qb, half):
        sm = self.rot("sm", 4)
        rs = self.small[:, sm * 16:sm * 16 + 4]
        pv = self.ps[bO][:, 0:260].rearrange("p (h e) -> p h e", e=65)
        self.recip(rs.rearrange("p (h o) -> p h o", o=1), pv[:, :, 64:65], reads=[self.PS[bO]], writes=[self.SM[sm]])
        self.tt("dve", self.otm[qb][:, half * 256:(half + 1) * 256].rearrange("p (h d) -> p h d", d=64), pv[:, :, 0:64],
                rs.rearrange("p (h o) -> p h o", o=1).broadcast_to([128, 4, 64]), ALU.mult,
                reads=[self.PS[bO], self.SM[sm]], writes=[self.OTM[qb]])

    def o_transpose(self, qb, dst_view):
        b = self.bank()
        pb = self.ps[b][:].bitcast(BF16)
        for cc in range(4):
            self.tr(pb[:, cc * 128:(cc + 1) * 128], self.otm[qb][:, cc * 128:(cc + 1) * 128], self.identB[:],
                    reads=[self.OTM[qb], self.CONST], bankbuf=self.PS[b])
        self.cp("act", dst_view, pb[:, 0:512].rearrange("p (c q) -> p c q", c=4) if dst_view.ndim == 3 else
                pb[:, 0:512].rearrange("p (c a w) -> p c a w", c=4, a=8), reads=[self.PS[b]], writes=self.OT)

    def attn_ctx(self, l):
        for s in range(2):
            for half in range(2):
                bO = [self.bank(), self.bank()]
                for b_ in bO:
                    self.hold(b_)
                for hh in range(4):
                    h = half * 4 + hh
                    cc, p0 = h // 2, 64 * (h % 2)
                    bS = self.bank()
                    for kb in range(2):
                        self.mm(self.ps[bS][:, kb * 256:(kb + 1) * 256], self.kt[p0:p0 + 64, cc, s * 256 + kb * 128:s * 256 + (kb + 1) * 128],
                                self.qt[p0:p0 + 64, cc, s * 256:(s + 1) * 256], True, True,
                                reads=[self.K[cc], self.Q[cc]], bankbuf=self.PS[bS])
                    pi = self.rot("pc", 3)
                    self.act(self.pc[pi], self.ps[bS][:], AF.Exp, reads=[self.PS[bS]], writes=[self.PC[pi]])
                    for qb in range(2):
                        for kb in range(2):
                            self.mm(self.ps[bO[qb]][:, hh * 65:(hh + 1) * 65], self.pc[pi][:, kb * 256 + qb * 128:kb * 256 + (qb + 1) * 128],
                                    self.vt[:, s * 2 + kb, h * 65:(h + 1) * 65], kb == 0, kb == 1,
                                    reads=[self.PC[pi], self.V[s * 2 + kb]], bankbuf=self.PS[bO[qb]])
                for qb in range(2):
                    self.o_evac(bO[qb], s * 2 + qb, half)
                    self.unhold(bO[qb])
            for qb in range(2):
                q4 = s * 2 + qb
                self.o_transpose(q4, self.oT[:, :, q4 * 128:(q4 + 1) * 128])

    def attn_prep(self, ti, j):
        S = self.S
        lt = ti - self.nct
        rows = 8 * self.nlt
        w0 = 8 * lt - 4
        base = self.nct * T
        bs = BLK_START[j]
        kb_i = j % 2
        for gi in range(4):
            vb = kb_i * 4 + gi
            for krl in range(4):
                r = w0 + 4 * gi + krl
                if 0 <= r < rows:
                    t0 = base + r * 64 + bs
                    rd = [self.Dt["vs"][t0 // T]]
                    self.dma("sp", self.vt[krl * 32:(krl + 1) * 32, vb, :], S["vs"][t0:t0 + 32, :], reads=rd, writes=[self.V[vb]])
        for cc in range(4):
            self.cp("pool", self.kblk[kb_i][:, cc, :].rearrange("p (r w) -> p r w", w=32),
                    self.kt[:, cc, :].rearrange("p (r w) -> p r w", w=64)[:, :, bs:bs + 32],
                    reads=[self.K[cc]], writes=[self.KB[kb_i][cc]])

    def attn_lat(self, l, ti):
        S, I = self.S, self.I
        lt = ti - self.nct
        rows = 8 * self.nlt
        w0 = 8 * lt - 4
        base = self.nct * T
        if lt == 0:
            self.load_cache(l)
        for j in range(4):
            cp = CP_OF_J[j]
            bs = BLK_START[j]
            kb_i = j % 2
            if j + 1 < 4:
                self.attn_prep(ti, j + 1)
            slabs = [self.slab(S[f"bt_{l}"][self.mask_idx[(lt, cp)], hq * 4:(hq + 1) * 4].rearrange("h p k -> p h k"), 2048, self.Dw[f"bt_{l}"], (4, 512))
                     for hq in range(2)]
            bOs = {}

            def scores(h):
                half, hh = divmod(h, 4)
                cc, p0 = h // 2, 64 * (h % 2)
                qv = self.qt[p0:p0 + 64, cc, :].rearrange("p (a w) -> p a w", w=64)[:, :, 16 * j:16 * j + 16]
                bS = self.bank()
                for gi in range(4):
                    self.mm(self.ps[bS][:, gi * 128:(gi + 1) * 128], self.kblk[kb_i][p0:p0 + 64, cc, gi * 128:(gi + 1) * 128], qv,
                            True, True, reads=[self.KB[kb_i][cc], self.Q[cc]], bankbuf=self.PS[bS])
                bC = self.bank()
                for cb in range(4):
                    self.mm(self.ps[bC][:, cb * 128:(cb + 1) * 128], self.kc[p0:p0 + 64, cc, cb * 128:(cb + 1) * 128], qv,
                            True, True, reads=[self.KCb, self.Q[cc]], bankbuf=self.PS[bC])
                pi = self.rot("pl", 3)
                tfi = self.rot("tf", 3)
                self.tt("dve", self.tmpf[tfi][:], self.ps[bS][:], slabs[half].ap[:, hh, :], ALU.add,
                        reads=[self.PS[bS], slabs[half].buf], writes=[self.TF[tfi]])
                ci = self.rot("pc", 3)
                self.act(self.pc[ci], self.ps[bC][:], AF.Exp, reads=[self.PS[bC]], writes=[self.PC[ci]])
                self.act(self.pl[pi], self.tmpf[tfi][:], AF.Exp, reads=[self.TF[tfi]], writes=[self.PL[pi]])
                return (h, pi, ci)

            def pv(st_):
                h, pi, ci = st_
                half, hh = divmod(h, 4)
                bO = bOs[half]
                oview = self.ps[bO][:, hh * 65:(hh + 1) * 65]
                for gi in range(4):
                    self.mm(oview, self.pl[pi][:, gi * 128:(gi + 1) * 128], self.vt[:, kb_i * 4 + gi, h * 65:(h + 1) * 65],
                            gi == 0, False, reads=[self.PL[pi], self.V[kb_i * 4 + gi]], bankbuf=self.PS[bO])
                for cb in range(4):
                    self.mm(oview, self.pc[ci][:, cb * 128:(cb + 1) * 128], self.vc[:, cb, h * 65:(h + 1) * 65],
                            False, cb == 3, reads=[self.PC[ci], self.VCb], bankbuf=self.PS[bO])
                if hh == 3:
                    self.o_evac(bO, j, half)
                    self.unhold(bO)

            pending = None
            for h in range(8):
                if h % 4 == 0:
                    bOs[h // 4] = self.bank()
                    self.hold(bOs[h // 4])
                cur = scores(h)
                if pending is not None:
                    pv(pending)
                pending = cur
            pv(pending)
            for sl in slabs:
                self.done(sl)
            self.o_transpose(j, self.oT[:, :, :].rearrange("p c (a w) -> p c a w", w=64)[:, :, :, 16 * j:16 * j + 16])

    def load_cache(self, l):
        I = self.I
        for kb in range(4):
            self.dma("pool", self.vc[:, kb, :].rearrange("p (h e) -> p h e", e=65)[:, :, 0:64],
                     I["cv"][l, kb * 128:(kb + 1) * 128, :].rearrange("p (h d) -> p h d", d=64), writes=[self.VCb], slow=True)
        for kb in range(4):
            tb = self.rot("tm", 2)
            self.dma("sp", self.tm[tb][:, 0:512], I["ck"][l, kb * 128:(kb + 1) * 128, :], writes=[self.TM[tb]])
            b = self.bank()
            for cc in range(4):
                self.tr(self.ps[b][:, cc * 128:(cc + 1) * 128], self.tm[tb][:, cc * 128:(cc + 1) * 128], self.identF[:],
                        reads=[self.TM[tb], self.CONST], bankbuf=self.PS[b])
            self.cp("dve", self.kc[:, :, kb * 128:(kb + 1) * 128], self.ps[b][:].rearrange("p (c t) -> p c t", c=4),
                    reads=[self.PS[b]], writes=[self.KCb] + self.KC4)

    def gates_merge(self, l, xb, ub, tokm):
        S = self.S
        for p in range(4):
            sl = {}
            for bi, br in enumerate("abc"):
                sl["g" + br] = self.in_slab(l, 16 + 4 * bi + p)
                sl["o" + br] = self.slab(S[f"o{br}_{l}"][p], 1024, self.Dw[f"o{br}_{l}"], (4, 256))
            for jj in range(2):
                mc = 2 * p + jj
                mi = self.rot("ma", 2)
                for bi, (br, src, SB) in enumerate((("a", self.sA, self.SA), ("b", self.tB, self.TB), ("c", self.oT, self.OT))):
                    bg = self.proj_chunk(sl["g" + br], jj, ub)
                    by = self.bank()
                    for kc in range(4):
                        self.mm(self.ps[by][:], sl["o" + br].ap[:, kc, jj * 128:(jj + 1) * 128], src[:, kc, :], kc == 0, kc == 3,
                                reads=[sl["o" + br].buf, SB[kc]], bankbuf=self.PS[by])
                    ti_ = self.rot("tf", 3)
                    self.act(self.tmpf[ti_][:], self.ps[bg][:], AF.Sigmoid, reads=[self.PS[bg]], writes=[self.TF[ti_]])
                    if bi == 0:
                        self.tt("dve", self.macc[mi], self.tmpf[ti_][:], self.ps[by][:], ALU.mult,
                                reads=[self.TF[ti_], self.PS[by]], writes=self.MA[mi])
                    else:
                        self.tt("dve", self.tmpf[ti_][:], self.tmpf[ti_][:], self.ps[by][:], ALU.mult,
                                reads=[self.TF[ti_], self.PS[by]], writes=[self.TF[ti_]])
                        if bi == 1:
                            self.tt("pool", self.macc[mi], self.macc[mi], self.tmpf[ti_][:], ALU.add,
                                    reads=self.MA[mi] + [self.TF[ti_]], writes=self.MA[mi])
                        else:
                            self.tt("pool", self.hv(mc), self.macc[mi], self.tmpf[ti_][:], ALU.add,
                                    reads=self.MA[mi] + [self.TF[ti_]], writes=[self.H[mc]])
            for s_ in sl.values():
                self.done(s_)
        for p in range(4):
            sm = self.slab(S[f"mg_{l}"][p], 2048, self.Dw[f"mg_{l}"], (8, 256))
            for jj in range(2):
                mc2 = 2 * p + jj
                b = self.bank()
                for mc in range(8):
                    self.mm(self.ps[b][:], sm.ap[:, mc, jj * 128:(jj + 1) * 128], self.hv(mc), mc == 0, mc == 7,
                            reads=[sm.buf, self.H[mc]], bankbuf=self.PS[b])
                self.stt("dve", self.xt[xb][:, mc2, :], self.ps[b][:], self.col(l, C_GT + (8 + mc2) * 2 + tokm),
                         self.xt[xb][:, mc2, :], ALU.mult, ALU.add,
                         reads=[self.PS[b], self.X[xb][mc2], self.COLS, self.MODS], writes=[self.X[xb][mc2]])
            self.done(sm)
    def emit_all(self):
        NL = self.NL
        self.reset_state()
        self.setup()
        self.memset("dve", self.col(0, C_EPS), EPS, [self.MODS])
        self.convert_p1(0)
        self.ada(0)
        self.COLS.const = True
        self.CONST.const = True
        self.SCT.const = True
        tasks_by_layer = {0: self.table_tasks(0) + [lambda: self.convert_p2(0)]}
        for l in range(1, NL):
            def ada_l(ll=l):
                self.MODS.const = False
                self.ada(ll)
                self.MODS.const = True
            tasks_by_layer[0] += [ada_l, (lambda ll=l: self.convert_p1(ll)), (lambda ll=l: self.convert_p2(ll))]
            tasks_by_layer[l] = self.table_tasks(l)

        self.MODS.const = True
        steps = []
        for l in range(NL):
            steps += [("p1", l, ti) for ti in range(self.NT)]
            steps += [("p2", l, ti) for ti in range(self.NT)]
        loaded = [-1]

        def load_step(k):
            ph, l_, ti_ = steps[k]
            loaded[0] = k
            (self.p1_load if ph == "p1" else self.p2_load)(l_, ti_, k % 2)

        for k, (ph, l, ti) in enumerate(steps):
            if loaded[0] < k:
                load_step(k)
            nxt = steps[k + 1] if k + 1 < len(steps) else None
            pf = None
            if nxt is not None and not (ph == "p1" and nxt[0] == "p2"):
                pf = (lambda kk=k + 1: load_step(kk))
            if ph == "p1":
                self.p1_compute(l, ti, k % 2, pf)
                tl = tasks_by_layer.get(l, [])
                per_slot = -(-(len(tl) + ti) // self.NT) if tl else 0
                for _ in range(max(per_slot, 2)):
                    if tl:
                        tl.pop(0)()
                if ti == self.NT - 1:
                    while tl:
                        tl.pop(0)()
            else:
                self.p2_compute(l, ti, k % 2, pf)
        if not self.dry:
            E = self.eng["pool"]
            deps = {}
            for ev in self.final_events:
                if ev is not None and deps.get(ev[0], 0) < ev[1]:
                    deps[ev[0]] = ev[1]
            waits = self._prune(E, deps)
            E.count += 1
            E.ops.append((waits, lambda e: e.memset(self.small[:, 60:64], 0.0), (E.sem, 1)))

    def replay(self):
        nc = self.nc
        sems = self.sem_handles

        def run(e, ops):
            for waits, fn, inc in ops:
                for s, v in waits:
                    e.wait_ge(sems[s], v)
                ins = fn(e)
                ins.then_inc(sems[inc[0]], inc[1])

        with nc.Block() as block:
            @block.tensor
            def _(e):
                run(e, self.eng["pe"].ops)

            @block.scalar
            def _(e):
                run(e, self.eng["act"].ops)

            @block.vector
            def _(e):
                run(e, self.eng["dve"].ops)

            @block.gpsimd
            def _(e):
                run(e, self.eng["pool"].ops)

            @block.sync
            def _(e):
                run(e, self.eng["sp"].ops)


def build_program(NL, nct, nlt):
    nc = bass.Bass("TRN2", target_bir_lowering=False)
    P = Prog(nc, NL, nct, nlt)
    P.declare()
    with contextlib.ExitStack() as stack:
        P.alloc(stack)
        P.dry = True
        P.emit_all()
        P.dry = False
        P.final_events = []
        P.emit_all()
        P.replay()
    return nc, P


def host_consts(P):
    ident = np.eye(128, dtype=np.float32)
    anti = np.ascontiguousarray(ident[::-1])
    ones = np.ones((128, 128), np.float32)
    blk = np.zeros((128, 128), np.float32)
    blk[:64, :64] = 1.0
    blk[64:, 64:] = 1.0
    return {"c_ident": ident, "c_anti": anti, "c_ones": ones, "c_blk": blk,
            "c_mask": np.stack(P.mask_list).astype(np.float32)}


_CACHE = {}


def run_cores(inputs, NL, nct, nlt, ncores):
    key = (NL, nct, nlt)
    if key not in _CACHE:
        _CACHE[key] = build_program(NL, nct, nlt)
    nc, P = _CACHE[key]
    cst = host_consts(P)
    f = lambda a: np.ascontiguousarray(np.asarray(a, dtype=np.float32))
    nseq = 2 * nct
    in_maps = []
    for c in range(ncores):
        m = dict(cst)
        m["xp"] = f(inputs["x_prompt"][c * nseq:(c + 1) * nseq]).reshape(nseq * 256, D)
        m["xs"] = f(inputs["x_sample"][c]).reshape(-1, D)
        m["ck"] = f(inputs["cache_k"][c]).reshape(NL, PAST, 512)
        m["cv"] = f(inputs["cache_v"][c]).reshape(NL, PAST, 512)
        m["cvec"] = np.stack([f(inputs["c"][c]), f(inputs["c_ctx"])])
        for nm, _ in WEIGHT_SHAPES:
            m[nm] = f(inputs[nm])
        in_maps.append(m)
    res = run_bass_kernel_spmd(nc, in_maps, core_ids=list(range(ncores)))
    outs = res.results
    yp = np.concatenate([o["yp"].reshape(nseq, 256, D) for o in outs], axis=0)
    ys = np.stack([o["ys"] for o in outs], axis=0)
    nk = np.concatenate([o["nk"].reshape(nseq, NL, 256, NH, HD) for o in outs], axis=0)
    nv = np.concatenate([o["nv"].reshape(nseq, NL, 256, NH, HD) for o in outs], axis=0)
    return (yp.astype(np.float32), ys.astype(np.float32), nk.astype(np.float32), nv.astype(np.float32))


def kernel(**inputs):
    return run_cores(inputs, 2, 2, 8, 8)
```

```python
import contextlib
import numpy as np
import concourse.bass as bass
import concourse.mybir as mybir
from concourse.bass_utils import run_bass_kernel_spmd

F32 = mybir.dt.float32
BF16 = mybir.dt.bfloat16
AF = mybir.ActivationFunctionType
ALU = mybir.AluOpType
AX = mybir.AxisListType

D = 1024
KC = 8
DFF = 2816
FC = 22
NIN = 7168
T = 512
NH = 8
HD = 64
PAST = 512
EPS = 1e-6
NSLOT = 9
SLOT = 2048
BLK_START = [0, 8, 24, 32]
CP_OF_J = [0, 1, 1, 2]
CP_OFF = [16, 8, 0]
RP_R, RP_C = 23, 63
NCOL = 420
C_G = 0
C_CAW = 24
C_CAB = 148
C_LNG = 152
C_LNB = 156
C_CBW = 160
C_GQ = 172
C_GK = 173
C_MOD = 176
C_A = 320
C_GT = 368
C_EPS = 416
WEIGHT_SHAPES = [("w_ada", [D, 9 * D]), ("b_ada", [9 * D]), ("g_ff1", [D]), ("w_ff1_gate", [D, DFF]), ("w_ff1_up", [D, DFF]),
                 ("w_ff1_down", [DFF, D]), ("g_mix", [D]), ("w_in", [D, NIN]), ("conv_a_w", [31, 512]),
                 ("conv_a_b", [512]), ("ln_a_g", [512]), ("ln_a_b", [512]), ("w_a_out", [512, D]),
                 ("conv_b_w", [3, 512]), ("w_b_out", [512, D]), ("q_norm_g", [64]), ("k_norm_g", [64]),
                 ("rpb", [8, 15, 31]), ("w_c_out", [512, D]), ("w_merge", [D, D]), ("g_ff2", [D]),
                 ("w_ff2_gate", [D, DFF]), ("w_ff2_up", [D, DFF]), ("w_ff2_down", [DFF, D])]


class Buf:
    __slots__ = ("name", "w", "r", "const")

    def __init__(self, name):
        self.name = name
        self.w = None
        self.r = {}
        self.const = False


class Eng:
    def __init__(self, name):
        self.name = name
        self.ops = []
        self.count = 0
        self.sem = None
        self.waited = {}
        self.dcount = 0


class DSem:
    def __init__(self, idx):
        self.idx = idx
        self.val = 0


class Slab:
    __slots__ = ("n", "slot", "ap", "buf")


def mask_for_tile(t, ntl, cp):
    rows = 8 * ntl
    kr_n = min(8, rows)
    m = np.zeros((128, 4, 8, 16), np.float32)
    j = [0, 1, 3][cp]
    bs = BLK_START[j]
    w0 = 8 * t - 4
    for gi in range(4):
        for krl in range(4):
            krow = w0 + 4 * gi + krl
            if krow < 0 or krow >= rows:
                continue
            for a in range(8):
                r = 8 * t + a
                rs = min(max(r - kr_n // 2, 0), rows - kr_n)
                if not (rs <= krow < rs + kr_n):
                    continue
                for kcl in range(32):
                    kc_ = bs + kcl
                    for qc in range(16):
                        q = 16 * j + qc
                        ws = min(max(q - 8, 0), 48)
                        if ws <= kc_ < ws + 16:
                            m[krl * 32 + kcl, gi, a, qc] = 1.0
    return m.reshape(128, 512)


class Prog:
    def __init__(self, nc, NL, n_ctx_tiles, n_lat_tiles):
        self.nc = nc
        self.NL = NL
        self.nct = n_ctx_tiles
        self.nlt = n_lat_tiles
        self.NT = n_ctx_tiles + n_lat_tiles
        self.nseq = 2 * n_ctx_tiles
        self.LT = T * n_lat_tiles
        self.NTOK = T * self.NT
        self.eng = {k: Eng(k) for k in ("pe", "act", "dve", "pool", "sp")}
        self.sem_handles = []
        self.dry = False
        self.requests = []
        self.req_list = []
        self.final_events = []
        pats = {}
        self.mask_idx = {}
        self.mask_list = []
        for t in range(n_lat_tiles):
            for cp in range(3):
                m = mask_for_tile(t, n_lat_tiles, cp)
                key = m.tobytes()
                if key not in pats:
                    pats[key] = len(self.mask_list)
                    self.mask_list.append(m)
                self.mask_idx[(t, cp)] = pats[key]

    def new_sem(self, stack, name):
        h = stack.enter_context(self.nc.semaphore(name))
        self.sem_handles.append(h)
        return len(self.sem_handles) - 1

    def _collect(self, reads, writes):
        deps = {}

        def add(ev):
            if ev is None:
                return
            s, v = ev
            if deps.get(s, 0) < v:
                deps[s] = v

        for b in reads:
            add(b.w)
        for b in writes:
            add(b.w)
            for s, v in b.r.items():
                add((s, v))
        return deps

    def _prune(self, E, deps):
        waits = []
        for s, v in deps.items():
            if E.name == "pe" and s == E.sem:
                continue
            if E.waited.get(s, 0) >= v:
                continue
            E.waited[s] = v
            waits.append((s, v))
        return waits

    def _mark(self, ev, reads, writes):
        s, v = ev
        for b in reads:
            if b.const:
                continue
            if b.r.get(s, 0) < v:
                b.r[s] = v
        for b in writes:
            b.w = ev
            b.r = {}

    def op(self, eng, fn, reads=(), writes=()):
        if self.dry:
            return None
        E = self.eng[eng]
        deps = self._collect(reads, writes)
        waits = self._prune(E, deps)
        E.count += 1
        ev = (E.sem, E.count)
        E.ops.append((waits, fn, (E.sem, 1)))
        self._mark(ev, reads, writes)
        return ev

    def dma(self, q, out, in_, reads=(), writes=(), sem=None, throttle=True, slow=False):
        if self.dry:
            return None
        E = self.eng[q]
        if sem is None:
            pool = self.dsems[q]
            S = pool[E.dcount % len(pool)]
            E.dcount += 1
        else:
            S = sem
        deps = self._collect(reads, writes)
        if throttle and S.val > 0:
            if deps.get(S.idx, 0) < S.val:
                deps[S.idx] = S.val
        waits = self._prune(E, deps)
        S.val += 16
        ev = (S.idx, S.val)
        E.ops.append((waits, (lambda e, o=out, i=in_, sl=slow: e.dma_start(out=o, in_=i, allow_slow_non_contiguous=True) if sl else e.dma_start(out=o, in_=i)), (S.idx, 16)))
        self._mark(ev, reads, writes)
        return ev

    def bank(self):
        st = self.st
        for _ in range(8):
            b = st["bank_rr"]
            st["bank_rr"] = (b + 1) % 8
            if b not in st["held"]:
                return b
        raise RuntimeError("no free psum bank")

    def hold(self, b):
        self.st["held"].add(b)

    def unhold(self, b):
        self.st["held"].discard(b)

    def rot(self, key, n):
        v = self.st.get(key, 0)
        self.st[key] = (v + 1) % n
        return v

    def slab(self, src_ap, nelem, srcbuf, shape):
        st = self.st
        n = st["req_i"]
        st["req_i"] += 1
        s = Slab()
        s.n = n
        s.slot = n % NSLOT
        s.buf = self.R[s.slot]
        base = self.ring[:, s.slot * SLOT: s.slot * SLOT + nelem]
        s.ap = base.rearrange("p (a b) -> p a b", a=shape[0])
        dst = s.ap if src_ap.ndim == 3 else base
        if self.dry:
            self.req_list.append((src_ap, dst, srcbuf))
            return s
        self.pump()
        assert st["loaded"] > n, "slab load not emitted"
        return s

    def pump(self):
        st = self.st
        L = self.req_list
        while st["loaded"] < len(L):
            m = st["loaded"]
            if m >= NSLOT and (m - NSLOT) not in st["released"]:
                break
            src_ap, dst, srcbuf = L[m]
            if srcbuf.w is None and m >= st["req_i"]:
                break
            slot = m % NSLOT
            self.dma("sp", dst, src_ap, reads=[srcbuf], writes=[self.R[slot]])
            st["loaded"] += 1
            st["released"].discard(m - NSLOT)

    def done(self, s):
        if self.dry:
            return
        self.st["released"].add(s.n)
        self.pump()

    def mm(self, out, lhsT, rhs, start, stop, reads, bankbuf):
        return self.op("pe", lambda e, o=out, l=lhsT, r=rhs, a=start, b=stop: e.matmul(o, l, r, start=a, stop=b),
                       reads=reads, writes=[bankbuf])

    def tr(self, out, in_, ident, reads, bankbuf):
        return self.op("pe", lambda e, o=out, i=in_, d=ident: e.transpose(o, i, d), reads=reads, writes=[bankbuf])

    def act(self, out, in_, func, reads, writes, bias=None, scale=None):
        kw = {}
        if bias is not None:
            kw["bias"] = bias
        if scale is not None:
            kw["scale"] = scale
        return self.op("act", lambda e, o=out, i=in_, f=func, k=kw: e.activation(out=o, in_=i, func=f, **k),
                       reads=list(reads) + [self.MODS, self.COLS], writes=writes)

    def tt(self, eng, out, in0, in1, op, reads, writes):
        return self.op(eng, lambda e, o=out, a=in0, b=in1, p=op: e.tensor_tensor(out=o, in0=a, in1=b, op=p),
                       reads=reads, writes=writes)

    def stt(self, eng, out, in0, scalar, in1, op0, op1, reads, writes):
        return self.op(eng, lambda e, o=out, a=in0, s=scalar, b=in1, p0=op0, p1=op1:
                       e.scalar_tensor_tensor(out=o, in0=a, scalar=s, in1=b, op0=p0, op1=p1),
                       reads=reads, writes=writes)

    def ts(self, eng, out, in0, s1, s2, op0, op1, reads, writes):
        if s2 is None:
            return self.op(eng, lambda e, o=out, a=in0, x=s1, p0=op0: e.tensor_scalar(out=o, in0=a, scalar1=x, scalar2=None, op0=p0),
                           reads=reads, writes=writes)
        return self.op(eng, lambda e, o=out, a=in0, x=s1, y=s2, p0=op0, p1=op1:
                       e.tensor_scalar(out=o, in0=a, scalar1=x, scalar2=y, op0=p0, op1=p1),
                       reads=reads, writes=writes)

    def cp(self, eng, out, in_, reads, writes):
        if eng == "act":
            return self.op("act", lambda e, o=out, i=in_: e.copy(out=o, in_=i), reads=reads, writes=writes)
        return self.op(eng, lambda e, o=out, i=in_: e.tensor_copy(out=o, in_=i), reads=reads, writes=writes)

    def recip(self, out, in_, reads, writes):
        return self.op("dve", lambda e, o=out, i=in_: e.reciprocal(out=o, in_=i), reads=reads, writes=writes)

    def memset(self, eng, ap, val, writes):
        return self.op(eng, lambda e, a=ap, v=val: e.memset(a, v), reads=(), writes=writes)

    def declare(self):
        nc = self.nc
        NL = self.NL

        def din(name, shape):
            return nc.dram_tensor(name, list(shape), F32, kind="ExternalInput").ap()

        def dout(name, shape):
            return nc.dram_tensor(name, list(shape), F32, kind="ExternalOutput").ap()

        def scr(name, shape, dt=BF16):
            return nc.dram_tensor(name, list(shape), dt, kind="Internal").ap()

        I = {}
        I["xp"] = din("xp", [self.nct * T, D])
        I["xs"] = din("xs", [self.LT, D])
        I["ck"] = din("ck", [NL, PAST, 512])
        I["cv"] = din("cv", [NL, PAST, 512])
        I["cvec"] = din("cvec", [2, D])
        for nm, sh in WEIGHT_SHAPES:
            I[nm] = din(nm, [NL] + sh)
        I["c_ident"] = din("c_ident", [128, 128])
        I["c_anti"] = din("c_anti", [128, 128])
        I["c_ones"] = din("c_ones", [128, 128])
        I["c_blk"] = din("c_blk", [128, 128])
        I["c_mask"] = din("c_mask", [len(self.mask_list), 128, 512])
        self.I = I
        O = {}
        O["yp"] = dout("yp", [self.nct * T, D])
        O["ys"] = dout("ys", [self.LT, D])
        O["nk"] = dout("nk", [self.nseq, NL, 256, 512])
        O["nv"] = dout("nv", [self.nseq, NL, 256, 512])
        self.O = O
        S = {}
        for l in range(NL):
            for f in ("1", "2"):
                S[f"g{f}_{l}"] = scr(f"wg{f}_{l}", [11, 128, 2048])
                S[f"u{f}_{l}"] = scr(f"wu{f}_{l}", [11, 128, 2048])
                S[f"d{f}_{l}"] = scr(f"wd{f}_{l}", [16, 128, 11 * 128])
            S[f"in_{l}"] = scr(f"win_{l}", [28, 128, 2048])
            for b in "abc":
                S[f"o{b}_{l}"] = scr(f"wo{b}_{l}", [4, 128, 4 * 256])
            S[f"mg_{l}"] = scr(f"wmg_{l}", [4, 128, 2048])
            S[f"ca_{l}"] = scr(f"wca_{l}", [8, 128, 2048])
            S[f"cb_{l}"] = scr(f"wcb_{l}", [1, 128, 12 * 128])
            S[f"bt_{l}"] = scr(f"wbt_{l}", [len(self.mask_list), 8, 128, 512])
            S[f"rp_{l}"] = scr(f"rp_{l}", [8, 1472])
        S["xres"] = scr("xres", [D, self.NTOK], F32)
        S["u2s"] = scr("u2s", [D, self.NTOK])
        S["xas"] = scr("xas", [512, self.NTOK])
        S["chs"] = scr("chs", [512, self.NTOK])
        S["bgs"] = scr("bgs", [512, self.NTOK])
        S["qs"] = scr("qs", [512, self.NTOK])
        S["ks"] = scr("ks", [512, self.NTOK])
        S["vs"] = scr("vs", [self.NTOK, 520])
        self.S = S
        self.Dw = {k: Buf("D" + k) for k in S}
        self.Dt = {}
        for nm in ("xres", "u2s", "xas", "chs", "bgs", "qs", "ks", "vs"):
            self.Dt[nm] = [Buf(f"D{nm}{i}") for i in range(self.NT)]
        self.Dout = Buf("Dout")

    def alloc(self, stack):
        nc = self.nc
        A = nc.alloc_sbuf_tensor
        self.xt = [A(f"xt{i}", [128, KC, T], F32) for i in range(2)]
        self.X = [[Buf(f"X{i}_{c}") for c in range(KC)] for i in range(2)]
        self.ut = [A(f"ut{i}", [128, KC, T], BF16) for i in range(2)]
        self.U = [[Buf(f"U{i}_{c}") for c in range(KC)] for i in range(2)]
        self.harena = A("harena", [128, FC * T], BF16)
        self.H = [Buf(f"H{j}") for j in range(FC)]
        self.ring = A("ring", [128, NSLOT * SLOT], BF16)
        self.R = [Buf(f"R{i}") for i in range(NSLOT)]
        self.tm = [A(f"tm{i}", [128, D], F32) for i in range(2)]
        self.TM = [Buf(f"TM{i}") for i in range(2)]
        self.sqb = [A(f"sqb{i}", [128, T], BF16) for i in range(2)]
        self.SQ = [Buf(f"SQ{i}") for i in range(2)]
        self.tmpf = [A(f"tmpf{i}", [128, T], F32) for i in range(3)]
        self.TF = [Buf(f"TF{i}") for i in range(3)]
        self.stat = [A(f"stat{i}", [128, T], F32) for i in range(3)]
        self.STB = [Buf(f"ST{i}") for i in range(3)]
        self.xa = A("xa", [128, 4, 572], BF16)
        self.XA = [Buf(f"XA{c}") for c in range(4)]
        self.ch = A("ch", [128, 4, 516], BF16)
        self.CH = [Buf(f"CH{c}") for c in range(4)]
        self.bg = A("bg", [128, 4, T], BF16)
        self.BG = [Buf(f"BG{c}") for c in range(4)]
        self.qt = A("qt", [128, 4, T], BF16)
        self.Q = [Buf(f"Q{c}") for c in range(4)]
        self.kt = A("kt", [128, 4, 2 * T], BF16)
        self.K = [Buf(f"K{c}") for c in range(4)]
        self.kblk = [A(f"kblk{i}", [128, 4, 4 * 128], BF16) for i in range(2)]
        self.KB = [[Buf(f"KB{i}_{c}") for c in range(4)] for i in range(2)]
        self.vt = A("vt", [128, 8, 520], BF16)
        self.Vk = [[Buf(f"V{i}_{k}") for k in range(4)] for i in range(8)]
        self.Vall = [b for v in self.Vk for b in v]
        self.maskt = A("maskt", [128, 3, T], BF16)
        self.MK = [Buf(f"MK{i}") for i in range(3)]
        self.sA = A("sA", [128, 4, T], BF16)
        self.SA = [Buf(f"SA{c}") for c in range(4)]
        self.tB = A("tB", [128, 4, T], BF16)
        self.TB = [Buf(f"TB{c}") for c in range(4)]
        self.oT = A("oT", [128, 4, T], BF16)
        self.OT = [Buf(f"OT{c}") for c in range(4)]
        self.kc = A("kc", [128, 4, PAST], BF16)
        self.KCb = Buf("KC")
        self.vc = A("vc", [128, 4, 520], BF16)
        self.VCb = Buf("VC")
        self.identF = A("identF", [128, 128], F32)
        self.identB = A("identB", [128, 128], BF16)
        self.anti = A("anti", [128, 128], BF16)
        self.ones = A("ones", [128, 128], BF16)
        self.blk = A("blk", [128, 128], BF16)
        self.CONST = Buf("CONST")
        self.cols = A("cols", [128, self.NL * NCOL], F32)
        self.COLS = Buf("COLS")
        self.MODS = Buf("MODS")
        self.gkrow = A("gkrow", [128, self.NL, 512], F32)
        self.small = A("small", [128, 64], F32)
        self.SM = [Buf(f"SM{i}") for i in range(4)]
        self.scT = A("scT", [128, KC, 2], F32)
        self.SCT = Buf("SCT")
        self.ps = [nc.alloc_psum_tensor(f"ps{i}", [128, T], F32) for i in range(8)]
        self.PS = [Buf(f"PS{i}") for i in range(8)]
        ha = self.harena
        self.ha_f = [ha[:, (2 * c) * T:(2 * c + 2) * T].bitcast(F32) for c in range(4)]
        self.HA = [[self.H[2 * c], self.H[2 * c + 1]] for c in range(4)]
        self.otm = [ha[:, (8 + q) * T:(9 + q) * T] for q in range(4)]
        self.OTM = [self.H[8 + q] for q in range(4)]
        self.pl = [ha[:, (12 + i) * T:(13 + i) * T] for i in range(3)]
        self.PL = [self.H[12 + i] for i in range(3)]
        self.pc = [ha[:, (15 + i) * T:(16 + i) * T] for i in range(3)]
        self.PC = [self.H[15 + i] for i in range(3)]
        self.macc = [ha[:, (18 + 2 * i) * T:(20 + 2 * i) * T].bitcast(F32) for i in range(2)]
        self.MA = [[self.H[18 + 2 * i], self.H[19 + 2 * i]] for i in range(2)]
        for k in ("pe", "act", "dve", "pool"):
            self.eng[k].sem = self.new_sem(stack, "c_" + k)
        self.dsems = {"sp": [DSem(self.new_sem(stack, f"dsp{i}")) for i in range(20)],
                      "pool": [DSem(self.new_sem(stack, f"dpl{i}")) for i in range(8)],
                      "act": [DSem(self.new_sem(stack, f"dac{i}")) for i in range(8)]}
        self.KC4 = [Buf(f"KC4_{i}") for i in range(4)]
        self.csem = {}
        for k in self.S:
            if k[0] in "gudiomcbr" and "_" in k:
                self.csem[k] = DSem(self.new_sem(stack, "cv_" + k))

    def hv(self, j):
        return self.harena[:, j * T:(j + 1) * T]

    def col(self, l, idx, n=1):
        b = l * NCOL + idx
        return self.cols[:, b:b + n]

    def reset_state(self):
        self.st = {"bank_rr": 0, "held": set(), "req_i": 0, "pending": [], "loaded": 0, "released": set()}

    def setup(self):
        I, S = self.I, self.S
        NL = self.NL
        C = [self.CONST]
        self.dma("pool", self.identB[:], I["c_ident"], writes=C)
        self.dma("pool", self.anti[:], I["c_anti"], writes=C)
        self.dma("pool", self.ones[:], I["c_ones"], writes=C)
        self.dma("pool", self.blk[:], I["c_blk"], writes=C)
        self.dma("sp", self.identF[:], I["c_ident"], writes=C)
        self.memset("pool", self.kt[:], 0.0, self.K)
        self.memset("pool", self.vt[:], 0.0, self.Vall)
        self.memset("pool", self.vc[:], 0.0, [self.VCb])
        self.memset("pool", self.xa[:], 0.0, self.XA)
        self.memset("pool", self.ch[:], 0.0, self.CH)
        self.memset("dve", self.kblk[0][:], 0.0, self.KB[0])
        self.memset("dve", self.kblk[1][:], 0.0, self.KB[1])
        self.memset("dve", self.vt[:].rearrange("p b (h e) -> p b h e", e=65)[:, :, :, 64:65], 1.0, self.Vall)
        self.memset("dve", self.vc[:].rearrange("p b (h e) -> p b h e", e=65)[:, :, :, 64:65], 1.0, [self.VCb])
        self.memset("dve", self.small[:], 0.0, self.SM)
        CL = [self.COLS]
        for l in range(NL):
            for i, nm in enumerate(("g_ff1", "g_mix", "g_ff2")):
                self.dma("sp", self.col(l, C_G + 8 * i, 8), I[nm][l].rearrange("(c p) -> p c", p=128), writes=CL, slow=True)
            for cc in range(4):
                self.dma("sp", self.col(l, C_CAW, 124).rearrange("p (k c) -> p k c", c=4)[:, :, cc],
                         I["conv_a_w"][l][:, cc * 128:(cc + 1) * 128].rearrange("k p -> p k"), writes=CL, slow=True)
                self.dma("sp", self.col(l, C_CBW, 12).rearrange("p (k c) -> p k c", c=4)[:, :, cc],
                         I["conv_b_w"][l][:, cc * 128:(cc + 1) * 128].rearrange("k p -> p k"), writes=CL, slow=True)
            for idx, nm in ((C_CAB, "conv_a_b"), (C_LNG, "ln_a_g"), (C_LNB, "ln_a_b")):
                self.dma("sp", self.col(l, idx, 4), I[nm][l].rearrange("(c p) -> p c", p=128), writes=CL, slow=True)
            for idx, nm in ((C_GQ, "q_norm_g"), (C_GK, "k_norm_g")):
                for hh in range(2):
                    self.dma("sp", self.cols[hh * 64:(hh + 1) * 64, l * NCOL + idx:l * NCOL + idx + 1],
                             I[nm][l].rearrange("(p o) -> p o", o=1), writes=CL, slow=True)
            self.dma("sp", self.gkrow[:, l, :].rearrange("p (h d) -> p h d", h=8),
                     bass.AP(I["k_norm_g"].tensor, l * 64, [[0, 128], [0, 8], [1, 64]]), writes=CL)
            self.ts("dve", self.col(l, C_GQ), self.col(l, C_GQ), 0.125, None, ALU.mult, None, reads=CL, writes=CL)
        for t_ in range(2):
            self.dma("sp", self.scT[:, :, t_], I["cvec"][t_].rearrange("(c p) -> p c", p=128), writes=[self.SCT], slow=True)
        self.act(self.scT[:], self.scT[:], AF.Silu, reads=[self.SCT], writes=[self.SCT])

    def ada(self, l):
        I = self.I
        CL = [self.COLS]
        ML = [self.MODS]
        wv = I["w_ada"][l].rearrange("(kc p) n -> p kc n", p=128)
        for ct in range(36):
            hb = ct % 2
            stg = self.harena[:, hb * 4096:(hb + 1) * 4096].bitcast(F32).rearrange("p (k n) -> p k n", k=KC)
            HB = self.H[hb * 8:hb * 8 + 8]
            self.dma("sp", stg, wv[:, :, ct * 256:(ct + 1) * 256], writes=HB)
            ti = self.rot("tf", 3)
            self.dma("sp", self.tmpf[ti][0:2, 0:256], bass.AP(I["b_ada"].tensor, l * 9 * D + ct * 256, [[0, 2], [1, 256]]),
                     writes=[self.TF[ti]])
            b = self.bank()
            for kc in range(KC):
                self.mm(self.ps[b][0:2, 0:256], self.scT[:, kc, :], stg[:, kc, :], kc == 0, kc == KC - 1,
                        reads=[self.SCT] + HB, bankbuf=self.PS[b])
            self.tt("dve", self.tmpf[ti][0:2, 0:256], self.ps[b][0:2, 0:256], self.tmpf[ti][0:2, 0:256], ALU.add,
                    reads=[self.PS[b], self.TF[ti]], writes=[self.TF[ti]])
            b2 = self.bank()
            for i in range(2):
                self.tr(self.ps[b2][:, 2 * i:2 * i + 2], self.tmpf[ti][0:2, i * 128:(i + 1) * 128], self.identF[0:2, 0:2],
                        reads=[self.TF[ti], self.CONST], bankbuf=self.PS[b2])
            self.cp("dve", self.col(l, C_MOD + ct * 4, 4), self.ps[b2][:, 0:4], reads=[self.PS[b2]], writes=ML)
        for i in range(3):
            sc = self.col(l, C_MOD + (3 * i + 1) * 16, 16)
            a = self.col(l, C_A + i * 16, 16)
            self.ts("dve", a, sc, 1.0, None, ALU.add, None, reads=ML, writes=ML)
            g = self.col(l, C_G + 8 * i, 8)
            self.tt("dve", a.rearrange("p (c t) -> p c t", t=2), a.rearrange("p (c t) -> p c t", t=2),
                    g.rearrange("p (c o) -> p c o", o=1).broadcast_to([128, 8, 2]), ALU.mult, reads=CL + ML, writes=ML)
            gt = self.col(l, C_GT + i * 16, 16)
            self.ts("dve", gt, self.col(l, C_MOD + (3 * i + 2) * 16, 16), 1.0 if i == 1 else 0.5, None, ALU.mult, None,
                    reads=ML, writes=ML)

    def convert(self, l, which):
        I, S = self.I, self.S

        def kslabs(key, w, ncol, kc, nslab):
            wv = w.rearrange("(kc p) n -> p kc n", p=128)
            for s in range(nslab):
                self.dma("pool", S[key][s].rearrange("p (a b) -> p a b", a=kc), wv[:, :, s * ncol:(s + 1) * ncol],
                         writes=[self.Dw[key]], sem=self.csem[key], throttle=False)

        def dslabs(key, w):
            wv = w.rearrange("(j p) n -> p j n", p=128)
            for mc in range(8):
                for hf in range(2):
                    self.dma("pool", S[key][mc * 2 + hf].rearrange("p (a b) -> p a b", a=11),
                             wv[:, hf * 11:(hf + 1) * 11, mc * 128:(mc + 1) * 128],
                             writes=[self.Dw[key]], sem=self.csem[key], throttle=False)

        if which == "p1":
            kslabs(f"g1_{l}", I["w_ff1_gate"][l], 256, 8, 11)
            kslabs(f"u1_{l}", I["w_ff1_up"][l], 256, 8, 11)
            dslabs(f"d1_{l}", I["w_ff1_down"][l])
            kslabs(f"in_{l}", I["w_in"][l], 256, 8, 28)
        else:
            kslabs(f"oa_{l}", I["w_a_out"][l], 256, 4, 4)
            kslabs(f"ob_{l}", I["w_b_out"][l], 256, 4, 4)
            kslabs(f"oc_{l}", I["w_c_out"][l], 256, 4, 4)
            kslabs(f"mg_{l}", I["w_merge"][l], 256, 8, 4)
            kslabs(f"g2_{l}", I["w_ff2_gate"][l], 256, 8, 11)
            kslabs(f"u2_{l}", I["w_ff2_up"][l], 256, 8, 11)
            dslabs(f"d2_{l}", I["w_ff2_down"][l])

    def convert_p1(self, l):
        self.convert(l, "p1")

    def convert_p2(self, l):
        self.convert(l, "p2")

    def table_tasks(self, l):
        I, S = self.I, self.S
        CL = [self.COLS]
        NP_ = 1472
        kcf = self.kc[:].rearrange("p c t -> p (c t)")
        KCB = [self.KCb] + self.KC4
        rpt = S[f"rp_{l}"].tensor

        def neg_view(i_):
            if i_ < 4:
                return self.sA[:, i_, :], self.SA[i_]
            if i_ < 8:
                return self.tB[:, i_ - 4, :], self.TB[i_ - 4]
            return self.oT[:, i_ - 8, :], self.OT[i_ - 8]

        def loads(h):
            pb = self.kblk[h % 2][:].rearrange("p c t -> p (c t)")
            for krl in range(4):
                src = bass.AP(rpt, h * NP_ + krl * RP_C, [[1, 32], [1, 1232]])
                self.dma("pool", pb[krl * 32:(krl + 1) * 32, 0:1232], src, reads=[self.Dw[f"rp_{l}"]], writes=self.KB[h % 2])

        def sub0():
            for cc in range(4):
                for hf in range(2):
                    si = cc * 2 + hf
                    for i in range(16):
                        k = hf * 16 + i
                        if k < 31:
                            self.ts("pool", kcf[:, i * 128:(i + 1) * 128], self.identF[:], self.col(l, C_CAW + k * 4 + cc), None,
                                    ALU.mult, None, reads=CL + [self.CONST], writes=KCB)
                        else:
                            self.memset("pool", kcf[:, i * 128:(i + 1) * 128], 0.0, KCB)
                    self.dma("pool", S[f"ca_{l}"][si], kcf, reads=KCB, writes=[self.Dw[f"ca_{l}"]], sem=self.csem[f"ca_{l}"], throttle=False)
            for k in range(3):
                for cc in range(4):
                    i = k * 4 + cc
                    self.ts("pool", kcf[:, i * 128:(i + 1) * 128], self.identF[:], self.col(l, C_CBW + k * 4 + cc), None,
                            ALU.mult, None, reads=CL + [self.CONST], writes=KCB)
            self.dma("pool", S[f"cb_{l}"][0], kcf[:, 0:1536], reads=KCB, writes=[self.Dw[f"cb_{l}"]], sem=self.csem[f"cb_{l}"], throttle=False)
            VB = [self.VCb]
            rpb16 = self.vc[:].rearrange("p b f -> p (b f)")[0:8, 0:NP_]
            self.memset("pool", rpb16, 0.0, VB)
            self.dma("pool", rpb16[:, 0:RP_R * RP_C].rearrange("p (r c) -> p r c", c=RP_C)[:, 4:19, 16:47], I["rpb"][l], reads=(), writes=VB, slow=True)
            self.dma("pool", S[f"rp_{l}"][:, 0:NP_], rpb16, reads=VB, writes=[self.Dw[f"rp_{l}"]], sem=self.csem[f"rp_{l}"], throttle=False)
            self.memset("pool", self.vc[:].rearrange("p b (h e) -> p b h e", e=65)[:, :, :, 64:65], 1.0, VB)
            for i_ in range(len(self.mask_list)):
                v_, b_ = neg_view(i_)
                self.dma("pool", v_, I["c_mask"][i_], writes=[b_])
                self.ts("pool", v_, v_, 30000.0, -30000.0, ALU.mult, ALU.add, reads=[b_], writes=[b_])
            loads(0)

        def sub_h(h):
            def f():
                if h + 1 < 8:
                    loads(h + 1)
                pb = self.kblk[h % 2][:].rearrange("p c t -> p (c t)")
                for cp in range(3):
                    off = [0, -8, -16][cp]
                    base = pb[:, 472 + off:473 + off]
                    srcv = bass.AP(base.tensor, base.offset, [list(base.ap[0]), [252, 4], [-RP_C, 8], [-1, 16]])
                    mi = self.rot("btc", 3)
                    self.cp("pool", self.maskt[:, mi, :].rearrange("p (g a q) -> p g a q", g=4, a=8), srcv, reads=self.KB[h % 2], writes=[self.MK[mi]])
                    for id_ in sorted(set(self.mask_idx[(t_, cp)] for t_ in range(self.nlt))):
                        v_, b_ = neg_view(id_)
                        ki = self.rot("bts", 4)
                        self.tt("pool", self.kc[:, ki, :], self.maskt[:, mi, :], v_, ALU.add, reads=[self.MK[mi], b_, self.KCb], writes=[self.KC4[ki]])
                        self.dma("pool", S[f"bt_{l}"][id_, h], self.kc[:, ki, :], reads=[self.KC4[ki]],
                                 writes=[self.Dw[f"bt_{l}"]], sem=self.csem[f"bt_{l}"], throttle=False)
            return f

        return [sub0] + [sub_h(h) for h in range(8)]

    def norm(self, l, i, xb, ub, tokm):
        b = self.bank()
        for c in range(KC):
            si = self.rot("sq", 2)
            self.act(self.sqb[si][:], self.xt[xb][:, c, :], AF.Square, reads=[self.X[xb][c]], writes=[self.SQ[si]])
            self.mm(self.ps[b][:], self.ones[:], self.sqb[si][:], c == 0, c == KC - 1,
                    reads=[self.CONST, self.SQ[si]], bankbuf=self.PS[b])
        s1 = self.rot("st", 3)
        self.act(self.stat[s1][:], self.ps[b][:], AF.Sqrt, reads=[self.PS[b]], writes=[self.STB[s1]],
                 bias=self.col(0, C_EPS), scale=1.0 / D)
        self.recip(self.stat[s1][:], self.stat[s1][:], reads=[self.STB[s1]], writes=[self.STB[s1]])
        for c in range(KC):
            ti = self.rot("tf", 3)
            self.stt("dve", self.tmpf[ti][:], self.xt[xb][:, c, :], self.col(l, C_A + (i * 8 + c) * 2 + tokm),
                     self.stat[s1][:], ALU.mult, ALU.mult,
                     reads=[self.X[xb][c], self.COLS, self.MODS, self.STB[s1]], writes=[self.TF[ti]])
            self.act(self.ut[ub][:, c, :], self.tmpf[ti][:], AF.Identity, reads=[self.TF[ti], self.COLS, self.MODS],
                     writes=[self.U[ub][c]], bias=self.col(l, C_MOD + (3 * i * 8 + c) * 2 + tokm))

    def ffn(self, l, f, xb, ub, tokm, mid=None):
        gi = 0 if f == "1" else 2
        S = self.S
        kg, ku, kd = f"g{f}_{l}", f"u{f}_{l}", f"d{f}_{l}"
        for s in range(11):
            sg = self.slab(S[kg][s], 2048, self.Dw[kg], (8, 256))
            su = self.slab(S[ku][s], 2048, self.Dw[ku], (8, 256))
            for jj in range(2):
                j = 2 * s + jj
                bg = self.bank()
                for kc in range(KC):
                    self.mm(self.ps[bg][:], sg.ap[:, kc, jj * 128:(jj + 1) * 128], self.ut[ub][:, kc, :], kc == 0, kc == KC - 1,
                            reads=[sg.buf, self.U[ub][kc]], bankbuf=self.PS[bg])
                bu = self.bank()
                for kc in range(KC):
                    self.mm(self.ps[bu][:], su.ap[:, kc, jj * 128:(jj + 1) * 128], self.ut[ub][:, kc, :], kc == 0, kc == KC - 1,
                            reads=[su.buf, self.U[ub][kc]], bankbuf=self.PS[bu])
                ti = self.rot("tf", 3)
                self.act(self.tmpf[ti][:], self.ps[bg][:], AF.Silu, reads=[self.PS[bg]], writes=[self.TF[ti]])
                self.tt("dve", self.hv(j), self.tmpf[ti][:], self.ps[bu][:], ALU.mult,
                        reads=[self.TF[ti], self.PS[bu]], writes=[self.H[j]])
            self.done(sg)
            self.done(su)
        if mid is not None:
            mid()
        for mc in range(8):
            b = self.bank()
            for hf in range(2):
                sd = self.slab(S[kd][mc * 2 + hf], 11 * 128, self.Dw[kd], (11, 128))
                for jj in range(11):
                    j = hf * 11 + jj
                    self.mm(self.ps[b][:], sd.ap[:, jj, :], self.hv(j), j == 0, j == FC - 1,
                            reads=[sd.buf, self.H[j]], bankbuf=self.PS[b])
                self.done(sd)
            self.stt("dve", self.xt[xb][:, mc, :], self.ps[b][:], self.col(l, C_GT + (gi * 8 + mc) * 2 + tokm),
                     self.xt[xb][:, mc, :], ALU.mult, ALU.add,
                     reads=[self.PS[b], self.X[xb][mc], self.COLS, self.MODS], writes=[self.X[xb][mc]])

    def load_x_tokmajor(self, ti, xb):
        I = self.I
        src = I["xp"] if ti < self.nct else I["xs"]
        r0 = (ti if ti < self.nct else ti - self.nct) * T
        for blk in range(4):
            tb = self.rot("tm", 2)
            self.dma("sp", self.tm[tb][:], src[r0 + blk * 128:r0 + (blk + 1) * 128, :], writes=[self.TM[tb]])
            for c0 in (0, 4):
                b = self.bank()
                for c in range(c0, c0 + 4):
                    self.tr(self.ps[b][:, (c - c0) * 128:(c - c0 + 1) * 128], self.tm[tb][:, c * 128:(c + 1) * 128], self.identF[:],
                            reads=[self.TM[tb], self.CONST], bankbuf=self.PS[b])
                self.cp("act" if c0 == 0 else "dve", self.xt[xb][:, c0:c0 + 4, blk * 128:(blk + 1) * 128],
                        self.ps[b][:].rearrange("p (c t) -> p c t", c=4), reads=[self.PS[b]], writes=self.X[xb][c0:c0 + 4])

    def store_y_tokmajor(self, ti, xb):
        O = self.O
        dst = O["yp"] if ti < self.nct else O["ys"]
        r0 = (ti if ti < self.nct else ti - self.nct) * T
        for blk in range(4):
            tb = self.rot("tm", 2)
            for c0 in (0, 4):
                b = self.bank()
                for c in range(c0, c0 + 4):
                    self.tr(self.ps[b][:, (c - c0) * 128:(c - c0 + 1) * 128], self.xt[xb][:, c, blk * 128:(blk + 1) * 128], self.identF[:],
                            reads=[self.X[xb][c], self.CONST], bankbuf=self.PS[b])
                self.cp("act" if c0 == 0 else "dve", self.tm[tb][:, c0 * 128:(c0 + 4) * 128], self.ps[b][:],
                        reads=[self.PS[b]], writes=[self.TM[tb]])
            ev = self.dma("sp", dst[r0 + blk * 128:r0 + (blk + 1) * 128, :], self.tm[tb][:], reads=[self.TM[tb]], writes=[self.Dout])
            self.final_events.append(ev)

    def p1_load(self, l, ti, a):
        S = self.S
        tok0 = ti * T
        if l == 0:
            self.load_x_tokmajor(ti, a)
        else:
            self.dma("sp", self.xt[a][:], S["xres"][:, tok0:tok0 + T].rearrange("(c p) t -> p c t", p=128),
                     reads=[self.Dt["xres"][ti]], writes=self.X[a])

    def p1_compute(self, l, ti, a, prefetch):
        S = self.S
        ctx = ti < self.nct
        tokm = 1 if ctx else 0
        tok0 = ti * T
        xb, ub, ub2 = a, a, 1 - a
        self.norm(l, 0, xb, ub, tokm)
        self.ffn(l, "1", xb, ub, tokm, mid=prefetch)
        self.norm(l, 1, xb, ub2, tokm)
        self.dma("sp", S["u2s"][:, tok0:tok0 + T].rearrange("(c p) t -> p c t", p=128), self.ut[ub2][:],
                 reads=self.U[ub2], writes=[self.Dt["u2s"][ti]])
        self.dma("sp", S["xres"][:, tok0:tok0 + T].rearrange("(c p) t -> p c t", p=128), self.xt[xb][:],
                 reads=self.X[xb], writes=[self.Dt["xres"][ti]])
        self.proj_early(l, ti, ub2)

    def in_slab(self, l, s):
        return self.slab(self.S[f"in_{l}"][s], 2048, self.Dw[f"in_{l}"], (8, 256))

    def proj_chunk(self, sl, jj, ub):
        b = self.bank()
        for kc in range(KC):
            self.mm(self.ps[b][:], sl.ap[:, kc, jj * 128:(jj + 1) * 128], self.ut[ub][:, kc, :], kc == 0, kc == KC - 1,
                    reads=[sl.buf, self.U[ub][kc]], bankbuf=self.PS[b])
        return b

    def xa_view(self, cc, ctx):
        if ctx:
            return self.xa[:, cc, :].rearrange("p (s w) -> p s w", s=2)[:, :, 15:271]
        return self.xa[:, cc, 15:15 + T]

    def ch_view(self, cc, ctx):
        if ctx:
            return self.ch[:, cc, :].rearrange("p (s w) -> p s w", s=2)[:, :, 1:257]
        return self.ch[:, cc, 1:1 + T]

    def v3(self, ap, ctx):
        return ap.rearrange("p (s w) -> p s w", s=2) if ctx else ap

    def proj_early(self, l, ti, ub):
        S, I, O = self.S, self.I, self.O
        ctx = ti < self.nct
        tok0 = ti * T
        for p in range(2):
            sv = self.in_slab(l, p)
            sg = self.in_slab(l, 2 + p)
            for jj in range(2):
                cc = 2 * p + jj
                bv = self.proj_chunk(sv, jj, ub)
                bg = self.proj_chunk(sg, jj, ub)
                ti_ = self.rot("tf", 3)
                self.act(self.tmpf[ti_][:], self.ps[bg][:], AF.Sigmoid, reads=[self.PS[bg]], writes=[self.TF[ti_]])
                self.tt("dve", self.xa_view(cc, ctx), self.v3(self.tmpf[ti_][:], ctx), self.v3(self.ps[bv][:], ctx), ALU.mult,
                        reads=[self.TF[ti_], self.PS[bv]], writes=[self.XA[cc]])
            self.done(sv)
            self.done(sg)
        for cc in range(4):
            self.dma("sp", S["xas"][cc * 128:(cc + 1) * 128, tok0:tok0 + T] if not ctx else
                     S["xas"][cc * 128:(cc + 1) * 128, tok0:tok0 + T].rearrange("p (s w) -> p s w", s=2),
                     self.xa_view(cc, ctx), reads=[self.XA[cc]], writes=[self.Dt["xas"][ti]])
        for p in range(2):
            sb = self.in_slab(l, 4 + p)
            for jj in range(2):
                cc = 2 * p + jj
                bb = self.proj_chunk(sb, jj, ub)
                self.cp("act", self.bg[:, cc, :], self.ps[bb][:], reads=[self.PS[bb]], writes=[self.BG[cc]])
            self.done(sb)
        self.dma("sp", S["bgs"][:, tok0:tok0 + T].rearrange("(c p) t -> p c t", p=128), self.bg[:],
                 reads=self.BG, writes=[self.Dt["bgs"][ti]])
        for p in range(2):
            sc = self.in_slab(l, 6 + p)
            sh = self.in_slab(l, 8 + p)
            for jj in range(2):
                cc = 2 * p + jj
                bc = self.proj_chunk(sc, jj, ub)
                bh = self.proj_chunk(sh, jj, ub)
                ti_ = self.rot("tf", 3)
                self.cp("act", self.tmpf[ti_][:], self.ps[bc][:], reads=[self.PS[bc]], writes=[self.TF[ti_]])
                self.tt("dve", self.ch_view(cc, ctx), self.v3(self.tmpf[ti_][:], ctx), self.v3(self.ps[bh][:], ctx), ALU.mult,
                        reads=[self.TF[ti_], self.PS[bh]], writes=[self.CH[cc]])
            self.done(sc)
            self.done(sh)
        for cc in range(4):
            self.dma("sp", S["chs"][cc * 128:(cc + 1) * 128, tok0:tok0 + T] if not ctx else
                     S["chs"][cc * 128:(cc + 1) * 128, tok0:tok0 + T].rearrange("p (s w) -> p s w", s=2),
                     self.ch_view(cc, ctx), reads=[self.CH[cc]], writes=[self.Dt["chs"][ti]])
        for which, s0, dst, DB, gcol in (("q", 10, self.qt, self.Q, C_GQ), ("k", 12, self.kt, self.K, C_GK)):
            for p in range(2):
                sl = self.in_slab(l, s0 + p)
                for jj in range(2):
                    cc = 2 * p + jj
                    bq = self.proj_chunk(sl, jj, ub)
                    si = self.rot("sq", 2)
                    self.act(self.sqb[si][:], self.ps[bq][:], AF.Square, reads=[self.PS[bq]], writes=[self.SQ[si]])
                    bs = self.bank()
                    self.mm(self.ps[bs][:], self.blk[:], self.sqb[si][:], True, True, reads=[self.CONST, self.SQ[si]], bankbuf=self.PS[bs])
                    s1 = self.rot("st", 3)
                    self.act(self.stat[s1][:], self.ps[bs][:], AF.Sqrt, reads=[self.PS[bs]], writes=[self.STB[s1]],
                             bias=self.col(0, C_EPS), scale=1.0 / HD)
                    self.recip(self.stat[s1][:], self.stat[s1][:], reads=[self.STB[s1]], writes=[self.STB[s1]])
                    self.stt("dve", dst[:, cc, 0:T], self.ps[bq][:], self.col(l, gcol), self.stat[s1][:], ALU.mult, ALU.mult,
                             reads=[self.PS[bq], self.STB[s1], self.COLS], writes=[DB[cc]])
                self.done(sl)
            nm = "qs" if which == "q" else "ks"
            self.dma("sp", S[nm][:, tok0:tok0 + T].rearrange("(c p) t -> p c t", p=128), dst[:, :, 0:T],
                     reads=DB, writes=[self.Dt[nm][ti]])
        s14 = self.in_slab(l, 14)
        s15 = self.in_slab(l, 15)
        for blk in range(4):
            b = self.bank()
            for hf, sl in ((0, s14), (1, s15)):
                for kc in range(KC):
                    self.mm(self.ps[b][:, hf * 256:(hf + 1) * 256], self.ut[ub][:, kc, blk * 128:(blk + 1) * 128], sl.ap[:, kc, :],
                            kc == 0, kc == KC - 1, reads=[sl.buf, self.U[ub][kc]], bankbuf=self.PS[b])
            self.cp("act", self.vt[:, blk, :].rearrange("p (h e) -> p h e", e=65)[:, :, 0:64],
                    self.ps[b][:].rearrange("p (h d) -> p h d", d=64), reads=[self.PS[b]], writes=self.Vk[blk])
            if ctx:
                tb = self.rot("tm", 2)
                self.cp("dve", self.tm[tb][:, 0:512], self.ps[b][:], reads=[self.PS[b]], writes=[self.TM[tb]])
                seq = ti * 2 + blk // 2
                ev = self.dma("sp", O["nv"][seq, l, (blk % 2) * 128:(blk % 2 + 1) * 128, :], self.tm[tb][:, 0:512],
                              reads=[self.TM[tb]], writes=[self.Dout])
                self.final_events.append(ev)
        self.done(s14)
        self.done(s15)
        self.dma("sp", S["vs"][tok0:tok0 + T, :].rearrange("(b p) f -> p b f", p=128), self.vt[:, 0:4, :],
                 reads=self.Vall[0:16], writes=[self.Dt["vs"][ti]])
        if ctx:
            s12 = self.in_slab(l, 12)
            s13 = self.in_slab(l, 13)
            for blk in range(4):
                b = self.bank()
                for hf, sl in ((0, s12), (1, s13)):
                    for kc in range(KC):
                        self.mm(self.ps[b][:, hf * 256:(hf + 1) * 256], self.ut[ub][:, kc, blk * 128:(blk + 1) * 128], sl.ap[:, kc, :],
                                kc == 0, kc == KC - 1, reads=[sl.buf, self.U[ub][kc]], bankbuf=self.PS[b])
                ti_ = self.rot("tf", 3)
                self.act(self.tmpf[ti_][:], self.ps[b][:], AF.Square, reads=[self.PS[b]], writes=[self.TF[ti_]])
                sm = self.rot("sm", 4)
                ss = self.small[:, sm * 16:sm * 16 + 8]
                self.op("dve", lambda e, o=ss, i=self.tmpf[ti_][:].rearrange("p (h d) -> p h d", d=64):
                        e.tensor_reduce(out=o, in_=i, axis=AX.X, op=ALU.add), reads=[self.TF[ti_]], writes=[self.SM[sm]])
                self.act(ss, ss, AF.Sqrt, reads=[self.SM[sm]], writes=[self.SM[sm]], bias=self.col(0, C_EPS), scale=1.0 / HD)
                self.recip(ss, ss, reads=[self.SM[sm]], writes=[self.SM[sm]])
                self.tt("dve", self.tmpf[ti_][:].rearrange("p (h d) -> p h d", d=64), self.ps[b][:].rearrange("p (h d) -> p h d", d=64),
                        ss.rearrange("p (h o) -> p h o", o=1).broadcast_to([128, 8, 64]), ALU.mult,
                        reads=[self.PS[b], self.SM[sm]], writes=[self.TF[ti_]])
                tb = self.rot("tm", 2)
                self.tt("dve", self.tm[tb][:, 0:512], self.tmpf[ti_][:], self.gkrow[:, l, :], ALU.mult,
                        reads=[self.TF[ti_], self.COLS], writes=[self.TM[tb]])
                seq = ti * 2 + blk // 2
                ev = self.dma("sp", O["nk"][seq, l, (blk % 2) * 128:(blk % 2 + 1) * 128, :], self.tm[tb][:, 0:512],
                              reads=[self.TM[tb]], writes=[self.Dout])
                self.final_events.append(ev)
            self.done(s12)
            self.done(s13)
    def p2_load(self, l, ti, a):
        S, I = self.S, self.I
        ctx = ti < self.nct
        tokm = 1 if ctx else 0
        tok0 = ti * T
        lt = ti - self.nct
        xb, ub = a, a
        fm = lambda nm: S[nm][:, tok0:tok0 + T].rearrange("(c p) t -> p c t", p=128)
        self.dma("sp", self.xt[xb][:], fm("xres"), reads=[self.Dt["xres"][ti]], writes=self.X[xb])
        self.dma("sp", self.ut[ub][:], fm("u2s"), reads=[self.Dt["u2s"][ti]], writes=self.U[ub])
        if ctx:
            for cc in range(4):
                self.dma("sp", self.xa_view(cc, True), S["xas"][cc * 128:(cc + 1) * 128, tok0:tok0 + T].rearrange("p (s w) -> p s w", s=2),
                         reads=[self.Dt["xas"][ti]], writes=[self.XA[cc]])
                self.dma("sp", self.ch_view(cc, True), S["chs"][cc * 128:(cc + 1) * 128, tok0:tok0 + T].rearrange("p (s w) -> p s w", s=2),
                         reads=[self.Dt["chs"][ti]], writes=[self.CH[cc]])
                xv = self.xa[:, cc, :].rearrange("p (s w) -> p s w", s=2)
                self.memset("pool", xv[:, :, 0:15], 0.0, [self.XA[cc]])
                self.memset("pool", xv[:, :, 271:286], 0.0, [self.XA[cc]])
                cv = self.ch[:, cc, :].rearrange("p (s w) -> p s w", s=2)
                self.memset("pool", cv[:, :, 0:1], 0.0, [self.CH[cc]])
                self.memset("pool", cv[:, :, 257:258], 0.0, [self.CH[cc]])
        else:
            first, last = lt == 0, lt == self.nlt - 1
            for nm, buf, BB, pad in (("xas", self.xa, self.XA, 15), ("chs", self.ch, self.CH, 1)):
                lo = tok0 - (0 if first else pad)
                hi = tok0 + T + (0 if last else pad)
                o0 = pad if first else 0
                rd = [self.Dt[nm][ti]] + ([] if first else [self.Dt[nm][ti - 1]]) + ([] if last else [self.Dt[nm][ti + 1]])
                self.dma("sp", buf[:, :, o0:o0 + hi - lo], S[nm][:, lo:hi].rearrange("(c p) t -> p c t", p=128), reads=rd, writes=BB)
                if first:
                    self.memset("pool", buf[:, :, 0:pad], 0.0, BB)
                if last:
                    self.memset("pool", buf[:, :, pad + T:pad + T + pad], 0.0, BB)
        self.dma("sp", self.bg[:], fm("bgs"), reads=[self.Dt["bgs"][ti]], writes=self.BG)
        self.dma("sp", self.qt[:], fm("qs"), reads=[self.Dt["qs"][ti]], writes=self.Q)
        if ctx:
            self.dma("sp", self.kt[:, :, 0:T], fm("ks"), reads=[self.Dt["ks"][ti]], writes=self.K)
            self.dma("sp", self.vt[:, 0:4, :], S["vs"][tok0:tok0 + T, :].rearrange("(b p) f -> p b f", p=128),
                     reads=[self.Dt["vs"][ti]], writes=self.Vall[0:16])
        else:
            rows = 8 * self.nlt
            w0 = 8 * lt - 4
            r_lo, r_hi = max(w0, 0), min(w0 + 16, rows)
            base = self.nct * T
            rd = [self.Dt["ks"][t2] for t2 in range(max(ti - 1, self.nct), min(ti + 2, self.NT))]
            self.dma("sp", self.kt[:, :, (r_lo - w0) * 64:(r_hi - w0) * 64],
                     S["ks"][:, base + r_lo * 64:base + r_hi * 64].rearrange("(c p) t -> p c t", p=128), reads=rd, writes=self.K)
    def p2_compute(self, l, ti, a, prefetch):
        S = self.S
        ctx = ti < self.nct
        tokm = 1 if ctx else 0
        tok0 = ti * T
        xb, ub = a, a
        if not ctx:
            self.attn_prep(ti, 0)
        ln_finish = self.branch_ab(l, ctx)
        if ctx:
            self.attn_ctx(l)
        else:
            self.attn_lat(l, ti)
        ln_finish()
        if prefetch is not None:
            prefetch()
        self.gates_merge(l, xb, ub, tokm)
        self.norm(l, 2, xb, ub, tokm)
        self.ffn(l, "2", xb, ub, tokm)
        if l == self.NL - 1:
            self.store_y_tokmajor(ti, xb)
        else:
            self.dma("sp", S["xres"][:, tok0:tok0 + T].rearrange("(c p) t -> p c t", p=128), self.xt[xb][:],
                     reads=self.X[xb], writes=[self.Dt["xres"][ti]])

    def conv(self, l, key, buf, BB, segw, ctx, evac):
        S = self.S
        segs = [(s_ * segw, s_ * 256, 256) for s_ in range(2)] if ctx else [(0, 0, T)]
        deferred = None
        if key == "ca":
            for cc in range(4):
                sls = [self.slab(S[f"ca_{l}"][cc * 2 + hf], 2048, self.Dw[f"ca_{l}"], (16, 128)) for hf in range(2)]
                b = self.bank()
                for (oi, oo, n) in segs:
                    for k in range(31):
                        sl = sls[k // 16]
                        self.mm(self.ps[b][:, oo:oo + n], sl.ap[:, k % 16, :], buf[:, cc, oi + k:oi + k + n], k == 0, k == 30,
                                reads=[sl.buf, BB[cc]], bankbuf=self.PS[b])
                for sl in sls:
                    self.done(sl)
                if deferred is not None:
                    deferred()
                deferred = evac(cc, b)
        else:
            sl = self.slab(S[f"cb_{l}"][0], 12 * 128, self.Dw[f"cb_{l}"], (12, 128))
            for cc in range(4):
                b = self.bank()
                for (oi, oo, n) in segs:
                    for k in range(3):
                        self.mm(self.ps[b][:, oo:oo + n], sl.ap[:, k * 4 + cc, :], buf[:, cc, oi + k:oi + k + n], k == 0, k == 2,
                                reads=[sl.buf, BB[cc]], bankbuf=self.PS[b])
                if deferred is not None:
                    deferred()
                deferred = evac(cc, b)
            self.done(sl)
        return deferred

    def branch_ab(self, l, ctx):
        bs1 = self.bank()
        self.hold(bs1)
        bs2 = self.bank()
        self.hold(bs2)

        def evac_a(cc, b):
            self.act(self.ha_f[cc], self.ps[b][:], AF.Identity, reads=[self.PS[b], self.COLS], writes=self.HA[cc],
                     bias=self.col(l, C_CAB + cc))
            si = self.rot("sq", 2)
            self.act(self.sqb[si][:], self.ps[b][:], AF.Square, reads=[self.PS[b], self.COLS], writes=[self.SQ[si]],
                     bias=self.col(l, C_CAB + cc))
            si2 = self.rot("sq", 2)
            self.act(self.sqb[si2][:], self.ps[b][:], AF.Identity, reads=[self.PS[b], self.COLS], writes=[self.SQ[si2]],
                     bias=self.col(l, C_CAB + cc))

            def stats():
                self.mm(self.ps[bs2][:], self.ones[:], self.sqb[si][:], cc == 0, cc == 3, reads=[self.CONST, self.SQ[si]], bankbuf=self.PS[bs2])
                self.mm(self.ps[bs1][:], self.ones[:], self.sqb[si2][:], cc == 0, cc == 3, reads=[self.CONST, self.SQ[si2]], bankbuf=self.PS[bs1])
            return stats

        last_a = self.conv(l, "ca", self.xa, self.XA, 286, ctx, evac_a)

        def evac_b(cc, b):
            self.tt("dve", self.tB[:, cc, :], self.ps[b][:], self.bg[:, cc, :], ALU.mult,
                    reads=[self.PS[b], self.BG[cc]], writes=[self.TB[cc]])
            if cc == 0:
                return last_a
            return None
        self.conv(l, "cb", self.ch, self.CH, 258, ctx, evac_b)
        m = self.rot("st", 3)
        self.op("act", lambda e, o=self.stat[m][:], i=self.ps[bs1][:]: e.mul(out=o, in_=i, mul=1.0 / 512), reads=[self.PS[bs1]], writes=[self.STB[m]])
        q = self.rot("st", 3)
        self.op("act", lambda e, o=self.stat[q][:], i=self.ps[bs2][:]: e.mul(out=o, in_=i, mul=1.0 / 512), reads=[self.PS[bs2]], writes=[self.STB[q]])
        self.unhold(bs1)
        self.unhold(bs2)

        def ln_finish():
            t_ = self.rot("tf", 3)
            self.tt("dve", self.tmpf[t_][:], self.stat[m][:], self.stat[m][:], ALU.mult, reads=[self.STB[m]], writes=[self.TF[t_]])
            self.tt("dve", self.stat[q][:], self.stat[q][:], self.tmpf[t_][:], ALU.subtract, reads=[self.STB[q], self.TF[t_]], writes=[self.STB[q]])
            self.act(self.stat[q][:], self.stat[q][:], AF.Sqrt, reads=[self.STB[q]], writes=[self.STB[q]], bias=self.col(0, C_EPS))
            self.recip(self.stat[q][:], self.stat[q][:], reads=[self.STB[q]], writes=[self.STB[q]])
            for cc in range(4):
                self.tt("dve", self.ha_f[cc], self.ha_f[cc], self.stat[m][:], ALU.subtract, reads=self.HA[cc] + [self.STB[m]], writes=self.HA[cc])
                self.tt("dve", self.ha_f[cc], self.ha_f[cc], self.stat[q][:], ALU.mult, reads=self.HA[cc] + [self.STB[q]], writes=self.HA[cc])
                self.act(self.sA[:, cc, :], self.ha_f[cc], AF.Silu, reads=self.HA[cc] + [self.COLS], writes=[self.SA[cc]],
                         bias=self.col(l, C_LNB + cc), scale=self.col(l, C_LNG + cc))
        return ln_finish

    def o_evac(self, bO, qb, half):
        sm = self.rot("sm", 4)
        rs = self.small[:, sm * 16:sm * 16 + 4]
        pv = self.ps[bO][:, 0:260].rearrange("p (h e) -> p h e", e=65)
        self.recip(rs.rearrange("p (h o) -> p h o", o=1), pv[:, :, 64:65], reads=[self.PS[bO]], writes=[self.SM[sm]])
        self.tt("dve", self.otm[qb][:, half * 256:(half + 1) * 256].rearrange("p (h d) -> p h d", d=64), pv[:, :, 0:64],
                rs.rearrange("p (h o) -> p h o", o=1).broadcast_to([128, 4, 64]), ALU.mult,
                reads=[self.PS[bO], self.SM[sm]], writes=[self.OTM[qb]])

    def o_transpose(self, qb, dst_view):
        b = self.bank()
        pb = self.ps[b][:].bitcast(BF16)
        for cc in range(4):
            self.tr(pb[:, cc * 128:(cc + 1) * 128], self.otm[qb][:, cc * 128:(cc + 1) * 128], self.identB[:],
                    reads=[self.OTM[qb], self.CONST], bankbuf=self.PS[b])
        self.cp("act", dst_view, pb[:, 0:512].rearrange("p (c q) -> p c q", c=4) if dst_view.ndim == 3 else
                pb[:, 0:512].rearrange("p (c a w) -> p c a w", c=4, a=8), reads=[self.PS[b]], writes=self.OT)

    def attn_ctx(self, l):
        for s in range(2):
            for half in range(2):
                bO = [self.bank(), self.bank()]
                for b_ in bO:
                    self.hold(b_)
                for hh in range(4):
                    h = half * 4 + hh
                    cc, p0 = h // 2, 64 * (h % 2)
                    bS = self.bank()
                    for kb in range(2):
                        self.mm(self.ps[bS][:, kb * 256:(kb + 1) * 256], self.kt[p0:p0 + 64, cc, s * 256 + kb * 128:s * 256 + (kb + 1) * 128],
                                self.qt[p0:p0 + 64, cc, s * 256:(s + 1) * 256], True, True,
                                reads=[self.K[cc], self.Q[cc]], bankbuf=self.PS[bS])
                    pi = self.rot("pc", 3)
                    self.act(self.pc[pi], self.ps[bS][:], AF.Exp, reads=[self.PS[bS]], writes=[self.PC[pi]])
                    for qb in range(2):
                        for kb in range(2):
                            self.mm(self.ps[bO[qb]][:, hh * 65:(hh + 1) * 65], self.pc[pi][:, kb * 256 + qb * 128:kb * 256 + (qb + 1) * 128],
                                    self.vt[:, s * 2 + kb, h * 65:(h + 1) * 65], kb == 0, kb == 1,
                                    reads=[self.PC[pi]] + self.Vk[s * 2 + kb], bankbuf=self.PS[bO[qb]])
                for qb in range(2):
                    self.o_evac(bO[qb], s * 2 + qb, half)
                    self.unhold(bO[qb])
            for qb in range(2):
                q4 = s * 2 + qb
                self.o_transpose(q4, self.oT[:, :, q4 * 128:(q4 + 1) * 128])

    def attn_prep(self, ti, j):
        S = self.S
        lt = ti - self.nct
        rows = 8 * self.nlt
        w0 = 8 * lt - 4
        base = self.nct * T
        bs = BLK_START[j]
        kb_i = j % 2
        for gi in range(4):
            vb = kb_i * 4 + gi
            for krl in range(4):
                r = w0 + 4 * gi + krl
                if 0 <= r < rows:
                    t0 = base + r * 64 + bs
                    rd = [self.Dt["vs"][t0 // T]]
                    self.dma("sp", self.vt[krl * 32:(krl + 1) * 32, vb, :], S["vs"][t0:t0 + 32, :], reads=rd, writes=[self.Vk[vb][krl]])
        for cc in range(4):
            self.cp("pool", self.kblk[kb_i][:, cc, :].rearrange("p (r w) -> p r w", w=32),
                    self.kt[:, cc, :].rearrange("p (r w) -> p r w", w=64)[:, :, bs:bs + 32],
                    reads=[self.K[cc]], writes=[self.KB[kb_i][cc]])

    def attn_lat(self, l, ti):
        S, I = self.S, self.I
        lt = ti - self.nct
        rows = 8 * self.nlt
        w0 = 8 * lt - 4
        base = self.nct * T
        if lt == 0:
            self.load_cache(l)
        for j in range(4):
            cp = CP_OF_J[j]
            bs = BLK_START[j]
            kb_i = j % 2
            if j + 1 < 4:
                self.attn_prep(ti, j + 1)
            slabs = [self.slab(S[f"bt_{l}"][self.mask_idx[(lt, cp)], hq * 4:(hq + 1) * 4].rearrange("h p k -> p h k"), 2048, self.Dw[f"bt_{l}"], (4, 512))
                     for hq in range(2)]
            bOs = {}

            def scores(h):
                half, hh = divmod(h, 4)
                cc, p0 = h // 2, 64 * (h % 2)
                qv = self.qt[p0:p0 + 64, cc, :].rearrange("p (a w) -> p a w", w=64)[:, :, 16 * j:16 * j + 16]
                bS = self.bank()
                for gi in range(4):
                    self.mm(self.ps[bS][:, gi * 128:(gi + 1) * 128], self.kblk[kb_i][p0:p0 + 64, cc, gi * 128:(gi + 1) * 128], qv,
                            True, True, reads=[self.KB[kb_i][cc], self.Q[cc]], bankbuf=self.PS[bS])
                bC = self.bank()
                for cb in range(4):
                    self.mm(self.ps[bC][:, cb * 128:(cb + 1) * 128], self.kc[p0:p0 + 64, cc, cb * 128:(cb + 1) * 128], qv,
                            True, True, reads=[self.KCb, self.Q[cc]], bankbuf=self.PS[bC])
                pi = self.rot("pl", 3)
                tfi = self.rot("tf", 3)
                self.tt("dve", self.tmpf[tfi][:], self.ps[bS][:], slabs[half].ap[:, hh, :], ALU.add,
                        reads=[self.PS[bS], slabs[half].buf], writes=[self.TF[tfi]])
                ci = self.rot("pc", 3)
                self.act(self.pc[ci], self.ps[bC][:], AF.Exp, reads=[self.PS[bC]], writes=[self.PC[ci]])
                self.act(self.pl[pi], self.tmpf[tfi][:], AF.Exp, reads=[self.TF[tfi]], writes=[self.PL[pi]])
                return (h, pi, ci)

            def pv(st_):
                h, pi, ci = st_
                half, hh = divmod(h, 4)
                bO = bOs[half]
                oview = self.ps[bO][:, hh * 65:(hh + 1) * 65]
                for gi in range(4):
                    self.mm(oview, self.pl[pi][:, gi * 128:(gi + 1) * 128], self.vt[:, kb_i * 4 + gi, h * 65:(h + 1) * 65],
                            gi == 0, False, reads=[self.PL[pi]] + self.Vk[kb_i * 4 + gi], bankbuf=self.PS[bO])
                for cb in range(4):
                    self.mm(oview, self.pc[ci][:, cb * 128:(cb + 1) * 128], self.vc[:, cb, h * 65:(h + 1) * 65],
                            False, cb == 3, reads=[self.PC[ci], self.VCb], bankbuf=self.PS[bO])
                if hh == 3:
                    self.o_evac(bO, j, half)
                    self.unhold(bO)

            pending = None
            for h in range(8):
                if h % 4 == 0:
                    bOs[h // 4] = self.bank()
                    self.hold(bOs[h // 4])
                cur = scores(h)
                if pending is not None:
                    pv(pending)
                pending = cur
            pv(pending)
            for sl in slabs:
                self.done(sl)
            self.o_transpose(j, self.oT[:, :, :].rearrange("p c (a w) -> p c a w", w=64)[:, :, :, 16 * j:16 * j + 16])

    def load_cache(self, l):
        I = self.I
        for kb in range(4):
            self.dma("pool", self.vc[:, kb, :].rearrange("p (h e) -> p h e", e=65)[:, :, 0:64],
                     I["cv"][l, kb * 128:(kb + 1) * 128, :].rearrange("p (h d) -> p h d", d=64), writes=[self.VCb], slow=True)
        for kb in range(4):
            tb = self.rot("tm", 2)
            self.dma("sp", self.tm[tb][:, 0:512], I["ck"][l, kb * 128:(kb + 1) * 128, :], writes=[self.TM[tb]])
            b = self.bank()
            for cc in range(4):
                self.tr(self.ps[b][:, cc * 128:(cc + 1) * 128], self.tm[tb][:, cc * 128:(cc + 1) * 128], self.identF[:],
                        reads=[self.TM[tb], self.CONST], bankbuf=self.PS[b])
            self.cp("dve", self.kc[:, :, kb * 128:(kb + 1) * 128], self.ps[b][:].rearrange("p (c t) -> p c t", c=4),
                    reads=[self.PS[b]], writes=[self.KCb] + self.KC4)

    def gates_merge(self, l, xb, ub, tokm):
        S = self.S
        for p in range(4):
            sl = {}
            for bi, br in enumerate("abc"):
                sl["g" + br] = self.in_slab(l, 16 + 4 * bi + p)
                sl["o" + br] = self.slab(S[f"o{br}_{l}"][p], 1024, self.Dw[f"o{br}_{l}"], (4, 256))
            for jj in range(2):
                mc = 2 * p + jj
                mi = self.rot("ma", 2)
                for bi, (br, src, SB) in enumerate((("a", self.sA, self.SA), ("b", self.tB, self.TB), ("c", self.oT, self.OT))):
                    bg = self.proj_chunk(sl["g" + br], jj, ub)
                    by = self.bank()
                    for kc in range(4):
                        self.mm(self.ps[by][:], sl["o" + br].ap[:, kc, jj * 128:(jj + 1) * 128], src[:, kc, :], kc == 0, kc == 3,
                                reads=[sl["o" + br].buf, SB[kc]], bankbuf=self.PS[by])
                    ti_ = self.rot("tf", 3)
                    self.act(self.tmpf[ti_][:], self.ps[bg][:], AF.Sigmoid, reads=[self.PS[bg]], writes=[self.TF[ti_]])
                    if bi == 0:
                        self.tt("dve", self.macc[mi], self.tmpf[ti_][:], self.ps[by][:], ALU.mult,
                                reads=[self.TF[ti_], self.PS[by]], writes=self.MA[mi])
                    else:
                        self.tt("dve", self.tmpf[ti_][:], self.tmpf[ti_][:], self.ps[by][:], ALU.mult,
                                reads=[self.TF[ti_], self.PS[by]], writes=[self.TF[ti_]])
                        if bi == 1:
                            self.tt("pool", self.macc[mi], self.macc[mi], self.tmpf[ti_][:], ALU.add,
                                    reads=self.MA[mi] + [self.TF[ti_]], writes=self.MA[mi])
                        else:
                            self.tt("pool", self.hv(mc), self.macc[mi], self.tmpf[ti_][:], ALU.add,
                                    reads=self.MA[mi] + [self.TF[ti_]], writes=[self.H[mc]])
            for s_ in sl.values():
                self.done(s_)
        for p in range(4):
            sm = self.slab(S[f"mg_{l}"][p], 2048, self.Dw[f"mg_{l}"], (8, 256))
            for jj in range(2):
                mc2 = 2 * p + jj
                b = self.bank()
                for mc in range(8):
                    self.mm(self.ps[b][:], sm.ap[:, mc, jj * 128:(jj + 1) * 128], self.hv(mc), mc == 0, mc == 7,
                            reads=[sm.buf, self.H[mc]], bankbuf=self.PS[b])
                self.stt("dve", self.xt[xb][:, mc2, :], self.ps[b][:], self.col(l, C_GT + (8 + mc2) * 2 + tokm),
                         self.xt[xb][:, mc2, :], ALU.mult, ALU.add,
                         reads=[self.PS[b], self.X[xb][mc2], self.COLS, self.MODS], writes=[self.X[xb][mc2]])
            self.done(sm)
    def emit_all(self):
        NL = self.NL
        self.reset_state()
        self.setup()
        self.memset("dve", self.col(0, C_EPS), EPS, [self.MODS])
        self.convert_p1(0)
        self.ada(0)
        self.COLS.const = True
        self.CONST.const = True
        self.SCT.const = True
        tasks_by_layer = {0: self.table_tasks(0) + [lambda: self.convert_p2(0)]}
        for l in range(1, NL):
            def ada_l(ll=l):
                self.MODS.const = False
                self.ada(ll)
                self.MODS.const = True
            tasks_by_layer[0] += [ada_l, (lambda ll=l: self.convert_p1(ll)), (lambda ll=l: self.convert_p2(ll))]
            tasks_by_layer[l] = self.table_tasks(l)

        self.MODS.const = True
        steps = []
        for l in range(NL):
            steps += [("p1", l, ti) for ti in range(self.NT)]
            steps += [("p2", l, ti) for ti in range(self.NT)]
        loaded = [-1]

        def load_step(k):
            ph, l_, ti_ = steps[k]
            loaded[0] = k
            (self.p1_load if ph == "p1" else self.p2_load)(l_, ti_, k % 2)

        for k, (ph, l, ti) in enumerate(steps):
            if loaded[0] < k:
                load_step(k)
            nxt = steps[k + 1] if k + 1 < len(steps) else None
            pf = None
            if nxt is not None and not (ph == "p1" and nxt[0] == "p2"):
                pf = (lambda kk=k + 1: load_step(kk))
            if ph == "p1":
                self.p1_compute(l, ti, k % 2, pf)
                tl = tasks_by_layer.get(l, [])
                per_slot = -(-(len(tl) + ti) // self.NT) if tl else 0
                for _ in range(max(per_slot, 2)):
                    if tl:
                        tl.pop(0)()
                if ti == self.NT - 1:
                    while tl:
                        tl.pop(0)()
            else:
                self.p2_compute(l, ti, k % 2, pf)
        if not self.dry:
            E = self.eng["pool"]
            deps = {}
            for ev in self.final_events:
                if ev is not None and deps.get(ev[0], 0) < ev[1]:
                    deps[ev[0]] = ev[1]
            waits = self._prune(E, deps)
            E.count += 1
            E.ops.append((waits, lambda e: e.memset(self.small[:, 60:64], 0.0), (E.sem, 1)))

    def replay(self):
        nc = self.nc
        sems = self.sem_handles

        def run(e, ops):
            for waits, fn, inc in ops:
                for s, v in waits:
                    e.wait_ge(sems[s], v)
                ins = fn(e)
                ins.then_inc(sems[inc[0]], inc[1])

        with nc.Block() as block:
            @block.tensor
            def _(e):
                run(e, self.eng["pe"].ops)

            @block.scalar
            def _(e):
                run(e, self.eng["act"].ops)

            @block.vector
            def _(e):
                run(e, self.eng["dve"].ops)

            @block.gpsimd
            def _(e):
                run(e, self.eng["pool"].ops)

            @block.sync
            def _(e):
                run(e, self.eng["sp"].ops)


def build_program(NL, nct, nlt):
    nc = bass.Bass("TRN2", target_bir_lowering=False)
    P = Prog(nc, NL, nct, nlt)
    P.declare()
    with contextlib.ExitStack() as stack:
        P.alloc(stack)
        P.dry = True
        P.emit_all()
        P.dry = False
        P.final_events = []
        P.emit_all()
        P.replay()
    return nc, P


def host_consts(P):
    ident = np.eye(128, dtype=np.float32)
    anti = np.ascontiguousarray(ident[::-1])
    ones = np.ones((128, 128), np.float32)
    blk = np.zeros((128, 128), np.float32)
    blk[:64, :64] = 1.0
    blk[64:, 64:] = 1.0
    return {"c_ident": ident, "c_anti": anti, "c_ones": ones, "c_blk": blk,
            "c_mask": np.stack(P.mask_list).astype(np.float32)}


_CACHE = {}


def run_cores(inputs, NL, nct, nlt, ncores):
    key = (NL, nct, nlt)
    if key not in _CACHE:
        _CACHE[key] = build_program(NL, nct, nlt)
    nc, P = _CACHE[key]
    cst = host_consts(P)
    f = lambda a: np.ascontiguousarray(np.asarray(a, dtype=np.float32))
    nseq = 2 * nct
    in_maps = []
    for c in range(ncores):
        m = dict(cst)
        m["xp"] = f(inputs["x_prompt"][c * nseq:(c + 1) * nseq]).reshape(nseq * 256, D)
        m["xs"] = f(inputs["x_sample"][c]).reshape(-1, D)
        m["ck"] = f(inputs["cache_k"][c]).reshape(NL, PAST, 512)
        m["cv"] = f(inputs["cache_v"][c]).reshape(NL, PAST, 512)
        m["cvec"] = np.stack([f(inputs["c"][c]), f(inputs["c_ctx"])])
        for nm, _ in WEIGHT_SHAPES:
            m[nm] = f(inputs[nm])
        in_maps.append(m)
    res = run_bass_kernel_spmd(nc, in_maps, core_ids=list(range(ncores)))
    outs = res.results
    yp = np.concatenate([o["yp"].reshape(nseq, 256, D) for o in outs], axis=0)
    ys = np.stack([o["ys"] for o in outs], axis=0)
    nk = np.concatenate([o["nk"].reshape(nseq, NL, 256, NH, HD) for o in outs], axis=0)
    nv = np.concatenate([o["nv"].reshape(nseq, NL, 256, NH, HD) for o in outs], axis=0)
    return (yp.astype(np.float32), ys.astype(np.float32), nk.astype(np.float32), nv.astype(np.float32))


def kernel(**inputs):
    return run_cores(inputs, 2, 2, 8, 8)
```

```python
import contextlib
import numpy as np
import concourse.bass as bass
import concourse.mybir as mybir
from concourse.bass_utils import run_bass_kernel_spmd

F32 = mybir.dt.float32
BF16 = mybir.dt.bfloat16
AF = mybir.ActivationFunctionType
ALU = mybir.AluOpType
AX = mybir.AxisListType

D = 1024
KC = 8
DFF = 2816
FC = 22
NIN = 7168
T = 512
NH = 8
HD = 64
PAST = 512
EPS = 1e-6
NSLOT = 9
SLOT = 2048
BLK_START = [0, 8, 24, 32]
CP_OF_J = [0, 1, 1, 2]
CP_OFF = [16, 8, 0]
RP_R, RP_C = 23, 63
NCOL = 420
C_G = 0
C_CAW = 24
C_CAB = 148
C_LNG = 152
C_LNB = 156
C_CBW = 160
C_GQ = 172
C_GK = 173
C_MOD = 176
C_A = 320
C_GT = 368
C_EPS = 416
WEIGHT_SHAPES = [("w_ada", [D, 9 * D]), ("b_ada", [9 * D]), ("g_ff1", [D]), ("w_ff1_gate", [D, DFF]), ("w_ff1_up", [D, DFF]),
                 ("w_ff1_down", [DFF, D]), ("g_mix", [D]), ("w_in", [D, NIN]), ("conv_a_w", [31, 512]),
                 ("conv_a_b", [512]), ("ln_a_g", [512]), ("ln_a_b", [512]), ("w_a_out", [512, D]),
                 ("conv_b_w", [3, 512]), ("w_b_out", [512, D]), ("q_norm_g", [64]), ("k_norm_g", [64]),
                 ("rpb", [8, 15, 31]), ("w_c_out", [512, D]), ("w_merge", [D, D]), ("g_ff2", [D]),
                 ("w_ff2_gate", [D, DFF]), ("w_ff2_up", [D, DFF]), ("w_ff2_down", [DFF, D])]


class Buf:
    __slots__ = ("name", "w", "r", "const")

    def __init__(self, name):
        self.name = name
        self.w = None
        self.r = {}
        self.const = False


class Eng:
    def __init__(self, name):
        self.name = name
        self.ops = []
        self.count = 0
        self.sem = None
        self.waited = {}
        self.dcount = 0


class DSem:
    def __init__(self, idx):
        self.idx = idx
        self.val = 0


class Slab:
    __slots__ = ("n", "slot", "ap", "buf")


def mask_for_tile(t, ntl, cp):
    rows = 8 * ntl
    kr_n = min(8, rows)
    m = np.zeros((128, 4, 8, 16), np.float32)
    j = [0, 1, 3][cp]
    bs = BLK_START[j]
    w0 = 8 * t - 4
    for gi in range(4):
        for krl in range(4):
            krow = w0 + 4 * gi + krl
            if krow < 0 or krow >= rows:
                continue
            for a in range(8):
                r = 8 * t + a
                rs = min(max(r - kr_n // 2, 0), rows - kr_n)
                if not (rs <= krow < rs + kr_n):
                    continue
                for kcl in range(32):
                    kc_ = bs + kcl
                    for qc in range(16):
                        q = 16 * j + qc
                        ws = min(max(q - 8, 0), 48)
                        if ws <= kc_ < ws + 16:
                            m[krl * 32 + kcl, gi, a, qc] = 1.0
    return m.reshape(128, 512)


class Prog:
    def __init__(self, nc, NL, n_ctx_tiles, n_lat_tiles):
        self.nc = nc
        self.NL = NL
        self.nct = n_ctx_tiles
        self.nlt = n_lat_tiles
        self.NT = n_ctx_tiles + n_lat_tiles
        self.nseq = 2 * n_ctx_tiles
        self.LT = T * n_lat_tiles
        self.NTOK = T * self.NT
        self.eng = {k: Eng(k) for k in ("pe", "act", "dve", "pool", "sp")}
        self.sem_handles = []
        self.dry = False
        self.requests = []
        self.req_list = []
        self.final_events = []
        pats = {}
        self.mask_idx = {}
        self.mask_list = []
        for t in range(n_lat_tiles):
            for cp in range(3):
                m = mask_for_tile(t, n_lat_tiles, cp)
                key = m.tobytes()
                if key not in pats:
                    pats[key] = len(self.mask_list)
                    self.mask_list.append(m)
                self.mask_idx[(t, cp)] = pats[key]

    def new_sem(self, stack, name):
        h = stack.enter_context(self.nc.semaphore(name))
        self.sem_handles.append(h)
        return len(self.sem_handles) - 1

    def _collect(self, reads, writes):
        deps = {}

        def add(ev):
            if ev is None:
                return
            s, v = ev
            if deps.get(s, 0) < v:
                deps[s] = v

        for b in reads:
            add(b.w)
        for b in writes:
            add(b.w)
            for s, v in b.r.items():
                add((s, v))
        return deps

    def _prune(self, E, deps):
        waits = []
        for s, v in deps.items():
            if E.name == "pe" and s == E.sem:
                continue
            if E.waited.get(s, 0) >= v:
                continue
            E.waited[s] = v
            waits.append((s, v))
        return waits

    def _mark(self, ev, reads, writes):
        s, v = ev
        for b in reads:
            if b.const:
                continue
            if b.r.get(s, 0) < v:
                b.r[s] = v
        for b in writes:
            b.w = ev
            b.r = {}

    def op(self, eng, fn, reads=(), writes=()):
        if self.dry:
            return None
        E = self.eng[eng]
        deps = self._collect(reads, writes)
        waits = self._prune(E, deps)
        E.count += 1
        ev = (E.sem, E.count)
        E.ops.append((waits, fn, (E.sem, 1)))
        self._mark(ev, reads, writes)
        return ev

    def dma(self, q, out, in_, reads=(), writes=(), sem=None, throttle=True, slow=False):
        if self.dry:
            return None
        E = self.eng[q]
        if sem is None:
            pool = self.dsems[q]
            S = pool[E.dcount % len(pool)]
            E.dcount += 1
        else:
            S = sem
        deps = self._collect(reads, writes)
        if throttle and S.val > 0:
            if deps.get(S.idx, 0) < S.val:
                deps[S.idx] = S.val
        waits = self._prune(E, deps)
        S.val += 16
        ev = (S.idx, S.val)
        E.ops.append((waits, (lambda e, o=out, i=in_, sl=slow: e.dma_start(out=o, in_=i, allow_slow_non_contiguous=True) if sl else e.dma_start(out=o, in_=i)), (S.idx, 16)))
        self._mark(ev, reads, writes)
        return ev

    def bank(self):
        st = self.st
        for _ in range(8):
            b = st["bank_rr"]
            st["bank_rr"] = (b + 1) % 8
            if b not in st["held"]:
                return b
        raise RuntimeError("no free psum bank")

    def hold(self, b):
        self.st["held"].add(b)

    def unhold(self, b):
        self.st["held"].discard(b)

    def rot(self, key, n):
        v = self.st.get(key, 0)
        self.st[key] = (v + 1) % n
        return v

    def slab(self, src_ap, nelem, srcbuf, shape):
        st = self.st
        n = st["req_i"]
        st["req_i"] += 1
        s = Slab()
        s.n = n
        s.slot = n % NSLOT
        s.buf = self.R[s.slot]
        base = self.ring[:, s.slot * SLOT: s.slot * SLOT + nelem]
        s.ap = base.rearrange("p (a b) -> p a b", a=shape[0])
        dst = s.ap if src_ap.ndim == 3 else base
        if self.dry:
            self.req_list.append((src_ap, dst, srcbuf))
            return s
        self.pump()
        assert st["loaded"] > n, "slab load not emitted"
        return s

    def pump(self):
        st = self.st
        L = self.req_list
        while st["loaded"] < len(L):
            m = st["loaded"]
            if m >= NSLOT and (m - NSLOT) not in st["released"]:
                break
            src_ap, dst, srcbuf = L[m]
            if srcbuf.w is None and m >= st["req_i"]:
                break
            slot = m % NSLOT
            self.dma("sp", dst, src_ap, reads=[srcbuf], writes=[self.R[slot]])
            st["loaded"] += 1
            st["released"].discard(m - NSLOT)

    def done(self, s):
        if self.dry:
            return
        self.st["released"].add(s.n)
        self.pump()

    def mm(self, out, lhsT, rhs, start, stop, reads, bankbuf):
        return self.op("pe", lambda e, o=out, l=lhsT, r=rhs, a=start, b=stop: e.matmul(o, l, r, start=a, stop=b),
                       reads=reads, writes=[bankbuf])

    def tr(self, out, in_, ident, reads, bankbuf):
        return self.op("pe", lambda e, o=out, i=in_, d=ident: e.transpose(o, i, d), reads=reads, writes=[bankbuf])

    def act(self, out, in_, func, reads, writes, bias=None, scale=None):
        kw = {}
        if bias is not None:
            kw["bias"] = bias
        if scale is not None:
            kw["scale"] = scale
        return self.op("act", lambda e, o=out, i=in_, f=func, k=kw: e.activation(out=o, in_=i, func=f, **k),
                       reads=list(reads) + [self.MODS, self.COLS], writes=writes)

    def tt(self, eng, out, in0, in1, op, reads, writes):
        return self.op(eng, lambda e, o=out, a=in0, b=in1, p=op: e.tensor_tensor(out=o, in0=a, in1=b, op=p),
                       reads=reads, writes=writes)

    def stt(self, eng, out, in0, scalar, in1, op0, op1, reads, writes):
        return self.op(eng, lambda e, o=out, a=in0, s=scalar, b=in1, p0=op0, p1=op1:
                       e.scalar_tensor_tensor(out=o, in0=a, scalar=s, in1=b, op0=p0, op1=p1),
                       reads=reads, writes=writes)

    def ts(self, eng, out, in0, s1, s2, op0, op1, reads, writes):
        if s2 is None:
            return self.op(eng, lambda e, o=out, a=in0, x=s1, p0=op0: e.tensor_scalar(out=o, in0=a, scalar1=x, scalar2=None, op0=p0),
                           reads=reads, writes=writes)
        return self.op(eng, lambda e, o=out, a=in0, x=s1, y=s2, p0=op0, p1=op1:
                       e.tensor_scalar(out=o, in0=a, scalar1=x, scalar2=y, op0=p0, op1=p1),
                       reads=reads, writes=writes)

    def cp(self, eng, out, in_, reads, writes):
        if eng == "act":
            return self.op("act", lambda e, o=out, i=in_: e.copy(out=o, in_=i), reads=reads, writes=writes)
        return self.op(eng, lambda e, o=out, i=in_: e.tensor_copy(out=o, in_=i), reads=reads, writes=writes)

    def recip(self, out, in_, reads, writes):
        return self.op("dve", lambda e, o=out, i=in_: e.reciprocal(out=o, in_=i), reads=reads, writes=writes)

    def memset(self, eng, ap, val, writes):
        return self.op(eng, lambda e, a=ap, v=val: e.memset(a, v), reads=(), writes=writes)

    def declare(self):
        nc = self.nc
        NL = self.NL

        def din(name, shape):
            return nc.dram_tensor(name, list(shape), F32, kind="ExternalInput").ap()

        def dout(name, shape):
            return nc.dram_tensor(name, list(shape), F32, kind="ExternalOutput").ap()

        def scr(name, shape, dt=BF16):
            return nc.dram_tensor(name, list(shape), dt, kind="Internal").ap()

        I = {}
        I["xp"] = din("xp", [self.nct * T, D])
        I["xs"] = din("xs", [self.LT, D])
        I["ck"] = din("ck", [NL, PAST, 512])
        I["cv"] = din("cv", [NL, PAST, 512])
        I["cvec"] = din("cvec", [2, D])
        for nm, sh in WEIGHT_SHAPES:
            I[nm] = din(nm, [NL] + sh)
        I["c_ident"] = din("c_ident", [128, 128])
        I["c_anti"] = din("c_anti", [128, 128])
        I["c_ones"] = din("c_ones", [128, 128])
        I["c_blk"] = din("c_blk", [128, 128])
        I["c_mask"] = din("c_mask", [len(self.mask_list), 128, 512])
        self.I = I
        O = {}
        O["yp"] = dout("yp", [self.nct * T, D])
        O["ys"] = dout("ys", [self.LT, D])
        O["nk"] = dout("nk", [self.nseq, NL, 256, 512])
        O["nv"] = dout("nv", [self.nseq, NL, 256, 512])
        self.O = O
        S = {}
        for l in range(NL):
            for f in ("1", "2"):
                S[f"g{f}_{l}"] = scr(f"wg{f}_{l}", [11, 128, 2048])
                S[f"u{f}_{l}"] = scr(f"wu{f}_{l}", [11, 128, 2048])
                S[f"d{f}_{l}"] = scr(f"wd{f}_{l}", [16, 128, 11 * 128])
            S[f"in_{l}"] = scr(f"win_{l}", [28, 128, 2048])
            for b in "abc":
                S[f"o{b}_{l}"] = scr(f"wo{b}_{l}", [4, 128, 4 * 256])
            S[f"mg_{l}"] = scr(f"wmg_{l}", [4, 128, 2048])
            S[f"ca_{l}"] = scr(f"wca_{l}", [8, 128, 2048])
            S[f"cb_{l}"] = scr(f"wcb_{l}", [1, 128, 12 * 128])
            S[f"bt_{l}"] = scr(f"wbt_{l}", [len(self.mask_list), 8, 128, 512])
            S[f"rp_{l}"] = scr(f"rp_{l}", [8, 1472])
        S["xres"] = scr("xres", [D, self.NTOK], F32)
        S["u2s"] = scr("u2s", [D, self.NTOK])
        S["xas"] = scr("xas", [512, self.NTOK])
        S["chs"] = scr("chs", [512, self.NTOK])
        S["bgs"] = scr("bgs", [512, self.NTOK])
        S["qs"] = scr("qs", [512, self.NTOK])
        S["ks"] = scr("ks", [512, self.NTOK])
        S["vs"] = scr("vs", [self.NTOK, 520])
        self.S = S
        self.Dw = {k: Buf("D" + k) for k in S}
        self.Dt = {}
        for nm in ("xres", "u2s", "xas", "chs", "bgs", "qs", "ks", "vs"):
            self.Dt[nm] = [Buf(f"D{nm}{i}") for i in range(self.NT)]
        self.Dout = Buf("Dout")

    def alloc(self, stack):
        nc = self.nc
        A = nc.alloc_sbuf_tensor
        self.xt = [A(f"xt{i}", [128, KC, T], F32) for i in range(2)]
        self.X = [[Buf(f"X{i}_{c}") for c in range(KC)] for i in range(2)]
        self.ut = [A(f"ut{i}", [128, KC, T], BF16) for i in range(2)]
        self.U = [[Buf(f"U{i}_{c}") for c in range(KC)] for i in range(2)]
        self.harena = A("harena", [128, FC * T], BF16)
        self.H = [Buf(f"H{j}") for j in range(FC)]
        self.ring = A("ring", [128, NSLOT * SLOT], BF16)
        self.R = [Buf(f"R{i}") for i in range(NSLOT)]
        self.tm = [A(f"tm{i}", [128, D], F32) for i in range(2)]
        self.TM = [Buf(f"TM{i}") for i in range(2)]
        self.sqb = [A(f"sqb{i}", [128, T], BF16) for i in range(2)]
        self.SQ = [Buf(f"SQ{i}") for i in range(2)]
        self.tmpf = [A(f"tmpf{i}", [128, T], F32) for i in range(3)]
        self.TF = [Buf(f"TF{i}") for i in range(3)]
        self.stat = [A(f"stat{i}", [128, T], F32) for i in range(3)]
        self.STB = [Buf(f"ST{i}") for i in range(3)]
        self.xa = A("xa", [128, 4, 572], BF16)
        self.XA = [Buf(f"XA{c}") for c in range(4)]
        self.ch = A("ch", [128, 4, 516], BF16)
        self.CH = [Buf(f"CH{c}") for c in range(4)]
        self.bg = A("bg", [128, 4, T], BF16)
        self.BG = [Buf(f"BG{c}") for c in range(4)]
        self.qt = A("qt", [128, 4, T], BF16)
        self.Q = [Buf(f"Q{c}") for c in range(4)]
        self.kt = A("kt", [128, 4, 2 * T], BF16)
        self.K = [Buf(f"K{c}") for c in range(4)]
        self.kblk = [A(f"kblk{i}", [128, 4, 4 * 128], BF16) for i in range(2)]
        self.KB = [[Buf(f"KB{i}_{c}") for c in range(4)] for i in range(2)]
        self.vt = A("vt", [128, 8, 520], BF16)
        self.V = [Buf(f"V{i}") for i in range(8)]
        self.maskt = A("maskt", [128, 3, T], BF16)
        self.MK = [Buf(f"MK{i}") for i in range(3)]
        self.sA = A("sA", [128, 4, T], BF16)
        self.SA = [Buf(f"SA{c}") for c in range(4)]
        self.tB = A("tB", [128, 4, T], BF16)
        self.TB = [Buf(f"TB{c}") for c in range(4)]
        self.oT = A("oT", [128, 4, T], BF16)
        self.OT = [Buf(f"OT{c}") for c in range(4)]
        self.kc = A("kc", [128, 4, PAST], BF16)
        self.KCb = Buf("KC")
        self.vc = A("vc", [128, 4, 520], BF16)
        self.VCb = Buf("VC")
        self.identF = A("identF", [128, 128], F32)
        self.identB = A("identB", [128, 128], BF16)
        self.anti = A("anti", [128, 128], BF16)
        self.ones = A("ones", [128, 128], BF16)
        self.blk = A("blk", [128, 128], BF16)
        self.CONST = Buf("CONST")
        self.cols = A("cols", [128, self.NL * NCOL], F32)
        self.COLS = Buf("COLS")
        self.MODS = Buf("MODS")
        self.gkrow = A("gkrow", [128, self.NL, 512], F32)
        self.small = A("small", [128, 64], F32)
        self.SM = [Buf(f"SM{i}") for i in range(4)]
        self.scT = A("scT", [128, KC, 2], F32)
        self.SCT = Buf("SCT")
        self.ps = [nc.alloc_psum_tensor(f"ps{i}", [128, T], F32) for i in range(8)]
        self.PS = [Buf(f"PS{i}") for i in range(8)]
        ha = self.harena
        self.ha_f = [ha[:, (2 * c) * T:(2 * c + 2) * T].bitcast(F32) for c in range(4)]
        self.HA = [[self.H[2 * c], self.H[2 * c + 1]] for c in range(4)]
        self.otm = [ha[:, (8 + q) * T:(9 + q) * T] for q in range(4)]
        self.OTM = [self.H[8 + q] for q in range(4)]
        self.pl = [ha[:, (12 + i) * T:(13 + i) * T] for i in range(3)]
        self.PL = [self.H[12 + i] for i in range(3)]
        self.pc = [ha[:, (15 + i) * T:(16 + i) * T] for i in range(3)]
        self.PC = [self.H[15 + i] for i in range(3)]
        self.macc = [ha[:, (18 + 2 * i) * T:(20 + 2 * i) * T].bitcast(F32) for i in range(2)]
        self.MA = [[self.H[18 + 2 * i], self.H[19 + 2 * i]] for i in range(2)]
        for k in ("pe", "act", "dve", "pool"):
            self.eng[k].sem = self.new_sem(stack, "c_" + k)
        self.dsems = {"sp": [DSem(self.new_sem(stack, f"dsp{i}")) for i in range(20)],
                      "pool": [DSem(self.new_sem(stack, f"dpl{i}")) for i in range(8)],
                      "act": [DSem(self.new_sem(stack, f"dac{i}")) for i in range(8)]}
        self.KC4 = [Buf(f"KC4_{i}") for i in range(4)]
        self.csem = {}
        for k in self.S:
            if k[0] in "gudiomcbr" and "_" in k:
                self.csem[k] = DSem(self.new_sem(stack, "cv_" + k))

    def hv(self, j):
        return self.harena[:, j * T:(j + 1) * T]

    def col(self, l, idx, n=1):
        b = l * NCOL + idx
        return self.cols[:, b:b + n]

    def reset_state(self):
        self.st = {"bank_rr": 0, "held": set(), "req_i": 0, "pending": [], "loaded": 0, "released": set()}

    def setup(self):
        I, S = self.I, self.S
        NL = self.NL
        C = [self.CONST]
        self.dma("pool", self.identB[:], I["c_ident"], writes=C)
        self.dma("pool", self.anti[:], I["c_anti"], writes=C)
        self.dma("pool", self.ones[:], I["c_ones"], writes=C)
        self.dma("pool", self.blk[:], I["c_blk"], writes=C)
        self.dma("sp", self.identF[:], I["c_ident"], writes=C)
        self.memset("pool", self.kt[:], 0.0, self.K)
        self.memset("pool", self.vt[:], 0.0, self.V)
        self.memset("pool", self.vc[:], 0.0, [self.VCb])
        self.memset("pool", self.xa[:], 0.0, self.XA)
        self.memset("pool", self.ch[:], 0.0, self.CH)
        self.memset("dve", self.kblk[0][:], 0.0, self.KB[0])
        self.memset("dve", self.kblk[1][:], 0.0, self.KB[1])
        self.memset("dve", self.vt[:].rearrange("p b (h e) -> p b h e", e=65)[:, :, :, 64:65], 1.0, self.V)
        self.memset("dve", self.vc[:].rearrange("p b (h e) -> p b h e", e=65)[:, :, :, 64:65], 1.0, [self.VCb])
        self.memset("dve", self.small[:], 0.0, self.SM)
        CL = [self.COLS]
        for l in range(NL):
            for i, nm in enumerate(("g_ff1", "g_mix", "g_ff2")):
                self.dma("sp", self.col(l, C_G + 8 * i, 8), I[nm][l].rearrange("(c p) -> p c", p=128), writes=CL, slow=True)
            for cc in range(4):
                self.dma("sp", self.col(l, C_CAW, 124).rearrange("p (k c) -> p k c", c=4)[:, :, cc],
                         I["conv_a_w"][l][:, cc * 128:(cc + 1) * 128].rearrange("k p -> p k"), writes=CL, slow=True)
                self.dma("sp", self.col(l, C_CBW, 12).rearrange("p (k c) -> p k c", c=4)[:, :, cc],
                         I["conv_b_w"][l][:, cc * 128:(cc + 1) * 128].rearrange("k p -> p k"), writes=CL, slow=True)
            for idx, nm in ((C_CAB, "conv_a_b"), (C_LNG, "ln_a_g"), (C_LNB, "ln_a_b")):
                self.dma("sp", self.col(l, idx, 4), I[nm][l].rearrange("(c p) -> p c", p=128), writes=CL, slow=True)
            for idx, nm in ((C_GQ, "q_norm_g"), (C_GK, "k_norm_g")):
                for hh in range(2):
                    self.dma("sp", self.cols[hh * 64:(hh + 1) * 64, l * NCOL + idx:l * NCOL + idx + 1],
                             I[nm][l].rearrange("(p o) -> p o", o=1), writes=CL, slow=True)
            self.dma("sp", self.gkrow[:, l, :].rearrange("p (h d) -> p h d", h=8),
                     bass.AP(I["k_norm_g"].tensor, l * 64, [[0, 128], [0, 8], [1, 64]]), writes=CL)
            self.ts("dve", self.col(l, C_GQ), self.col(l, C_GQ), 0.125, None, ALU.mult, None, reads=CL, writes=CL)
        for t_ in range(2):
            self.dma("sp", self.scT[:, :, t_], I["cvec"][t_].rearrange("(c p) -> p c", p=128), writes=[self.SCT], slow=True)
        self.act(self.scT[:], self.scT[:], AF.Silu, reads=[self.SCT], writes=[self.SCT])

    def ada(self, l):
        I = self.I
        CL = [self.COLS]
        ML = [self.MODS]
        wv = I["w_ada"][l].rearrange("(kc p) n -> p kc n", p=128)
        for ct in range(36):
            hb = ct % 2
            stg = self.harena[:, hb * 4096:(hb + 1) * 4096].bitcast(F32).rearrange("p (k n) -> p k n", k=KC)
            HB = self.H[hb * 8:hb * 8 + 8]
            self.dma("sp", stg, wv[:, :, ct * 256:(ct + 1) * 256], writes=HB)
            ti = self.rot("tf", 3)
            self.dma("sp", self.tmpf[ti][0:2, 0:256], bass.AP(I["b_ada"].tensor, l * 9 * D + ct * 256, [[0, 2], [1, 256]]),
                     writes=[self.TF[ti]])
            b = self.bank()
            for kc in range(KC):
                self.mm(self.ps[b][0:2, 0:256], self.scT[:, kc, :], stg[:, kc, :], kc == 0, kc == KC - 1,
                        reads=[self.SCT] + HB, bankbuf=self.PS[b])
            self.tt("dve", self.tmpf[ti][0:2, 0:256], self.ps[b][0:2, 0:256], self.tmpf[ti][0:2, 0:256], ALU.add,
                    reads=[self.PS[b], self.TF[ti]], writes=[self.TF[ti]])
            b2 = self.bank()
            for i in range(2):
                self.tr(self.ps[b2][:, 2 * i:2 * i + 2], self.tmpf[ti][0:2, i * 128:(i + 1) * 128], self.identF[0:2, 0:2],
                        reads=[self.TF[ti], self.CONST], bankbuf=self.PS[b2])
            self.cp("dve", self.col(l, C_MOD + ct * 4, 4), self.ps[b2][:, 0:4], reads=[self.PS[b2]], writes=ML)
        for i in range(3):
            sc = self.col(l, C_MOD + (3 * i + 1) * 16, 16)
            a = self.col(l, C_A + i * 16, 16)
            self.ts("dve", a, sc, 1.0, None, ALU.add, None, reads=ML, writes=ML)
            g = self.col(l, C_G + 8 * i, 8)
            self.tt("dve", a.rearrange("p (c t) -> p c t", t=2), a.rearrange("p (c t) -> p c t", t=2),
                    g.rearrange("p (c o) -> p c o", o=1).broadcast_to([128, 8, 2]), ALU.mult, reads=CL + ML, writes=ML)
            gt = self.col(l, C_GT + i * 16, 16)
            self.ts("dve", gt, self.col(l, C_MOD + (3 * i + 2) * 16, 16), 1.0 if i == 1 else 0.5, None, ALU.mult, None,
                    reads=ML, writes=ML)

    def convert(self, l, which):
        I, S = self.I, self.S

        def kslabs(key, w, ncol, kc, nslab):
            wv = w.rearrange("(kc p) n -> p kc n", p=128)
            for s in range(nslab):
                self.dma("pool", S[key][s].rearrange("p (a b) -> p a b", a=kc), wv[:, :, s * ncol:(s + 1) * ncol],
                         writes=[self.Dw[key]], sem=self.csem[key], throttle=False)

        def dslabs(key, w):
            wv = w.rearrange("(j p) n -> p j n", p=128)
            for mc in range(8):
                for hf in range(2):
                    self.dma("pool", S[key][mc * 2 + hf].rearrange("p (a b) -> p a b", a=11),
                             wv[:, hf * 11:(hf + 1) * 11, mc * 128:(mc + 1) * 128],
                             writes=[self.Dw[key]], sem=self.csem[key], throttle=False)

        if which == "p1":
            kslabs(f"g1_{l}", I["w_ff1_gate"][l], 256, 8, 11)
            kslabs(f"u1_{l}", I["w_ff1_up"][l], 256, 8, 11)
            dslabs(f"d1_{l}", I["w_ff1_down"][l])
            kslabs(f"in_{l}", I["w_in"][l], 256, 8, 28)
        else:
            kslabs(f"oa_{l}", I["w_a_out"][l], 256, 4, 4)
            kslabs(f"ob_{l}", I["w_b_out"][l], 256, 4, 4)
            kslabs(f"oc_{l}", I["w_c_out"][l], 256, 4, 4)
            kslabs(f"mg_{l}", I["w_merge"][l], 256, 8, 4)
            kslabs(f"g2_{l}", I["w_ff2_gate"][l], 256, 8, 11)
            kslabs(f"u2_{l}", I["w_ff2_up"][l], 256, 8, 11)
            dslabs(f"d2_{l}", I["w_ff2_down"][l])

    def convert_p1(self, l):
        self.convert(l, "p1")

    def convert_p2(self, l):
        self.convert(l, "p2")

    def table_tasks(self, l):
        I, S = self.I, self.S
        CL = [self.COLS]
        NP_ = 1472
        kcf = self.kc[:].rearrange("p c t -> p (c t)")
        KCB = [self.KCb] + self.KC4
        rpt = S[f"rp_{l}"].tensor

        def neg_view(i_):
            if i_ < 4:
                return self.sA[:, i_, :], self.SA[i_]
            if i_ < 8:
                return self.tB[:, i_ - 4, :], self.TB[i_ - 4]
            return self.oT[:, i_ - 8, :], self.OT[i_ - 8]

        def loads(h):
            pb = self.kblk[h % 2][:].rearrange("p c t -> p (c t)")
            for krl in range(4):
                src = bass.AP(rpt, h * NP_ + krl * RP_C, [[1, 32], [1, 1232]])
                self.dma("pool", pb[krl * 32:(krl + 1) * 32, 0:1232], src, reads=[self.Dw[f"rp_{l}"]], writes=self.KB[h % 2])

        def sub0():
            for cc in range(4):
                for hf in range(2):
                    si = cc * 2 + hf
                    for i in range(16):
                        k = hf * 16 + i
                        if k < 31:
                            self.ts("pool", kcf[:, i * 128:(i + 1) * 128], self.identF[:], self.col(l, C_CAW + k * 4 + cc), None,
                                    ALU.mult, None, reads=CL + [self.CONST], writes=KCB)
                        else:
                            self.memset("pool", kcf[:, i * 128:(i + 1) * 128], 0.0, KCB)
                    self.dma("pool", S[f"ca_{l}"][si], kcf, reads=KCB, writes=[self.Dw[f"ca_{l}"]], sem=self.csem[f"ca_{l}"], throttle=False)
            for k in range(3):
                for cc in range(4):
                    i = k * 4 + cc
                    self.ts("pool", kcf[:, i * 128:(i + 1) * 128], self.identF[:], self.col(l, C_CBW + k * 4 + cc), None,
                            ALU.mult, None, reads=CL + [self.CONST], writes=KCB)
            self.dma("pool", S[f"cb_{l}"][0], kcf[:, 0:1536], reads=KCB, writes=[self.Dw[f"cb_{l}"]], sem=self.csem[f"cb_{l}"], throttle=False)
            VB = [self.VCb]
            rpb16 = self.vc[:].rearrange("p b f -> p (b f)")[0:8, 0:NP_]
            self.memset("pool", rpb16, 0.0, VB)
            self.dma("pool", rpb16[:, 0:RP_R * RP_C].rearrange("p (r c) -> p r c", c=RP_C)[:, 4:19, 16:47], I["rpb"][l], reads=(), writes=VB, slow=True)
            self.dma("pool", S[f"rp_{l}"][:, 0:NP_], rpb16, reads=VB, writes=[self.Dw[f"rp_{l}"]], sem=self.csem[f"rp_{l}"], throttle=False)
            self.memset("pool", self.vc[:].rearrange("p b (h e) -> p b h e", e=65)[:, :, :, 64:65], 1.0, VB)
            for i_ in range(len(self.mask_list)):
                v_, b_ = neg_view(i_)
                self.dma("pool", v_, I["c_mask"][i_], writes=[b_])
                self.ts("pool", v_, v_, 30000.0, -30000.0, ALU.mult, ALU.add, reads=[b_], writes=[b_])
            loads(0)

        def sub_h(h):
            def f():
                if h + 1 < 8:
                    loads(h + 1)
                pb = self.kblk[h % 2][:].rearrange("p c t -> p (c t)")
                for cp in range(3):
                    off = [0, -8, -16][cp]
                    base = pb[:, 472 + off:473 + off]
                    srcv = bass.AP(base.tensor, base.offset, [list(base.ap[0]), [252, 4], [-RP_C, 8], [-1, 16]])
                    mi = self.rot("btc", 3)
                    self.cp("pool", self.maskt[:, mi, :].rearrange("p (g a q) -> p g a q", g=4, a=8), srcv, reads=self.KB[h % 2], writes=[self.MK[mi]])
                    for id_ in sorted(set(self.mask_idx[(t_, cp)] for t_ in range(self.nlt))):
                        v_, b_ = neg_view(id_)
                        ki = self.rot("bts", 4)
                        self.tt("pool", self.kc[:, ki, :], self.maskt[:, mi, :], v_, ALU.add, reads=[self.MK[mi], b_, self.KCb], writes=[self.KC4[ki]])
                        self.dma("pool", S[f"bt_{l}"][id_, h], self.kc[:, ki, :], reads=[self.KC4[ki]],
                                 writes=[self.Dw[f"bt_{l}"]], sem=self.csem[f"bt_{l}"], throttle=False)
            return f

        return [sub0] + [sub_h(h) for h in range(8)]

    def norm(self, l, i, xb, ub, tokm):
        b = self.bank()
        for c in range(KC):
            si = self.rot("sq", 2)
            self.act(self.sqb[si][:], self.xt[xb][:, c, :], AF.Square, reads=[self.X[xb][c]], writes=[self.SQ[si]])
            self.mm(self.ps[b][:], self.ones[:], self.sqb[si][:], c == 0, c == KC - 1,
                    reads=[self.CONST, self.SQ[si]], bankbuf=self.PS[b])
        s1 = self.rot("st", 3)
        self.act(self.stat[s1][:], self.ps[b][:], AF.Sqrt, reads=[self.PS[b]], writes=[self.STB[s1]],
                 bias=self.col(0, C_EPS), scale=1.0 / D)
        self.recip(self.stat[s1][:], self.stat[s1][:], reads=[self.STB[s1]], writes=[self.STB[s1]])
        for c in range(KC):
            ti = self.rot("tf", 3)
            self.stt("dve", self.tmpf[ti][:], self.xt[xb][:, c, :], self.col(l, C_A + (i * 8 + c) * 2 + tokm),
                     self.stat[s1][:], ALU.mult, ALU.mult,
                     reads=[self.X[xb][c], self.COLS, self.MODS, self.STB[s1]], writes=[self.TF[ti]])
            self.act(self.ut[ub][:, c, :], self.tmpf[ti][:], AF.Identity, reads=[self.TF[ti], self.COLS, self.MODS],
                     writes=[self.U[ub][c]], bias=self.col(l, C_MOD + (3 * i * 8 + c) * 2 + tokm))

    def ffn(self, l, f, xb, ub, tokm, mid=None):
        gi = 0 if f == "1" else 2
        S = self.S
        kg, ku, kd = f"g{f}_{l}", f"u{f}_{l}", f"d{f}_{l}"
        for s in range(11):
            sg = self.slab(S[kg][s], 2048, self.Dw[kg], (8, 256))
            su = self.slab(S[ku][s], 2048, self.Dw[ku], (8, 256))
            for jj in range(2):
                j = 2 * s + jj
                bg = self.bank()
                for kc in range(KC):
                    self.mm(self.ps[bg][:], sg.ap[:, kc, jj * 128:(jj + 1) * 128], self.ut[ub][:, kc, :], kc == 0, kc == KC - 1,
                            reads=[sg.buf, self.U[ub][kc]], bankbuf=self.PS[bg])
                bu = self.bank()
                for kc in range(KC):
                    self.mm(self.ps[bu][:], su.ap[:, kc, jj * 128:(jj + 1) * 128], self.ut[ub][:, kc, :], kc == 0, kc == KC - 1,
                            reads=[su.buf, self.U[ub][kc]], bankbuf=self.PS[bu])
                ti = self.rot("tf", 3)
                self.act(self.tmpf[ti][:], self.ps[bg][:], AF.Silu, reads=[self.PS[bg]], writes=[self.TF[ti]])
                self.tt("dve", self.hv(j), self.tmpf[ti][:], self.ps[bu][:], ALU.mult,
                        reads=[self.TF[ti], self.PS[bu]], writes=[self.H[j]])
            self.done(sg)
            self.done(su)
        if mid is not None:
            mid()
        for mc in range(8):
            b = self.bank()
            for hf in range(2):
                sd = self.slab(S[kd][mc * 2 + hf], 11 * 128, self.Dw[kd], (11, 128))
                for jj in range(11):
                    j = hf * 11 + jj
                    self.mm(self.ps[b][:], sd.ap[:, jj, :], self.hv(j), j == 0, j == FC - 1,
                            reads=[sd.buf, self.H[j]], bankbuf=self.PS[b])
                self.done(sd)
            self.stt("dve", self.xt[xb][:, mc, :], self.ps[b][:], self.col(l, C_GT + (gi * 8 + mc) * 2 + tokm),
                     self.xt[xb][:, mc, :], ALU.mult, ALU.add,
                     reads=[self.PS[b], self.X[xb][mc], self.COLS, self.MODS], writes=[self.X[xb][mc]])

    def load_x_tokmajor(self, ti, xb):
        I = self.I
        src = I["xp"] if ti < self.nct else I["xs"]
        r0 = (ti if ti < self.nct else ti - self.nct) * T
        for blk in range(4):
            tb = self.rot("tm", 2)
            self.dma("sp", self.tm[tb][:], src[r0 + blk * 128:r0 + (blk + 1) * 128, :], writes=[self.TM[tb]])
            for c0 in (0, 4):
                b = self.bank()
                for c in range(c0, c0 + 4):
                    self.tr(self.ps[b][:, (c - c0) * 128:(c - c0 + 1) * 128], self.tm[tb][:, c * 128:(c + 1) * 128], self.identF[:],
                            reads=[self.TM[tb], self.CONST], bankbuf=self.PS[b])
                self.cp("act" if c0 == 0 else "dve", self.xt[xb][:, c0:c0 + 4, blk * 128:(blk + 1) * 128],
                        self.ps[b][:].rearrange("p (c t) -> p c t", c=4), reads=[self.PS[b]], writes=self.X[xb][c0:c0 + 4])

    def store_y_tokmajor(self, ti, xb):
        O = self.O
        dst = O["yp"] if ti < self.nct else O["ys"]
        r0 = (ti if ti < self.nct else ti - self.nct) * T
        for blk in range(4):
            tb = self.rot("tm", 2)
            for c0 in (0, 4):
                b = self.bank()
                for c in range(c0, c0 + 4):
                    self.tr(self.ps[b][:, (c - c0) * 128:(c - c0 + 1) * 128], self.xt[xb][:, c, blk * 128:(blk + 1) * 128], self.identF[:],
                            reads=[self.X[xb][c], self.CONST], bankbuf=self.PS[b])
                self.cp("act" if c0 == 0 else "dve", self.tm[tb][:, c0 * 128:(c0 + 4) * 128], self.ps[b][:],
                        reads=[self.PS[b]], writes=[self.TM[tb]])
            ev = self.dma("sp", dst[r0 + blk * 128:r0 + (blk + 1) * 128, :], self.tm[tb][:], reads=[self.TM[tb]], writes=[self.Dout])
            self.final_events.append(ev)

    def p1_load(self, l, ti, a):
        S = self.S
        tok0 = ti * T
        if l == 0:
            self.load_x_tokmajor(ti, a)
        else:
            self.dma("sp", self.xt[a][:], S["xres"][:, tok0:tok0 + T].rearrange("(c p) t -> p c t", p=128),
                     reads=[self.Dt["xres"][ti]], writes=self.X[a])

    def p1_compute(self, l, ti, a, prefetch):
        S = self.S
        ctx = ti < self.nct
        tokm = 1 if ctx else 0
        tok0 = ti * T
        xb, ub, ub2 = a, a, 1 - a
        self.norm(l, 0, xb, ub, tokm)
        self.ffn(l, "1", xb, ub, tokm, mid=prefetch)
        self.norm(l, 1, xb, ub2, tokm)
        self.dma("sp", S["u2s"][:, tok0:tok0 + T].rearrange("(c p) t -> p c t", p=128), self.ut[ub2][:],
                 reads=self.U[ub2], writes=[self.Dt["u2s"][ti]])
        self.dma("sp", S["xres"][:, tok0:tok0 + T].rearrange("(c p) t -> p c t", p=128), self.xt[xb][:],
                 reads=self.X[xb], writes=[self.Dt["xres"][ti]])
        self.proj_early(l, ti, ub2)

    def in_slab(self, l, s):
        return self.slab(self.S[f"in_{l}"][s], 2048, self.Dw[f"in_{l}"], (8, 256))

    def proj_chunk(self, sl, jj, ub):
        b = self.bank()
        for kc in range(KC):
            self.mm(self.ps[b][:], sl.ap[:, kc, jj * 128:(jj + 1) * 128], self.ut[ub][:, kc, :], kc == 0, kc == KC - 1,
                    reads=[sl.buf, self.U[ub][kc]], bankbuf=self.PS[b])
        return b

    def xa_view(self, cc, ctx):
        if ctx:
            return self.xa[:, cc, :].rearrange("p (s w) -> p s w", s=2)[:, :, 15:271]
        return self.xa[:, cc, 15:15 + T]

    def ch_view(self, cc, ctx):
        if ctx:
            return self.ch[:, cc, :].rearrange("p (s w) -> p s w", s=2)[:, :, 1:257]
        return self.ch[:, cc, 1:1 + T]

    def v3(self, ap, ctx):
        return ap.rearrange("p (s w) -> p s w", s=2) if ctx else ap

    def proj_early(self, l, ti, ub):
        S, I, O = self.S, self.I, self.O
        ctx = ti < self.nct
        tok0 = ti * T
        for p in range(2):
            sv = self.in_slab(l, p)
            sg = self.in_slab(l, 2 + p)
            for jj in range(2):
                cc = 2 * p + jj
                bv = self.proj_chunk(sv, jj, ub)
                bg = self.proj_chunk(sg, jj, ub)
                ti_ = self.rot("tf", 3)
                self.act(self.tmpf[ti_][:], self.ps[bg][:], AF.Sigmoid, reads=[self.PS[bg]], writes=[self.TF[ti_]])
                self.tt("dve", self.xa_view(cc, ctx), self.v3(self.tmpf[ti_][:], ctx), self.v3(self.ps[bv][:], ctx), ALU.mult,
                        reads=[self.TF[ti_], self.PS[bv]], writes=[self.XA[cc]])
            self.done(sv)
            self.done(sg)
        for cc in range(4):
            self.dma("sp", S["xas"][cc * 128:(cc + 1) * 128, tok0:tok0 + T] if not ctx else
                     S["xas"][cc * 128:(cc + 1) * 128, tok0:tok0 + T].rearrange("p (s w) -> p s w", s=2),
                     self.xa_view(cc, ctx), reads=[self.XA[cc]], writes=[self.Dt["xas"][ti]])
        for p in range(2):
            sb = self.in_slab(l, 4 + p)
            for jj in range(2):
                cc = 2 * p + jj
                bb = self.proj_chunk(sb, jj, ub)
                self.cp("act", self.bg[:, cc, :], self.ps[bb][:], reads=[self.PS[bb]], writes=[self.BG[cc]])
            self.done(sb)
        self.dma("sp", S["bgs"][:, tok0:tok0 + T].rearrange("(c p) t -> p c t", p=128), self.bg[:],
                 reads=self.BG, writes=[self.Dt["bgs"][ti]])
        for p in range(2):
            sc = self.in_slab(l, 6 + p)
            sh = self.in_slab(l, 8 + p)
            for jj in range(2):
                cc = 2 * p + jj
                bc = self.proj_chunk(sc, jj, ub)
                bh = self.proj_chunk(sh, jj, ub)
                ti_ = self.rot("tf", 3)
                self.cp("act", self.tmpf[ti_][:], self.ps[bc][:], reads=[self.PS[bc]], writes=[self.TF[ti_]])
                self.tt("dve", self.ch_view(cc, ctx), self.v3(self.tmpf[ti_][:], ctx), self.v3(self.ps[bh][:], ctx), ALU.mult,
                        reads=[self.TF[ti_], self.PS[bh]], writes=[self.CH[cc]])
            self.done(sc)
            self.done(sh)
        for cc in range(4):
            self.dma("sp", S["chs"][cc * 128:(cc + 1) * 128, tok0:tok0 + T] if not ctx else
                     S["chs"][cc * 128:(cc + 1) * 128, tok0:tok0 + T].rearrange("p (s w) -> p s w", s=2),
                     self.ch_view(cc, ctx), reads=[self.CH[cc]], writes=[self.Dt["chs"][ti]])
        for which, s0, dst, DB, gcol in (("q", 10, self.qt, self.Q, C_GQ), ("k", 12, self.kt, self.K, C_GK)):
            for p in range(2):
                sl = self.in_slab(l, s0 + p)
                for jj in range(2):
                    cc = 2 * p + jj
                    bq = self.proj_chunk(sl, jj, ub)
                    si = self.rot("sq", 2)
                    self.act(self.sqb[si][:], self.ps[bq][:], AF.Square, reads=[self.PS[bq]], writes=[self.SQ[si]])
                    bs = self.bank()
                    self.mm(self.ps[bs][:], self.blk[:], self.sqb[si][:], True, True, reads=[self.CONST, self.SQ[si]], bankbuf=self.PS[bs])
                    s1 = self.rot("st", 3)
                    self.act(self.stat[s1][:], self.ps[bs][:], AF.Sqrt, reads=[self.PS[bs]], writes=[self.STB[s1]],
                             bias=self.col(0, C_EPS), scale=1.0 / HD)
                    self.recip(self.stat[s1][:], self.stat[s1][:], reads=[self.STB[s1]], writes=[self.STB[s1]])
                    self.stt("dve", dst[:, cc, 0:T], self.ps[bq][:], self.col(l, gcol), self.stat[s1][:], ALU.mult, ALU.mult,
                             reads=[self.PS[bq], self.STB[s1], self.COLS], writes=[DB[cc]])
                self.done(sl)
            nm = "qs" if which == "q" else "ks"
            self.dma("sp", S[nm][:, tok0:tok0 + T].rearrange("(c p) t -> p c t", p=128), dst[:, :, 0:T],
                     reads=DB, writes=[self.Dt[nm][ti]])
        s14 = self.in_slab(l, 14)
        s15 = self.in_slab(l, 15)
        for blk in range(4):
            b = self.bank()
            for hf, sl in ((0, s14), (1, s15)):
                for kc in range(KC):
                    self.mm(self.ps[b][:, hf * 256:(hf + 1) * 256], self.ut[ub][:, kc, blk * 128:(blk + 1) * 128], sl.ap[:, kc, :],
                            kc == 0, kc == KC - 1, reads=[sl.buf, self.U[ub][kc]], bankbuf=self.PS[b])
            self.cp("act", self.vt[:, blk, :].rearrange("p (h e) -> p h e", e=65)[:, :, 0:64],
                    self.ps[b][:].rearrange("p (h d) -> p h d", d=64), reads=[self.PS[b]], writes=[self.V[blk]])
            if ctx:
                tb = self.rot("tm", 2)
                self.cp("dve", self.tm[tb][:, 0:512], self.ps[b][:], reads=[self.PS[b]], writes=[self.TM[tb]])
                seq = ti * 2 + blk // 2
                ev = self.dma("sp", O["nv"][seq, l, (blk % 2) * 128:(blk % 2 + 1) * 128, :], self.tm[tb][:, 0:512],
                              reads=[self.TM[tb]], writes=[self.Dout])
                self.final_events.append(ev)
        self.done(s14)
        self.done(s15)
        self.dma("sp", S["vs"][tok0:tok0 + T, :].rearrange("(b p) f -> p b f", p=128), self.vt[:, 0:4, :],
                 reads=self.V[0:4], writes=[self.Dt["vs"][ti]])
        if ctx:
            s12 = self.in_slab(l, 12)
            s13 = self.in_slab(l, 13)
            for blk in range(4):
                b = self.bank()
                for hf, sl in ((0, s12), (1, s13)):
                    for kc in range(KC):
                        self.mm(self.ps[b][:, hf * 256:(hf + 1) * 256], self.ut[ub][:, kc, blk * 128:(blk + 1) * 128], sl.ap[:, kc, :],
                                kc == 0, kc == KC - 1, reads=[sl.buf, self.U[ub][kc]], bankbuf=self.PS[b])
                ti_ = self.rot("tf", 3)
                self.act(self.tmpf[ti_][:], self.ps[b][:], AF.Square, reads=[self.PS[b]], writes=[self.TF[ti_]])
                sm = self.rot("sm", 4)
                ss = self.small[:, sm * 16:sm * 16 + 8]
                self.op("dve", lambda e, o=ss, i=self.tmpf[ti_][:].rearrange("p (h d) -> p h d", d=64):
                        e.tensor_reduce(out=o, in_=i, axis=AX.X, op=ALU.add), reads=[self.TF[ti_]], writes=[self.SM[sm]])
                self.act(ss, ss, AF.Sqrt, reads=[self.SM[sm]], writes=[self.SM[sm]], bias=self.col(0, C_EPS), scale=1.0 / HD)
                self.recip(ss, ss, reads=[self.SM[sm]], writes=[self.SM[sm]])
                self.tt("dve", self.tmpf[ti_][:].rearrange("p (h d) -> p h d", d=64), self.ps[b][:].rearrange("p (h d) -> p h d", d=64),
                        ss.rearrange("p (h o) -> p h o", o=1).broadcast_to([128, 8, 64]), ALU.mult,
                        reads=[self.PS[b], self.SM[sm]], writes=[self.TF[ti_]])
                tb = self.rot("tm", 2)
                self.tt("dve", self.tm[tb][:, 0:512], self.tmpf[ti_][:], self.gkrow[:, l, :], ALU.mult,
                        reads=[self.TF[ti_], self.COLS], writes=[self.TM[tb]])
                seq = ti * 2 + blk // 2
                ev = self.dma("sp", O["nk"][seq, l, (blk % 2) * 128:(blk % 2 + 1) * 128, :], self.tm[tb][:, 0:512],
                              reads=[self.TM[tb]], writes=[self.Dout])
                self.final_events.append(ev)
            self.done(s12)
            self.done(s13)
    def p2_load(self, l, ti, a):
        S, I = self.S, self.I
        ctx = ti < self.nct
        tokm = 1 if ctx else 0
        tok0 = ti * T
        lt = ti - self.nct
        xb, ub = a, a
        fm = lambda nm: S[nm][:, tok0:tok0 + T].rearrange("(c p) t -> p c t", p=128)
        self.dma("sp", self.xt[xb][:], fm("xres"), reads=[self.Dt["xres"][ti]], writes=self.X[xb])
        self.dma("sp", self.ut[ub][:], fm("u2s"), reads=[self.Dt["u2s"][ti]], writes=self.U[ub])
        if ctx:
            for cc in range(4):
                self.dma("sp", self.xa_view(cc, True), S["xas"][cc * 128:(cc + 1) * 128, tok0:tok0 + T].rearrange("p (s w) -> p s w", s=2),
                         reads=[self.Dt["xas"][ti]], writes=[self.XA[cc]])
                self.dma("sp", self.ch_view(cc, True), S["chs"][cc * 128:(cc + 1) * 128, tok0:tok0 + T].rearrange("p (s w) -> p s w", s=2),
                         reads=[self.Dt["chs"][ti]], writes=[self.CH[cc]])
                xv = self.xa[:, cc, :].rearrange("p (s w) -> p s w", s=2)
                self.memset("pool", xv[:, :, 0:15], 0.0, [self.XA[cc]])
                self.memset("pool", xv[:, :, 271:286], 0.0, [self.XA[cc]])
                cv = self.ch[:, cc, :].rearrange("p (s w) -> p s w", s=2)
                self.memset("pool", cv[:, :, 0:1], 0.0, [self.CH[cc]])
                self.memset("pool", cv[:, :, 257:258], 0.0, [self.CH[cc]])
        else:
            first, last = lt == 0, lt == self.nlt - 1
            for nm, buf, BB, pad in (("xas", self.xa, self.XA, 15), ("chs", self.ch, self.CH, 1)):
                lo = tok0 - (0 if first else pad)
                hi = tok0 + T + (0 if last else pad)
                o0 = pad if first else 0
                rd = [self.Dt[nm][ti]] + ([] if first else [self.Dt[nm][ti - 1]]) + ([] if last else [self.Dt[nm][ti + 1]])
                self.dma("sp", buf[:, :, o0:o0 + hi - lo], S[nm][:, lo:hi].rearrange("(c p) t -> p c t", p=128), reads=rd, writes=BB)
                if first:
                    self.memset("pool", buf[:, :, 0:pad], 0.0, BB)
                if last:
                    self.memset("pool", buf[:, :, pad + T:pad + T + pad], 0.0, BB)
        self.dma("sp", self.bg[:], fm("bgs"), reads=[self.Dt["bgs"][ti]], writes=self.BG)
        self.dma("sp", self.qt[:], fm("qs"), reads=[self.Dt["qs"][ti]], writes=self.Q)
        if ctx:
            self.dma("sp", self.kt[:, :, 0:T], fm("ks"), reads=[self.Dt["ks"][ti]], writes=self.K)
            self.dma("sp", self.vt[:, 0:4, :], S["vs"][tok0:tok0 + T, :].rearrange("(b p) f -> p b f", p=128),
                     reads=[self.Dt["vs"][ti]], writes=self.V[0:4])
        else:
            rows = 8 * self.nlt
            w0 = 8 * lt - 4
            r_lo, r_hi = max(w0, 0), min(w0 + 16, rows)
            base = self.nct * T
            rd = [self.Dt["ks"][t2] for t2 in range(max(ti - 1, self.nct), min(ti + 2, self.NT))]
            self.dma("sp", self.kt[:, :, (r_lo - w0) * 64:(r_hi - w0) * 64],
                     S["ks"][:, base + r_lo * 64:base + r_hi * 64].rearrange("(c p) t -> p c t", p=128), reads=rd, writes=self.K)
    def p2_compute(self, l, ti, a, prefetch):
        S = self.S
        ctx = ti < self.nct
        tokm = 1 if ctx else 0
        tok0 = ti * T
        xb, ub = a, a
        if not ctx:
            self.attn_prep(ti, 0)
        ln_finish = self.branch_ab(l, ctx)
        if ctx:
            self.attn_ctx(l)
        else:
            self.attn_lat(l, ti)
        ln_finish()
        if prefetch is not None:
            prefetch()
        self.gates_merge(l, xb, ub, tokm)
        self.norm(l, 2, xb, ub, tokm)
        self.ffn(l, "2", xb, ub, tokm)
        if l == self.NL - 1:
            self.store_y_tokmajor(ti, xb)
        else:
            self.dma("sp", S["xres"][:, tok0:tok0 + T].rearrange("(c p) t -> p c t", p=128), self.xt[xb][:],
                     reads=self.X[xb], writes=[self.Dt["xres"][ti]])

    def conv(self, l, key, buf, BB, segw, ctx, evac):
        S = self.S
        segs = [(s_ * segw, s_ * 256, 256) for s_ in range(2)] if ctx else [(0, 0, T)]
        deferred = None
        if key == "ca":
            for cc in range(4):
                sls = [self.slab(S[f"ca_{l}"][cc * 2 + hf], 2048, self.Dw[f"ca_{l}"], (16, 128)) for hf in range(2)]
                b = self.bank()
                for (oi, oo, n) in segs:
                    for k in range(31):
                        sl = sls[k // 16]
                        self.mm(self.ps[b][:, oo:oo + n], sl.ap[:, k % 16, :], buf[:, cc, oi + k:oi + k + n], k == 0, k == 30,
                                reads=[sl.buf, BB[cc]], bankbuf=self.PS[b])
                for sl in sls:
                    self.done(sl)
                if deferred is not None:
                    deferred()
                deferred = evac(cc, b)
        else:
            sl = self.slab(S[f"cb_{l}"][0], 12 * 128, self.Dw[f"cb_{l}"], (12, 128))
            for cc in range(4):
                b = self.bank()
                for (oi, oo, n) in segs:
                    for k in range(3):
                        self.mm(self.ps[b][:, oo:oo + n], sl.ap[:, k * 4 + cc, :], buf[:, cc, oi + k:oi + k + n], k == 0, k == 2,
                                reads=[sl.buf, BB[cc]], bankbuf=self.PS[b])
                if deferred is not None:
                    deferred()
                deferred = evac(cc, b)
            self.done(sl)
        return deferred

    def branch_ab(self, l, ctx):
        bs1 = self.bank()
        self.hold(bs1)
        bs2 = self.bank()
        self.hold(bs2)

        def evac_a(cc, b):
            self.act(self.ha_f[cc], self.ps[b][:], AF.Identity, reads=[self.PS[b], self.COLS], writes=self.HA[cc],
                     bias=self.col(l, C_CAB + cc))
            si = self.rot("sq", 2)
            self.act(self.sqb[si][:], self.ps[b][:], AF.Square, reads=[self.PS[b], self.COLS], writes=[self.SQ[si]],
                     bias=self.col(l, C_CAB + cc))
            si2 = self.rot("sq", 2)
            self.act(self.sqb[si2][:], self.ps[b][:], AF.Identity, reads=[self.PS[b], self.COLS], writes=[self.SQ[si2]],
                     bias=self.col(l, C_CAB + cc))

            def stats():
                self.mm(self.ps[bs2][:], self.ones[:], self.sqb[si][:], cc == 0, cc == 3, reads=[self.CONST, self.SQ[si]], bankbuf=self.PS[bs2])
                self.mm(self.ps[bs1][:], self.ones[:], self.sqb[si2][:], cc == 0, cc == 3, reads=[self.CONST, self.SQ[si2]], bankbuf=self.PS[bs1])
            return stats

        last_a = self.conv(l, "ca", self.xa, self.XA, 286, ctx, evac_a)

        def evac_b(cc, b):
            self.tt("dve", self.tB[:, cc, :], self.ps[b][:], self.bg[:, cc, :], ALU.mult,
                    reads=[self.PS[b], self.BG[cc]], writes=[self.TB[cc]])
            if cc == 0:
                return last_a
            return None
        self.conv(l, "cb", self.ch, self.CH, 258, ctx, evac_b)
        m = self.rot("st", 3)
        self.op("act", lambda e, o=self.stat[m][:], i=self.ps[bs1][:]: e.mul(out=o, in_=i, mul=1.0 / 512), reads=[self.PS[bs1]], writes=[self.STB[m]])
        q = self.rot("st", 3)
        self.op("act", lambda e, o=self.stat[q][:], i=self.ps[bs2][:]: e.mul(out=o, in_=i, mul=1.0 / 512), reads=[self.PS[bs2]], writes=[self.STB[q]])
        self.unhold(bs1)
        self.unhold(bs2)

        def ln_finish():
            t_ = self.rot("tf", 3)
            self.tt("dve", self.tmpf[t_][:], self.stat[m][:], self.stat[m][:], ALU.mult, reads=[self.STB[m]], writes=[self.TF[t_]])
            self.tt("dve", self.stat[q][:], self.stat[q][:], self.tmpf[t_][:], ALU.subtract, reads=[self.STB[q], self.TF[t_]], writes=[self.STB[q]])
            self.act(self.stat[q][:], self.stat[q][:], AF.Sqrt, reads=[self.STB[q]], writes=[self.STB[q]], bias=self.col(0, C_EPS))
            self.recip(self.stat[q][:], self.stat[q][:], reads=[self.STB[q]], writes=[self.STB[q]])
            for cc in range(4):
                self.tt("dve", self.ha_f[cc], self.ha_f[cc], self.stat[m][:], ALU.subtract, reads=self.HA[cc] + [self.STB[m]], writes=self.HA[cc])
                self.tt("dve", self.ha_f[cc], self.ha_f[cc], self.stat[q][:], ALU.mult, reads=self.HA[cc] + [self.STB[q]], writes=self.HA[cc])
                self.act(self.sA[:, cc, :], self.ha_f[cc], AF.Silu, reads=self.HA[cc] + [self.COLS], writes=[self.SA[cc]],
                         bias=self.col(l, C_LNB + cc), scale=self.col(l, C_LNG + cc))
        return ln_finish

    def o_evac(self, bO, qb, half):
        sm = self.rot("sm", 4)
        rs = self.small[:, sm * 16:sm * 16 + 4]
        pv = self.ps[bO][:, 0:260].rearrange("p (h e) -> p h e", e=65)
        self.recip(rs.rearrange("p (h o) -> p h o", o=1), pv[:, :, 64:65], reads=[self.PS[bO]], writes=[self.SM[sm]])
        self.tt("dve", self.otm[qb][:, half * 256:(half + 1) * 256].rearrange("p (h d) -> p h d", d=64), pv[:, :, 0:64],
                rs.rearrange("p (h o) -> p h o", o=1).broadcast_to([128, 4, 64]), ALU.mult,
                reads=[self.PS[bO], self.SM[sm]], writes=[self.OTM[qb]])

    def o_transpose(self, qb, dst_view):
        b = self.bank()
        pb = self.ps[b][:].bitcast(BF16)
        for cc in range(4):
            self.tr(pb[:, cc * 128:(cc + 1) * 128], self.otm[qb][:, cc * 128:(cc + 1) * 128], self.identB[:],
                    reads=[self.OTM[qb], self.CONST], bankbuf=self.PS[b])
        self.cp("act", dst_view, pb[:, 0:512].rearrange("p (c q) -> p c q", c=4) if dst_view.ndim == 3 else
                pb[:, 0:512].rearrange("p (c a w) -> p c a w", c=4, a=8), reads=[self.PS[b]], writes=self.OT)

    def attn_ctx(self, l):
        for s in range(2):
            for half in range(2):
                bO = [self.bank(), self.bank()]
                for b_ in bO:
                    self.hold(b_)
                for hh in range(4):
                    h = half * 4 + hh
                    cc, p0 = h // 2, 64 * (h % 2)
                    bS = self.bank()
                    for kb in range(2):
                        self.mm(self.ps[bS][:, kb * 256:(kb + 1) * 256], self.kt[p0:p0 + 64, cc, s * 256 + kb * 128:s * 256 + (kb + 1) * 128],
                                self.qt[p0:p0 + 64, cc, s * 256:(s + 1) * 256], True, True,
                                reads=[self.K[cc], self.Q[cc]], bankbuf=self.PS[bS])
                    pi = self.rot("pc", 3)
                    self.act(self.pc[pi], self.ps[bS][:], AF.Exp, reads=[self.PS[bS]], writes=[self.PC[pi]])
                    for qb in range(2):
                        for kb in range(2):
                            self.mm(self.ps[bO[qb]][:, hh * 65:(hh + 1) * 65], self.pc[pi][:, kb * 256 + qb * 128:kb * 256 + (qb + 1) * 128],
                                    self.vt[:, s * 2 + kb, h * 65:(h + 1) * 65], kb == 0, kb == 1,
                                    reads=[self.PC[pi], self.V[s * 2 + kb]], bankbuf=self.PS[bO[qb]])
                for qb in range(2):
                    self.o_evac(bO[qb], s * 2 + qb, half)
                    self.unhold(bO[qb])
            for qb in range(2):
                q4 = s * 2 + qb
                self.o_transpose(q4, self.oT[:, :, q4 * 128:(q4 + 1) * 128])

    def attn_prep(self, ti, j):
        S = self.S
        lt = ti - self.nct
        rows = 8 * self.nlt
        w0 = 8 * lt - 4
        base = self.nct * T
        bs = BLK_START[j]
        kb_i = j % 2
        for gi in range(4):
            vb = kb_i * 4 + gi
            r = w0 + 4 * gi
            if 0 <= r < rows:
                t0 = base + r * 64 + bs
                rd = [self.Dt["vs"][t0 // T]]
                src = bass.AP(S["vs"].tensor, t0 * 520, [[64 * 520, 4], [520, 32], [1, 520]])
                self.dma("sp", self.vt[:, vb, :], src, reads=rd, writes=[self.V[vb]])
        for cc in range(4):
            self.cp("pool", self.kblk[kb_i][:, cc, :].rearrange("p (r w) -> p r w", w=32),
                    self.kt[:, cc, :].rearrange("p (r w) -> p r w", w=64)[:, :, bs:bs + 32],
                    reads=[self.K[cc]], writes=[self.KB[kb_i][cc]])

    def attn_lat(self, l, ti):
        S, I = self.S, self.I
        lt = ti - self.nct
        rows = 8 * self.nlt
        w0 = 8 * lt - 4
        base = self.nct * T
        if lt == 0:
            self.load_cache(l)
        for j in range(4):
            cp = CP_OF_J[j]
            bs = BLK_START[j]
            kb_i = j % 2
            if j + 1 < 4:
                self.attn_prep(ti, j + 1)
            slabs = [self.slab(S[f"bt_{l}"][self.mask_idx[(lt, cp)], hq * 4:(hq + 1) * 4].rearrange("h p k -> p h k"), 2048, self.Dw[f"bt_{l}"], (4, 512))
                     for hq in range(2)]
            bOs = {}

            def scores(h):
                half, hh = divmod(h, 4)
                cc, p0 = h // 2, 64 * (h % 2)
                qv = self.qt[p0:p0 + 64, cc, :].rearrange("p (a w) -> p a w", w=64)[:, :, 16 * j:16 * j + 16]
                bS = self.bank()
                for gi in range(4):
                    self.mm(self.ps[bS][:, gi * 128:(gi + 1) * 128], self.kblk[kb_i][p0:p0 + 64, cc, gi * 128:(gi + 1) * 128], qv,
                            True, True, reads=[self.KB[kb_i][cc], self.Q[cc]], bankbuf=self.PS[bS])
                bC = self.bank()
                for cb in range(4):
                    self.mm(self.ps[bC][:, cb * 128:(cb + 1) * 128], self.kc[p0:p0 + 64, cc, cb * 128:(cb + 1) * 128], qv,
                            True, True, reads=[self.KCb, self.Q[cc]], bankbuf=self.PS[bC])
                pi = self.rot("pl", 3)
                tfi = self.rot("tf", 3)
                self.tt("dve", self.tmpf[tfi][:], self.ps[bS][:], slabs[half].ap[:, hh, :], ALU.add,
                        reads=[self.PS[bS], slabs[half].buf], writes=[self.TF[tfi]])
                ci = self.rot("pc", 3)
                self.act(self.pc[ci], self.ps[bC][:], AF.Exp, reads=[self.PS[bC]], writes=[self.PC[ci]])
                self.act(self.pl[pi], self.tmpf[tfi][:], AF.Exp, reads=[self.TF[tfi]], writes=[self.PL[pi]])
                return (h, pi, ci)

            def pv(st_):
                h, pi, ci = st_
                half, hh = divmod(h, 4)
                bO = bOs[half]
                oview = self.ps[bO][:, hh * 65:(hh + 1) * 65]
                for gi in range(4):
                    self.mm(oview, self.pl[pi][:, gi * 128:(gi + 1) * 128], self.vt[:, kb_i * 4 + gi, h * 65:(h + 1) * 65],
                            gi == 0, False, reads=[self.PL[pi], self.V[kb_i * 4 + gi]], bankbuf=self.PS[bO])
                for cb in range(4):
                    self.mm(oview, self.pc[ci][:, cb * 128:(cb + 1) * 128], self.vc[:, cb, h * 65:(h + 1) * 65],
                            False, cb == 3, reads=[self.PC[ci], self.VCb], bankbuf=self.PS[bO])
                if hh == 3:
                    self.o_evac(bO, j, half)
                    self.unhold(bO)

            pending = None
            for h in range(8):
                if h % 4 == 0:
                    bOs[h // 4] = self.bank()
                    self.hold(bOs[h // 4])
                cur = scores(h)
                if pending is not None:
                    pv(pending)
                pending = cur
            pv(pending)
            for sl in slabs:
                self.done(sl)
            self.o_transpose(j, self.oT[:, :, :].rearrange("p c (a w) -> p c a w", w=64)[:, :, :, 16 * j:16 * j + 16])

    def load_cache(self, l):
        I = self.I
        for kb in range(4):
            self.dma("pool", self.vc[:, kb, :].rearrange("p (h e) -> p h e", e=65)[:, :, 0:64],
                     I["cv"][l, kb * 128:(kb + 1) * 128, :].rearrange("p (h d) -> p h d", d=64), writes=[self.VCb], slow=True)
        for kb in range(4):
            tb = self.rot("tm", 2)
            self.dma("sp", self.tm[tb][:, 0:512], I["ck"][l, kb * 128:(kb + 1) * 128, :], writes=[self.TM[tb]])
            b = self.bank()
            for cc in range(4):
                self.tr(self.ps[b][:, cc * 128:(cc + 1) * 128], self.tm[tb][:, cc * 128:(cc + 1) * 128], self.identF[:],
                        reads=[self.TM[tb], self.CONST], bankbuf=self.PS[b])
            self.cp("dve", self.kc[:, :, kb * 128:(kb + 1) * 128], self.ps[b][:].rearrange("p (c t) -> p c t", c=4),
                    reads=[self.PS[b]], writes=[self.KCb] + self.KC4)

    def gates_merge(self, l, xb, ub, tokm):
        S = self.S
        for p in range(4):
            sl = {}
            for bi, br in enumerate("abc"):
                sl["g" + br] = self.in_slab(l, 16 + 4 * bi + p)
                sl["o" + br] = self.slab(S[f"o{br}_{l}"][p], 1024, self.Dw[f"o{br}_{l}"], (4, 256))
            for jj in range(2):
                mc = 2 * p + jj
                mi = self.rot("ma", 2)
                for bi, (br, src, SB) in enumerate((("a", self.sA, self.SA), ("b", self.tB, self.TB), ("c", self.oT, self.OT))):
                    bg = self.proj_chunk(sl["g" + br], jj, ub)
                    by = self.bank()
                    for kc in range(4):
                        self.mm(self.ps[by][:], sl["o" + br].ap[:, kc, jj * 128:(jj + 1) * 128], src[:, kc, :], kc == 0, kc == 3,
                                reads=[sl["o" + br].buf, SB[kc]], bankbuf=self.PS[by])
                    ti_ = self.rot("tf", 3)
                    self.act(self.tmpf[ti_][:], self.ps[bg][:], AF.Sigmoid, reads=[self.PS[bg]], writes=[self.TF[ti_]])
                    if bi == 0:
                        self.tt("dve", self.macc[mi], self.tmpf[ti_][:], self.ps[by][:], ALU.mult,
                                reads=[self.TF[ti_], self.PS[by]], writes=self.MA[mi])
                    else:
                        self.tt("dve", self.tmpf[ti_][:], self.tmpf[ti_][:], self.ps[by][:], ALU.mult,
                                reads=[self.TF[ti_], self.PS[by]], writes=[self.TF[ti_]])
                        if bi == 1:
                            self.tt("pool", self.macc[mi], self.macc[mi], self.tmpf[ti_][:], ALU.add,
                                    reads=self.MA[mi] + [self.TF[ti_]], writes=self.MA[mi])
                        else:
                            self.tt("pool", self.hv(mc), self.macc[mi], self.tmpf[ti_][:], ALU.add,
                                    reads=self.MA[mi] + [self.TF[ti_]], writes=[self.H[mc]])
            for s_ in sl.values():
                self.done(s_)
        for p in range(4):
            sm = self.slab(S[f"mg_{l}"][p], 2048, self.Dw[f"mg_{l}"], (8, 256))
            for jj in range(2):
                mc2 = 2 * p + jj
                b = self.bank()
                for mc in range(8):
                    self.mm(self.ps[b][:], sm.ap[:, mc, jj * 128:(jj + 1) * 128], self.hv(mc), mc == 0, mc == 7,
                            reads=[sm.buf, self.H[mc]], bankbuf=self.PS[b])
                self.stt("dve", self.xt[xb][:, mc2, :], self.ps[b][:], self.col(l, C_GT + (8 + mc2) * 2 + tokm),
                         self.xt[xb][:, mc2, :], ALU.mult, ALU.add,
                         reads=[self.PS[b], self.X[xb][mc2], self.COLS, self.MODS], writes=[self.X[xb][mc2]])
            self.done(sm)
    def emit_all(self):
        NL = self.NL
        self.reset_state()
        self.setup()
        self.memset("dve", self.col(0, C_EPS), EPS, [self.MODS])
        self.convert_p1(0)
        self.ada(0)
        self.COLS.const = True
        self.CONST.const = True
        self.SCT.const = True
        tasks_by_layer = {0: self.table_tasks(0) + [lambda: self.convert_p2(0)]}
        for l in range(1, NL):
            def ada_l(ll=l):
                self.MODS.const = False
                self.ada(ll)
                self.MODS.const = True
            tasks_by_layer[0] += [ada_l, (lambda ll=l: self.convert_p1(ll)), (lambda ll=l: self.convert_p2(ll))]
            tasks_by_layer[l] = self.table_tasks(l)

        self.MODS.const = True
        steps = []
        for l in range(NL):
            steps += [("p1", l, ti) for ti in range(self.NT)]
            steps += [("p2", l, ti) for ti in range(self.NT)]
        loaded = [-1]

        def load_step(k):
            ph, l_, ti_ = steps[k]
            loaded[0] = k
            (self.p1_load if ph == "p1" else self.p2_load)(l_, ti_, k % 2)

        for k, (ph, l, ti) in enumerate(steps):
            if loaded[0] < k:
                load_step(k)
            nxt = steps[k + 1] if k + 1 < len(steps) else None
            pf = None
            if nxt is not None and not (ph == "p1" and nxt[0] == "p2"):
                pf = (lambda kk=k + 1: load_step(kk))
            if ph == "p1":
                self.p1_compute(l, ti, k % 2, pf)
                tl = tasks_by_layer.get(l, [])
                per_slot = -(-(len(tl) + ti) // self.NT) if tl else 0
                for _ in range(max(per_slot, 2)):
                    if tl:
                        tl.pop(0)()
                if ti == self.NT - 1:
                    while tl:
                        tl.pop(0)()
            else:
                self.p2_compute(l, ti, k % 2, pf)
        if not self.dry:
            E = self.eng["pool"]
            deps = {}
            for ev in self.final_events:
                if ev is not None and deps.get(ev[0], 0) < ev[1]:
                    deps[ev[0]] = ev[1]
            waits = self._prune(E, deps)
            E.count += 1
            E.ops.append((waits, lambda e: e.memset(self.small[:, 60:64], 0.0), (E.sem, 1)))

    def replay(self):
        nc = self.nc
        sems = self.sem_handles

        def run(e, ops):
            for waits, fn, inc in ops:
                for s, v in waits:
                    e.wait_ge(sems[s], v)
                ins = fn(e)
                ins.then_inc(sems[inc[0]], inc[1])

        with nc.Block() as block:
            @block.tensor
            def _(e):
                run(e, self.eng["pe"].ops)

            @block.scalar
            def _(e):
                run(e, self.eng["act"].ops)

            @block.vector
            def _(e):
                run(e, self.eng["dve"].ops)

            @block.gpsimd
            def _(e):
                run(e, self.eng["pool"].ops)

            @block.sync
            def _(e):
                run(e, self.eng["sp"].ops)


def build_program(NL, nct, nlt):
    nc = bass.Bass("TRN2", target_bir_lowering=False)
    P = Prog(nc, NL, nct, nlt)
    P.declare()
    with contextlib.ExitStack() as stack:
        P.alloc(stack)
        P.dry = True
        P.emit_all()
        P.dry = False
        P.final_events = []
        P.emit_all()
        P.replay()
    return nc, P


def host_consts(P):
    ident = np.eye(128, dtype=np.float32)
    anti = np.ascontiguousarray(ident[::-1])
    ones = np.ones((128, 128), np.float32)
    blk = np.zeros((128, 128), np.float32)
    blk[:64, :64] = 1.0
    blk[64:, 64:] = 1.0
    return {"c_ident": ident, "c_anti": anti, "c_ones": ones, "c_blk": blk,
            "c_mask": np.stack(P.mask_list).astype(np.float32)}


_CACHE = {}


def run_cores(inputs, NL, nct, nlt, ncores):
    key = (NL, nct, nlt)
    if key not in _CACHE:
        _CACHE[key] = build_program(NL, nct, nlt)
    nc, P = _CACHE[key]
    cst = host_consts(P)
    f = lambda a: np.ascontiguousarray(np.asarray(a, dtype=np.float32))
    nseq = 2 * nct
    in_maps = []
    for c in range(ncores):
        m = dict(cst)
        m["xp"] = f(inputs["x_prompt"][c * nseq:(c + 1) * nseq]).reshape(nseq * 256, D)
        m["xs"] = f(inputs["x_sample"][c]).reshape(-1, D)
        m["ck"] = f(inputs["cache_k"][c]).reshape(NL, PAST, 512)
        m["cv"] = f(inputs["cache_v"][c]).reshape(NL, PAST, 512)
        m["cvec"] = np.stack([f(inputs["c"][c]), f(inputs["c_ctx"])])
        for nm, _ in WEIGHT_SHAPES:
            m[nm] = f(inputs[nm])
        in_maps.append(m)
    res = run_bass_kernel_spmd(nc, in_maps, core_ids=list(range(ncores)))
    outs = res.results
    yp = np.concatenate([o["yp"].reshape(nseq, 256, D) for o in outs], axis=0)
    ys = np.stack([o["ys"] for o in outs], axis=0)
    nk = np.concatenate([o["nk"].reshape(nseq, NL, 256, NH, HD) for o in outs], axis=0)
    nv = np.concatenate([o["nv"].reshape(nseq, NL, 256, NH, HD) for o in outs], axis=0)
    return (yp.astype(np.float32), ys.astype(np.float32), nk.astype(np.float32), nv.astype(np.float32))


def kernel(**inputs):
    return run_cores(inputs, 2, 2, 8, 8)
```
